# Optimizing a Trainium2 kernel written in Bass

```python
import math
import jax, jax.numpy as jnp
from jax import lax
import numpy as np

D_MODEL = 1024
BATCH = 4
SEQ = 8192
DEPTH = 2

HEAD_DIM = 64
ROPE_THETA = 10000.0
NORM_EPS = 1e-6
Q_BLOCK = 128
BIG = 1e9

NSA_HEADS = 8
NSA_KV_GROUPS = 2
NSA_HEADS_PER_GROUP = NSA_HEADS // NSA_KV_GROUPS
CMP_BLOCK = 32
CMP_STRIDE = 16
CMP_HIDDEN = 256
SLC_BLOCK = 64
SLC_TOPK = 16
WINDOW = 512
NSA_WIDTH = NSA_HEADS * HEAD_DIM
NSA_KV_WIDTH = NSA_KV_GROUPS * HEAD_DIM

DIFF_HEADS = 4
DIFF_QK_WIDTH = DIFF_HEADS * 2 * HEAD_DIM
DIFF_V_DIM = 2 * HEAD_DIM
DIFF_WIDTH = DIFF_HEADS * DIFF_V_DIM

SGU_CHUNK = 128
SGU_GROUPS = 4
SGU_WIDTH = 512
SGU_GROUP_DIM = SGU_WIDTH // SGU_GROUPS

N_BRANCHES = 3
BRANCH_WIDTH = 512
D_FF = -(-8 * D_MODEL // (3 * 256)) * 256

IN_SPLITS = [NSA_WIDTH, 6 * NSA_KV_WIDTH, NSA_HEADS * 3,
             DIFF_QK_WIDTH, DIFF_QK_WIDTH, DIFF_WIDTH, 2 * SGU_WIDTH]
D_IN = sum(IN_SPLITS)
IN_OFFSETS = [int(v) for v in np.cumsum(IN_SPLITS)[:-1]]

kernel_name = "hybrid_nsa_diffattn_sgu_block"


def rmsnorm(x, g):
    xf = x.astype(jnp.float32)
    y = xf * lax.rsqrt(jnp.mean(xf * xf, axis=-1, keepdims=True) + NORM_EPS)
    return y.astype(x.dtype) * g


def rope(x, positions):
    half = x.shape[-1] // 2
    inv_freq = ROPE_THETA ** (-jnp.arange(half, dtype=jnp.float32) / half)
    ang = positions.astype(jnp.float32)[..., None] * inv_freq
    cos = jnp.cos(ang)[:, :, None, :].astype(x.dtype)
    sin = jnp.sin(ang)[:, :, None, :].astype(x.dtype)
    x1, x2 = x[..., :half], x[..., half:]
    return jnp.concatenate([x1 * cos - x2 * sin, x2 * cos + x1 * sin], axis=-1)


def masked_softmax(logits, mask):
    lf = jnp.where(mask, logits.astype(jnp.float32), -1e30)
    p = jax.nn.softmax(lf, axis=-1)
    return jnp.where(mask, p, 0.0)


def map_query_blocks(fn, seq):
    out = lax.map(fn, jnp.arange(seq // Q_BLOCK))
    nb, b, t, w = out.shape
    return out.transpose(1, 0, 2, 3).reshape(b, nb * t, w)


def nsa_attention(q, k_cmp, v_cmp, k_slc, v_slc, k_win, v_win, gates, positions,
                  pos_k, wk1, wk2, pos_v, wv1, wv2):
    B, S = q.shape[:2]
    G, HPG, d = NSA_KV_GROUPS, NSA_HEADS_PER_GROUP, HEAD_DIM
    scale = d ** -0.5
    q_rot = rope(q, positions)
    k_slc = rope(k_slc, positions)
    k_win = rope(k_win, positions)

    n_cmp = (S - CMP_BLOCK) // CMP_STRIDE + 1
    cmp_idx = np.arange(n_cmp)[:, None] * CMP_STRIDE + np.arange(CMP_BLOCK)[None, :]
    cmp_end = jnp.asarray(np.arange(n_cmp) * CMP_STRIDE + CMP_BLOCK - 1)

    def compress(kv, pos_emb, w1, w2):
        blocks = kv[:, cmp_idx] + pos_emb[:, None, :]
        blocks = blocks.transpose(0, 1, 3, 2, 4).reshape(B, n_cmp, G, CMP_BLOCK * d)
        return jax.nn.gelu(blocks @ w1) @ w2

    kc = compress(k_cmp, pos_k, wk1, wk2)
    vc = compress(v_cmp, pos_v, wv1, wv2)

    n_slc = S // SLC_BLOCK
    slc_k = min(SLC_TOPK, n_slc)
    c_start = np.arange(n_cmp)[:, None] * CMP_STRIDE
    s_start = np.arange(n_slc)[None, :] * SLC_BLOCK
    overlap = jnp.asarray(((c_start < s_start + SLC_BLOCK) &
                           (c_start + CMP_BLOCK > s_start)).astype(np.float32))
    slc_starts = jnp.arange(n_slc) * SLC_BLOCK
    blk_ids = jnp.arange(n_slc)

    kb = k_slc.reshape(B, n_slc, SLC_BLOCK, G, d).transpose(0, 3, 1, 2, 4)
    vb = v_slc.reshape(B, n_slc, SLC_BLOCK, G, d).transpose(0, 3, 1, 2, 4)
    gather = jax.vmap(jax.vmap(lambda blocks, ids: blocks[ids]))

    k_win_pad = jnp.pad(k_win, ((0, 0), (WINDOW, 0), (0, 0), (0, 0)))
    v_win_pad = jnp.pad(v_win, ((0, 0), (WINDOW, 0), (0, 0), (0, 0)))

    def block(i):
        s0 = i * Q_BLOCK
        t = s0 + jnp.arange(Q_BLOCK)
        qg = lax.dynamic_slice_in_dim(q, s0, Q_BLOCK, axis=1).reshape(B, Q_BLOCK, G, HPG, d)
        qrg = lax.dynamic_slice_in_dim(q_rot, s0, Q_BLOCK, axis=1).reshape(B, Q_BLOCK, G, HPG, d)
        gb = lax.dynamic_slice_in_dim(gates, s0, Q_BLOCK, axis=1).reshape(B, Q_BLOCK, G, HPG, 3)

        logit_c = jnp.einsum('btghd,bngd->bghtn', qg, kc) * scale
        p_c = masked_softmax(logit_c, cmp_end[None, :] <= t[:, None])
        o_cmp = jnp.einsum('bghtn,bngd->btghd', p_c.astype(vc.dtype), vc)

        imp = jnp.einsum('bghtn,nj->bgtj', p_c, overlap)
        eligible = slc_starts[None, :] <= t[:, None]
        cur = t // SLC_BLOCK
        forced = (blk_ids[None, :] == 0) | (blk_ids[None, :] == cur[:, None]) | \
                 (blk_ids[None, :] == cur[:, None] - 1)
        score = jnp.where(forced, BIG, jnp.where(eligible, imp, -BIG))
        _, idx = lax.top_k(score, slc_k)
        ks = gather(kb, idx).reshape(B, G, Q_BLOCK, slc_k * SLC_BLOCK, d)
        vs = gather(vb, idx).reshape(B, G, Q_BLOCK, slc_k * SLC_BLOCK, d)
        kpos = (idx[..., None] * SLC_BLOCK + jnp.arange(SLC_BLOCK)).reshape(B, G, Q_BLOCK, slc_k * SLC_BLOCK)
        mask_s = (kpos <= t[None, None, :, None])[:, :, None]
        logit_s = jnp.einsum('btghd,bgtkd->bghtk', qrg, ks) * scale
        p_s = masked_softmax(logit_s, mask_s)
        o_slc = jnp.einsum('bghtk,bgtkd->btghd', p_s.astype(vs.dtype), vs)

        kw = lax.dynamic_slice_in_dim(k_win_pad, s0, Q_BLOCK + WINDOW, axis=1)
        vw = lax.dynamic_slice_in_dim(v_win_pad, s0, Q_BLOCK + WINDOW, axis=1)
        wpos = s0 - WINDOW + jnp.arange(Q_BLOCK + WINDOW)
        mask_w = (wpos[None, :] <= t[:, None]) & (wpos[None, :] > t[:, None] - WINDOW) & (wpos[None, :] >= 0)
        logit_w = jnp.einsum('btghd,bkgd->bghtk', qrg, kw) * scale
        p_w = masked_softmax(logit_w, mask_w)
        o_win = jnp.einsum('bghtk,bkgd->btghd', p_w.astype(vw.dtype), vw)

        out = gb[..., 0:1] * o_cmp + gb[..., 1:2] * o_slc + gb[..., 2:3] * o_win
        return out.reshape(B, Q_BLOCK, NSA_WIDTH)

    return map_query_blocks(block, S)


def diff_attention(q, k, v, positions, lq1, lk1, lq2, lk2, subln_g, lambda_init):
    B, S = q.shape[:2]
    scale = HEAD_DIM ** -0.5
    q = rope(q, positions)
    k = rope(k, positions)
    lam = (jnp.exp(jnp.sum(lq1 * lk1).astype(jnp.float32))
           - jnp.exp(jnp.sum(lq2 * lk2).astype(jnp.float32)) + lambda_init)
    kpos = jnp.arange(S)

    def block(i):
        s0 = i * Q_BLOCK
        t = s0 + jnp.arange(Q_BLOCK)
        qb = lax.dynamic_slice_in_dim(q, s0, Q_BLOCK, axis=1)
        logits = jnp.einsum('bthd,bshd->bhts', qb, k) * scale
        p = masked_softmax(logits, kpos[None, :] <= t[:, None])
        p = p.reshape(B, DIFF_HEADS, 2, Q_BLOCK, S)
        a = p[:, :, 0] - lam * p[:, :, 1]
        o = jnp.einsum('bhts,bshe->bthe', a.astype(v.dtype), v)
        o = rmsnorm(o, subln_g) * (1.0 - lambda_init)
        return o.reshape(B, Q_BLOCK, DIFF_WIDTH)

    return map_query_blocks(block, S)


def chunked_sgu(uv, norm_g, w_s, b_s):
    B, S = uv.shape[:2]
    z = jax.nn.gelu(uv)
    u, v = z[..., :SGU_WIDTH], z[..., SGU_WIDTH:]
    v = rmsnorm(v, norm_g).reshape(B, S // SGU_CHUNK, SGU_CHUNK, SGU_GROUPS, SGU_GROUP_DIM)
    causal = jnp.tril(jnp.ones((SGU_CHUNK, SGU_CHUNK), dtype=bool))
    w = jnp.where(causal[None], w_s, 0.0)
    s = jnp.einsum('gts,bnsgc->bntgc', w, v) + b_s.T[:, :, None]
    return u * s.reshape(B, S, SGU_WIDTH)


def setup_inputs(seed: int = 0) -> dict:
    key = jax.random.key(seed)
    ks = iter(jax.random.split(key, 40))
    L, D, d = DEPTH, D_MODEL, HEAD_DIM

    def normal(shape, scale):
        return jax.random.normal(next(ks), shape, jnp.float32) * scale

    def gain(shape):
        return 1.0 + normal(shape, 0.05)

    x = normal((BATCH, SEQ, D), 1.0)
    offset = jax.random.randint(next(ks), (BATCH, 1), 0, 4096, dtype=jnp.int32)
    positions = offset + jnp.arange(SEQ, dtype=jnp.int32)[None, :]
    return {
        "x": x,
        "positions": positions,
        "attn_norm": gain((L, D)),
        "w_in": normal((L, D, D_IN), D ** -0.5),
        "cmp_pos_k": normal((L, CMP_BLOCK, d), 0.1),
        "cmp_k_w1": normal((L, CMP_BLOCK * d, CMP_HIDDEN), (CMP_BLOCK * d) ** -0.5),
        "cmp_k_w2": normal((L, CMP_HIDDEN, d), CMP_HIDDEN ** -0.5),
        "cmp_pos_v": normal((L, CMP_BLOCK, d), 0.1),
        "cmp_v_w1": normal((L, CMP_BLOCK * d, CMP_HIDDEN), (CMP_BLOCK * d) ** -0.5),
        "cmp_v_w2": normal((L, CMP_HIDDEN, d), CMP_HIDDEN ** -0.5),
        "diff_lq1": normal((L, d), 0.1),
        "diff_lk1": normal((L, d), 0.1),
        "diff_lq2": normal((L, d), 0.1),
        "diff_lk2": normal((L, d), 0.1),
        "diff_subln": gain((L, DIFF_V_DIM)),
        "sgu_norm": gain((L, SGU_WIDTH)),
        "sgu_w": normal((L, SGU_GROUPS, SGU_CHUNK, SGU_CHUNK), 0.5 * SGU_CHUNK ** -0.5),
        "sgu_b": 1.0 + normal((L, SGU_GROUPS, SGU_CHUNK), 0.1),
        "w_branch_a": normal((L, NSA_WIDTH, D), NSA_WIDTH ** -0.5),
        "w_branch_b": normal((L, DIFF_WIDTH, D), DIFF_WIDTH ** -0.5),
        "w_branch_c": normal((L, SGU_WIDTH, D), SGU_WIDTH ** -0.5),
        "w_merge": normal((L, D, N_BRANCHES * D), D ** -0.5),
        "b_merge": normal((L, N_BRANCHES * D), 0.02),
        "w_out": normal((L, D, D), D ** -0.5),
        "ffn_norm": gain((L, D)),
        "w_ffn1": normal((L, D, D_FF), D ** -0.5),
        "w_ffn3": normal((L, D, D_FF), D ** -0.5),
        "w_ffn2": normal((L, D_FF, D), D_FF ** -0.5),
        "final_norm": gain((D,)),
    }


def reference(x, positions, attn_norm, w_in, cmp_pos_k, cmp_k_w1, cmp_k_w2, cmp_pos_v, cmp_v_w1,
              cmp_v_w2, diff_lq1, diff_lk1, diff_lq2, diff_lk2, diff_subln, sgu_norm, sgu_w, sgu_b,
              w_branch_a, w_branch_b, w_branch_c, w_merge, b_merge, w_out, ffn_norm, w_ffn1,
              w_ffn3, w_ffn2, final_norm):
    B, S, D = x.shape
    for l in range(DEPTH):
        lambda_init = 0.8 - 0.6 * math.exp(-0.3 * l)
        h = rmsnorm(x, attn_norm[l])
        proj = h @ w_in[l]
        q_a, kv_a, g_a, q_b, k_b, v_b, uv_c = jnp.split(proj, IN_OFFSETS, axis=-1)
        k_cmp, v_cmp, k_slc, v_slc, k_win, v_win = [
            t.reshape(B, S, NSA_KV_GROUPS, HEAD_DIM) for t in jnp.split(kv_a, 6, axis=-1)]
        o_a = nsa_attention(q_a.reshape(B, S, NSA_HEADS, HEAD_DIM), k_cmp, v_cmp, k_slc, v_slc,
                            k_win, v_win, jax.nn.sigmoid(g_a).reshape(B, S, NSA_HEADS, 3), positions,
                            cmp_pos_k[l], cmp_k_w1[l], cmp_k_w2[l], cmp_pos_v[l], cmp_v_w1[l], cmp_v_w2[l])
        o_b = diff_attention(q_b.reshape(B, S, 2 * DIFF_HEADS, HEAD_DIM),
                             k_b.reshape(B, S, 2 * DIFF_HEADS, HEAD_DIM),
                             v_b.reshape(B, S, DIFF_HEADS, DIFF_V_DIM), positions,
                             diff_lq1[l], diff_lk1[l], diff_lq2[l], diff_lk2[l], diff_subln[l], lambda_init)
        o_c = chunked_sgu(uv_c, sgu_norm[l], sgu_w[l], sgu_b[l])
        gates = jax.nn.sigmoid(h @ w_merge[l] + b_merge[l]).reshape(B, S, N_BRANCHES, D)
        mixed = (gates[:, :, 0] * (o_a @ w_branch_a[l])
                 + gates[:, :, 1] * (o_b @ w_branch_b[l])
                 + gates[:, :, 2] * (o_c @ w_branch_c[l]))
        x = x + mixed @ w_out[l]
        h = rmsnorm(x, ffn_norm[l])
        x = x + (jax.nn.silu(h @ w_ffn1[l]) * (h @ w_ffn3[l])) @ w_ffn2[l]
    return rmsnorm(x, final_norm)
```

```python
import contextlib
import math
import numpy as np
import ml_dtypes
import concourse.bass as bass
import concourse.mybir as mybir
from concourse.bass_utils import run_bass_kernel_spmd

F32 = mybir.dt.float32
BF16 = mybir.dt.bfloat16
I32 = mybir.dt.int32
AF = mybir.ActivationFunctionType
ALU = mybir.AluOpType
AX = mybir.AxisListType

D = 1024
DFF = 2816
EPS = 1e-6
NEGM = 32768.0

COMPUTE = ("pe", "act", "dve", "pool")
DMAQ = {"q_sp": "sp", "q_pool": "pool", "q_act": "act", "q_cc": "pool"}
QINC = {"q_sp": 16, "q_pool": 16, "q_act": 16, "q_cc": 1}
NSEM_PER_Q = 10


class Buf:
    __slots__ = ("name", "w", "r", "excl")

    def __init__(self, name):
        self.name = name
        self.w = {}
        self.r = {}
        self.excl = False


class Tn:
    def __init__(self, t, name):
        self.t = t
        self.b = Buf(name)

    def __getitem__(self, idx):
        return self.t[idx]


def _norm(lst):
    out = []
    for x in lst:
        if isinstance(x, tuple):
            b, k = x
        else:
            b, k = x, None
        if isinstance(b, Tn):
            b = b.b
        out.append((b, k))
    return out


class Prog:
    def __init__(self, nc):
        self.nc = nc
        self.ops = []

    def add(self, stream, fn, reads=(), writes=()):
        i = len(self.ops)
        deps = {}

        def ck(d, k):
            if k is None:
                return list(d.keys())
            return [kk for kk in (k, None) if kk in d]

        reads = _norm(reads)
        writes = _norm(writes)
        writes = writes + [(b, k) for (b, k) in reads if b.excl and (b, k) not in writes]
        for (b, k) in reads:
            for kk in ck(b.w, k):
                deps[b.w[kk]] = True
        for (b, k) in writes:
            for kk in ck(b.w, k):
                deps.setdefault(b.w[kk], False)
            for kk in ck(b.r, k):
                for s, j in b.r[kk].items():
                    if isinstance(j, list):
                        for jj in j:
                            deps.setdefault(jj, False)
                    else:
                        deps.setdefault(j, False)
        for (b, k) in reads:
            d = b.r.setdefault(k, {})
            if stream in DMAQ:
                d.setdefault(stream, [])
                d[stream].append(i)
            else:
                d[stream] = i
        for (b, k) in writes:
            if k is None:
                b.w = {None: i}
                b.r = {}
            else:
                b.w[k] = i
                b.r.pop(k, None)
        deps.pop(i, None)
        self.ops.append(dict(stream=stream, fn=fn, deps=deps, sig=None))
        return i

    def emit(self, es):
        nc = self.nc
        ops = self.ops
        need = [[] for _ in ops]
        for c, o in enumerate(ops):
            cs = o["stream"]
            best = {}
            for p, raw in o["deps"].items():
                ps = ops[p]["stream"]
                if ps in DMAQ:
                    need[c].append(p)
                    continue
                if ps == cs:
                    if cs == "pe" or not raw:
                        continue
                if ps not in best or best[ps] < p:
                    best[ps] = p
            need[c].extend(best.values())
        signal = [False] * len(ops)
        for c in range(len(ops)):
            for p in need[c]:
                signal[p] = True
        sems = {}
        for s in COMPUTE:
            sems[s] = es.enter_context(nc.semaphore("s_" + s))
        qsems = {}
        for q in DMAQ:
            qsems[q] = [es.enter_context(nc.semaphore("s_%s_%d" % (q, j))) for j in range(NSEM_PER_Q)]
        cnt = {s: 0 for s in COMPUTE}
        qcnt = {q: 0 for q in DMAQ}
        qsemcnt = {q: [0] * NSEM_PER_Q for q in DMAQ}
        for i, o in enumerate(ops):
            s = o["stream"]
            if s in DMAQ:
                j = qcnt[s] % NSEM_PER_Q
                qcnt[s] += 1
                qsemcnt[s][j] += 1
                o["sig"] = (qsems[s][j], QINC[s] * qsemcnt[s][j], (s, j))
                o["prev"] = (qsems[s][j], QINC[s] * (qsemcnt[s][j] - 1), (s, j))
            elif signal[i]:
                cnt[s] += 1
                o["sig"] = (sems[s], cnt[s], s)
        per_eng = {e: [] for e in ("pe", "act", "dve", "pool", "sp")}
        for i, o in enumerate(ops):
            s = o["stream"]
            per_eng[DMAQ.get(s, s)].append(i)
        waited = {e: {} for e in per_eng}

        def run_engine(ename, eng):
            wd = waited[ename]
            for i in per_eng[ename]:
                o = ops[i]
                ws = []
                for p in need[i]:
                    ws.append(ops[p]["sig"])
                if o["stream"] in DMAQ:
                    if o["prev"][1] > 0:
                        ws.append(o["prev"])
                for sem, val, key in ws:
                    if wd.get(key, 0) >= val:
                        continue
                    wd[key] = val
                    eng.wait_ge(sem, val)
                ins = o["fn"](eng)
                if o["sig"] is not None:
                    sem, val, key = o["sig"]
                    ins.then_inc(sem, QINC.get(o["stream"], 1))
            for q, e in DMAQ.items():
                if e != ename:
                    continue
                for j in range(NSEM_PER_Q):
                    v = QINC[q] * qsemcnt[q][j]
                    if v > 0 and wd.get((q, j), 0) < v:
                        eng.wait_ge(qsems[q][j], v)

        with nc.Block() as block:
            @block.tensor
            def _(e):
                run_engine("pe", e)

            @block.scalar
            def _(e):
                run_engine("act", e)

            @block.vector
            def _(e):
                run_engine("dve", e)

            @block.gpsimd
            def _(e):
                run_engine("pool", e)

            @block.sync
            def _(e):
                run_engine("sp", e)


class KB:
    def __init__(self):
        self.nc = bass.Bass("TRN2", target_bir_lowering=False)
        self.es = contextlib.ExitStack()
        self.pr = Prog(self.nc)
        self.off = 16896
        self.maxoff = 0
        self.sfx = ""
        self.io = {}
        self.uid = 0
        self.alloc_log = []
        self.prev_list = []
        self.dram = {}
        self.psums = {}
        self.dummy = self.sb("dummy", [128, 16], F32)
        self.base = self.off

    SHARED = ("posB", "identb", "identN", "cmpbias", "cdiag", "EE", "eaW", "ovl", "cm01")

    def begin_phase(self, sfx):
        self.sfx = sfx
        self.off = self.base
        self.alloc_log = []

    def phase_barrier(self):
        if self.prev_list:
            self.barrier(self.prev_list, list(self.alloc_log))

    def end_phase(self):
        self.prev_list = list(self.alloc_log)

    def sb(self, name, shape, dt):
        nb = 4 if dt in (F32, I32) else 2
        size = nb
        for s_ in shape[1:]:
            size *= s_
        size = (size + 63) // 64 * 64
        off = self.off
        self.off += size
        self.maxoff = max(self.maxoff, self.off)
        assert self.off <= 229376 - 256, ("SBUF overflow", name, self.off)
        self.uid += 1
        name = "%s%s_%d" % (name, self.sfx, self.uid)
        t = Tn(self.nc.alloc_sbuf_tensor_at(name, list(shape), dt, offset=off), name)
        self.alloc_log.append(t)
        return t

    def barrier(self, old, new):
        d = self.dummy
        self.pr.add("dve", lambda e: e.memset(d[:, :], 0.0), list(old), list(new) + list(old))

    def psum(self, name, shape=(128, 512), dt=F32):
        if name in self.psums:
            return self.psums[name]
        t = Tn(self.es.enter_context(self.nc.psum_tensor(name, list(shape), dt)), name)
        t.b.excl = True
        self.psums[name] = t
        return t

    def din(self, name, shape, dt=F32):
        if name in self.io:
            return self.io[name]
        if name not in self.SHARED:
            name = name + self.sfx
        if name in self.dram:
            return self.dram[name]
        t = self.nc.dram_tensor(name, list(shape), dt, kind="ExternalInput")
        self.dram[name] = Tn(t.ap(), name)
        return self.dram[name]

    def dout(self, name, shape, dt=F32):
        if name in self.io:
            return self.io[name]
        t = self.nc.dram_tensor(name + self.sfx, list(shape), dt, kind="ExternalOutput")
        return Tn(t.ap(), name)

    def dscr(self, name, shape, dt):
        t = self.nc.dram_tensor(name + self.sfx, list(shape), dt, kind="Internal")
        return Tn(t.ap(), name)

    def dma(self, q, out, in_, r, w):
        self.pr.add(q, lambda e: e.dma_start(out=out, in_=in_), r, w)

    def mm(self, out, lhsT, rhs, start, stop, r, w):
        self.pr.add("pe", lambda e: e.matmul(out, lhsT, rhs, start=start, stop=stop, skip_group_check=True), r, w)

    def tr(self, out, in_, ident, r, w):
        self.pr.add("pe", lambda e: e.transpose(out, in_, ident), r, w)

    def act(self, out, in_, func, r, w, bias=None, scale=None):
        kw = {}
        if bias is not None:
            kw["bias"] = bias
        if scale is not None:
            kw["scale"] = scale
        self.pr.add("act", lambda e: e.activation(out=out, in_=in_, func=func, **kw), r, w)

    def ts(self, eng, out, in0, s1, s2, op0, op1, r, w, accum_out=None):
        if op1 is None:
            self.pr.add(eng, lambda e: e.tensor_scalar(out=out, in0=in0, scalar1=s1, scalar2=None, op0=op0), r, w)
        elif accum_out is not None:
            self.pr.add(eng, lambda e: e.tensor_scalar(out=out, in0=in0, scalar1=s1, scalar2=s2, op0=op0, op1=op1,
                                                      accum_out=accum_out), r, w)
        else:
            self.pr.add(eng, lambda e: e.tensor_scalar(out=out, in0=in0, scalar1=s1, scalar2=s2, op0=op0, op1=op1), r, w)

    def tt(self, eng, out, in0, in1, op, r, w):
        self.pr.add(eng, lambda e: e.tensor_tensor(out=out, in0=in0, in1=in1, op=op), r, w)

    def stt(self, out, in0, scalar, in1, op0, op1, r, w, accum_out=None):
        if accum_out is None:
            self.pr.add("dve", lambda e: e.scalar_tensor_tensor(out=out, in0=in0, scalar=scalar, in1=in1, op0=op0, op1=op1), r, w)
        else:
            self.pr.add("dve", lambda e: e.scalar_tensor_tensor(out=out, in0=in0, scalar=scalar, in1=in1, op0=op0, op1=op1,
                                                                accum_out=accum_out), r, w)

    def cp(self, eng, out, in_, r, w):
        if eng == "act":
            self.pr.add("act", lambda e: e.copy(out=out, in_=in_), r, w)
        else:
            self.pr.add(eng, lambda e: e.tensor_copy(out=out, in_=in_), r, w)

    def recip(self, out, in_, r, w):
        self.pr.add("dve", lambda e: e.reciprocal(out=out, in_=in_), r, w)

    def memset(self, eng, ap, val, w):
        self.pr.add(eng, lambda e: e.memset(ap, val), (), w)

    def finish(self):
        self.pr.emit(self.es)
        self.es.close()
        return self.nc


def _fin(kb, own):
    kb.end_phase()
    kb.io = {}
    if own:
        return kb.finish()
    return None


def load_w(kb, q, dst, src_ap, kchunks, r=(), extra_w=()):
    v = src_ap.t.rearrange("(c p) n -> p c n", p=128)
    for c in range(kchunks):
        kb.dma(q, dst[:, c, :], v[:, c, :], [src_ap] + list(r), [(dst, c)] + list(extra_w))


def rmsnorm_fm(kb, xg, hT, gcol, ones, epsT, ps_ss, tmpA, rstdB, G=512):
    kb.tt("pool", hT[:, :, :], xg[:, :, :], xg[:, :, :], ALU.mult, [xg], [hT])
    for c in range(8):
        kb.mm(ps_ss[:, :G], ones[:, :], hT[:, c, :], c == 0, c == 7, [ones, hT], [ps_ss])
    kb.act(tmpA[:, :G], ps_ss[:, :G], AF.Sqrt, [ps_ss, epsT], [tmpA], bias=epsT[:, 0:1], scale=1.0 / D)
    kb.recip(rstdB[:, :G], tmpA[:, :G], [tmpA], [rstdB])
    for c in range(8):
        kb.stt(hT[:, c, :], xg[:, c, :], gcol[:, c:c + 1], rstdB[:, :G], ALU.mult, ALU.mult, [xg, gcol, rstdB], [(hT, c)])


def build_c1(T, kb=None, io=None, sfx=""):
    own = kb is None
    if own:
        kb = KB()
    kb.io = io or {}
    kb.begin_phase(sfx)
    G = 512
    NG = T // G
    xT = kb.din("xT", [D, T])
    osrc_fn = kb.io.get("osrc_fn")
    osrc_tn = kb.io.get("osrc_tn")
    if osrc_fn is None:
        oaT = kb.din("oaT", [512, T], BF16)
        obT = kb.din("obT", [512, T], BF16)
        oaTv = oaT.t.rearrange("(c p) t -> p c t", p=128)
        obTv = obT.t.rearrange("(c p) t -> p c t", p=128)
    gA_d = kb.din("gA", [128, 8])
    wsgu_d = kb.din("w_sgu", [D, 1024])
    gB_d = kb.din("sgu_gB", [128, 512])
    swT_d = kb.din("sgu_wT", [128, 4, 128])
    sbB_d = kb.din("sgu_bB", [128, 4, 128])
    cm_d = kb.din("cm01", [128, 128])
    wbr_d = kb.din("w_br", [1536, D])
    wm_d = kb.din("w_merge", [D, 3072])
    bm_d = kb.din("bm", [128, 24])
    wo_d = kb.din("w_out", [D, D])
    x1T = kb.dout("x1T", [D, T])

    Wsgu = kb.sb("Wsgu", [128, 8, 1024], BF16)
    Wbr = kb.sb("Wbr", [128, 12, 1024], BF16)
    Wm = kb.sb("Wm", [128, 8, 3072], BF16)
    Wo = kb.sb("Wo", [128, 8, 1024], BF16)
    gA = kb.sb("gA_s", [128, 8], F32)
    gB = kb.sb("gB_s", [128, 512], F32)
    swT = kb.sb("swT_s", [128, 4, 128], F32)
    swTb = kb.sb("swTb", [128, 4, 128], BF16)
    sbB = kb.sb("sbB_s", [128, 4, 128], F32)
    cm = kb.sb("cm_s", [128, 128], F32)
    bm = kb.sb("bm_s", [128, 24], F32)
    ones = kb.sb("ones", [128, 128], BF16)
    epsT = kb.sb("epsT", [128, 1], F32)
    eps512 = kb.sb("eps512", [128, 1], F32)

    xg = kb.sb("xg", [128, 8, G], F32)
    hT = kb.sb("hT", [128, 8, G], BF16)
    tmpA = kb.sb("tmpA", [128, G], F32)
    rstdB = kb.sb("rstdB", [128, G], F32)
    uT = kb.sb("uT", [128, 4, G], BF16)
    vg = kb.sb("vg", [128, 512], F32)
    vjunk = kb.sb("vjunk", [128, 512], F32)
    vn = kb.sb("vn", [128, 512], BF16)
    ssv = kb.sb("ssv", [128, 4], F32)
    ocT = kb.sb("ocT", [128, 4, G], BF16)
    stmp = kb.sb("stmp", [128, G], F32)
    oa = kb.sb("oa", [128, 4, G], BF16)
    ob = kb.sb("ob", [128, 4, G], BF16)
    gate = [kb.sb("gate%d" % i, [128, G], F32) for i in range(2)]
    acc = kb.sb("acc", [128, G], F32)
    mixedT = kb.sb("mixedT", [128, 8, G], BF16)
    PS = [kb.psum("ps%d" % i) for i in range(7)]
    kb.phase_barrier()

    kb.memset("dve", ones[:, :], 1.0, [ones])
    kb.memset("dve", epsT[:, :], EPS, [epsT])
    kb.memset("dve", eps512[:, :], EPS, [eps512])
    for (dst, src) in ((gA, gA_d), (gB, gB_d), (sbB, sbB_d), (cm, cm_d), (bm, bm_d), (swT, swT_d)):
        kb.dma("q_sp", dst.t[:], src.t, [src], [dst])
    for g4 in range(4):
        kb.tt("dve", swTb[:, g4, :], swT[:, g4, :], cm[:, :], ALU.mult, [swT, cm], [(swTb, g4)])
    load_w(kb, "q_pool", Wsgu, wsgu_d, 8)
    load_w(kb, "q_pool", Wm, wm_d, 8)
    load_w(kb, "q_pool", Wbr, wbr_d, 12)
    load_w(kb, "q_pool", Wo, wo_d, 8)

    xTv = xT.t.rearrange("(c p) t -> p c t", p=128)
    x1Tv = x1T.t.rearrange("(c p) t -> p c t", p=128)
    pi = [0]

    def nps():
        p = PS[pi[0] % 3]
        pi[0] += 1
        return p

    for gi in range(NG):
        t0 = gi * G
        for c in range(8):
            kb.dma("q_sp", xg[:, c, :], xTv[:, c, t0:t0 + G], [xT], [(xg, c)])
        for c in range(4):
            if osrc_fn is None:
                kb.dma("q_sp", oa[:, c, :], oaTv[:, c, t0:t0 + G], [oaT], [(oa, c)])
                kb.dma("q_sp", ob[:, c, :], obTv[:, c, t0:t0 + G], [obT], [(ob, c)])
            else:
                kb.dma("q_sp", oa[:, c, :], osrc_fn(0, c, t0, G), [osrc_tn], [(oa, c)])
                kb.dma("q_sp", ob[:, c, :], osrc_fn(1, c, t0, G), [osrc_tn], [(ob, c)])
        rmsnorm_fm(kb, xg, hT, gA, ones, epsT, nps(), tmpA, rstdB, G)
        for uc in range(4):
            p = nps()
            for k in range(8):
                kb.mm(p[:, :G], Wsgu[:, k, uc * 128:(uc + 1) * 128], hT[:, k, :], k == 0, k == 7, [(Wsgu, k), hT], [p])
            kb.act(uT[:, uc, :], p[:, :G], AF.Gelu_apprx_tanh, [p], [(uT, uc)])
        ps_s = PS[3:7]
        for tt in range(4):
            p = nps()
            for k in range(8):
                kb.mm(p[:, :512], hT[:, k, tt * 128:(tt + 1) * 128], Wsgu[:, k, 512:1024], k == 0, k == 7, [hT, (Wsgu, k)], [p])
            kb.act(vg[:, :], p[:, :512], AF.Gelu_apprx_tanh, [p], [vg])
            kb.stt(vjunk[:, :], vg[:, :], 1.0, vg[:, :], ALU.mult, ALU.mult, [vg], [vjunk, (ssv, tt)], accum_out=ssv[:, tt:tt + 1])
            kb.act(ssv[:, tt:tt + 1], ssv[:, tt:tt + 1], AF.Sqrt, [(ssv, tt), eps512], [(ssv, tt)], bias=eps512[:, 0:1], scale=1.0 / 512)
            kb.recip(ssv[:, tt:tt + 1], ssv[:, tt:tt + 1], [(ssv, tt)], [(ssv, tt)])
            kb.stt(vn[:, :], vg[:, :], ssv[:, tt:tt + 1], gB[:, :], ALU.mult, ALU.mult, [vg, (ssv, tt), gB], [vn])
            for g4 in range(4):
                kb.mm(ps_s[g4][:, tt * 128:(tt + 1) * 128], vn[:, g4 * 128:(g4 + 1) * 128], swTb[:, g4, :], True, True,
                      [vn, swTb], [(ps_s[g4], tt)])
        for g4 in range(4):
            for tt in range(4):
                kb.tt("dve", stmp[:, tt * 128:(tt + 1) * 128], ps_s[g4][:, tt * 128:(tt + 1) * 128], sbB[:, g4, :], ALU.add,
                      [ps_s[g4], sbB], [(stmp, tt)])
            kb.tt("dve", ocT[:, g4, :], stmp[:, :], uT[:, g4, :], ALU.mult, [stmp, (uT, g4)], [(ocT, g4)])
        osrc = (oa, ob, ocT)
        for oc in range(8):
            for br in range(3):
                pg = nps()
                for k in range(8):
                    kb.mm(pg[:, :G], Wm[:, k, br * 1024 + oc * 128: br * 1024 + (oc + 1) * 128], hT[:, k, :], k == 0, k == 7,
                          [(Wm, k), hT], [pg])
                gt = gate[br % 2]
                kb.act(gt[:, :], pg[:, :G], AF.Sigmoid, [pg, bm], [gt], bias=bm[:, br * 8 + oc: br * 8 + oc + 1])
                pb = nps()
                for k in range(4):
                    kb.mm(pb[:, :G], Wbr[:, br * 4 + k, oc * 128:(oc + 1) * 128], osrc[br][:, k, :], k == 0, k == 3,
                          [(Wbr, br * 4 + k), (osrc[br], k)], [pb])
                if br == 0:
                    kb.tt("dve", acc[:, :], gt[:, :], pb[:, :G], ALU.mult, [gt, pb], [acc])
                else:
                    kb.tt("dve", gt[:, :], gt[:, :], pb[:, :G], ALU.mult, [gt, pb], [gt])
                    if br == 1:
                        kb.tt("dve", acc[:, :], acc[:, :], gt[:, :], ALU.add, [acc, gt], [acc])
                    else:
                        kb.tt("dve", mixedT[:, oc, :], acc[:, :], gt[:, :], ALU.add, [acc, gt], [(mixedT, oc)])
        for oc in range(8):
            p = nps()
            for k in range(8):
                kb.mm(p[:, :G], Wo[:, k, oc * 128:(oc + 1) * 128], mixedT[:, k, :], k == 0, k == 7, [(Wo, k), (mixedT, k)], [p])
            kb.tt("dve", xg[:, oc, :], xg[:, oc, :], p[:, :G], ALU.add, [(xg, oc), p], [(xg, oc)])
            kb.dma("q_sp", x1Tv[:, oc, t0:t0 + G], xg[:, oc, :], [(xg, oc)], [x1T])
    return _fin(kb, own)


def build_c2(T, kb=None, io=None, sfx=""):
    own = kb is None
    if own:
        kb = KB()
    kb.io = io or {}
    kb.begin_phase(sfx)
    G = 512
    NG = T // G
    NF = DFF // 128
    x1T = kb.din("x1T", [D, T])
    gF_d = kb.din("gF", [128, 8])
    gZ_d = kb.din("gZ", [128, 8])
    w1_d = kb.din("w1", [D, DFF])
    w3_d = kb.din("w3", [D, DFF])
    w2_d = kb.din("w2", [DFF, D])
    x2T = kb.dout("x2T", [D, T]) if kb.io.get("want_x2", True) else None
    x2nT = kb.dout("x2nT", [D, T]) if kb.io.get("want_x2n", True) else None

    W1 = kb.sb("W1", [128, 8, DFF], BF16)
    W3 = kb.sb("W3", [128, 8, DFF], BF16)
    W2 = kb.sb("W2", [128, NF, D], BF16)
    gF = kb.sb("gF_s", [128, 8], F32)
    gZ = kb.sb("gZ_s", [128, 8], F32)
    ones = kb.sb("ones", [128, 128], BF16)
    epsT = kb.sb("epsT", [128, 1], F32)
    xg = kb.sb("xg", [128, 8, G], F32)
    hT = kb.sb("hT", [128, 8, G], BF16)
    tmpA = kb.sb("tmpA", [128, G], F32)
    rstdB = kb.sb("rstdB", [128, G], F32)
    aT = kb.sb("aT", [128, NF, G], BF16)
    sl = [kb.sb("sl%d" % i, [128, G], F32) for i in range(2)]
    PS = [kb.psum("ps%d" % i) for i in range(7)]
    kb.phase_barrier()
    want_x2 = kb.io.get("want_x2", True)
    want_x2n = kb.io.get("want_x2n", True)

    kb.memset("dve", ones[:, :], 1.0, [ones])
    kb.memset("dve", epsT[:, :], EPS, [epsT])
    kb.dma("q_sp", gF.t[:], gF_d.t, [gF_d], [gF])
    kb.dma("q_sp", gZ.t[:], gZ_d.t, [gZ_d], [gZ])
    load_w(kb, "q_pool", W1, w1_d, 8)
    load_w(kb, "q_pool", W3, w3_d, 8)
    load_w(kb, "q_pool", W2, w2_d, NF)
    x1Tv = x1T.t.rearrange("(c p) t -> p c t", p=128)
    x2Tv = x2T.t.rearrange("(c p) t -> p c t", p=128) if x2T is not None else None
    x2nTv = x2nT.t.rearrange("(c p) t -> p c t", p=128) if x2nT is not None else None
    pi = [0]

    def nps():
        p = PS[pi[0] % 7]
        pi[0] += 1
        return p

    for gi in range(NG):
        t0 = gi * G
        for c in range(8):
            kb.dma("q_sp", xg[:, c, :], x1Tv[:, c, t0:t0 + G], [x1T], [(xg, c)])
        rmsnorm_fm(kb, xg, hT, gF, ones, epsT, nps(), tmpA, rstdB, G)
        for fc in range(NF):
            p1 = nps()
            for k in range(8):
                kb.mm(p1[:, :G], W1[:, k, fc * 128:(fc + 1) * 128], hT[:, k, :], k == 0, k == 7, [(W1, k), hT], [p1])
            p3 = nps()
            for k in range(8):
                kb.mm(p3[:, :G], W3[:, k, fc * 128:(fc + 1) * 128], hT[:, k, :], k == 0, k == 7, [(W3, k), hT], [p3])
            s = sl[fc % 2]
            kb.act(s[:, :], p1[:, :G], AF.Silu, [p1], [s])
            kb.tt("dve", aT[:, fc, :], s[:, :], p3[:, :G], ALU.mult, [s, p3], [(aT, fc)])
        for oc in range(8):
            p = nps()
            for k in range(NF):
                kb.mm(p[:, :G], W2[:, k, oc * 128:(oc + 1) * 128], aT[:, k, :], k == 0, k == NF - 1, [(W2, k), (aT, k)], [p])
            kb.tt("dve", xg[:, oc, :], xg[:, oc, :], p[:, :G], ALU.add, [(xg, oc), p], [(xg, oc)])
            if want_x2:
                kb.dma("q_sp", x2Tv[:, oc, t0:t0 + G], xg[:, oc, :], [(xg, oc)], [x2T])
        if want_x2n:
            rmsnorm_fm_f32(kb, xg, hT, gZ, ones, epsT, nps(), tmpA, rstdB, G)
            for oc in range(8):
                kb.dma("q_sp", x2nTv[:, oc, t0:t0 + G], xg[:, oc, :], [(xg, oc)], [x2nT])
    return _fin(kb, own)


def rmsnorm_fm_f32(kb, xg, hT, gcol, ones, epsT, ps_ss, tmpA, rstdB, G=512):
    kb.tt("pool", hT[:, :, :], xg[:, :, :], xg[:, :, :], ALU.mult, [xg], [hT])
    for c in range(8):
        kb.mm(ps_ss[:, :G], ones[:, :], hT[:, c, :], c == 0, c == 7, [ones, hT], [ps_ss])
    kb.act(tmpA[:, :G], ps_ss[:, :G], AF.Sqrt, [ps_ss, epsT], [tmpA], bias=epsT[:, 0:1], scale=1.0 / D)
    kb.recip(rstdB[:, :G], tmpA[:, :G], [tmpA], [rstdB])
    for c in range(8):
        kb.stt(xg[:, c, :], xg[:, c, :], gcol[:, c:c + 1], rstdB[:, :G], ALU.mult, ALU.mult, [(xg, c), gcol, rstdB], [(xg, c)])


def _col8(v):
    return np.ascontiguousarray(v.reshape(8, 128).T)


def prep_c1(l, inp):
    w_in = inp["w_in"][l]
    d = {}
    d["gA"] = _col8(inp["attn_norm"][l])
    d["w_sgu"] = np.ascontiguousarray(w_in[:, 2840:3864])
    d["sgu_gB"] = np.ascontiguousarray(np.broadcast_to(inp["sgu_norm"][l][None, :], (128, 512)))
    d["sgu_wT"] = np.ascontiguousarray(inp["sgu_w"][l].transpose(2, 0, 1))
    d["sgu_bB"] = np.ascontiguousarray(np.broadcast_to(inp["sgu_b"][l][None], (128, 4, 128)))
    d["cm01"] = np.triu(np.ones((128, 128), np.float32))
    d["w_br"] = np.ascontiguousarray(np.concatenate([inp["w_branch_a"][l], inp["w_branch_b"][l], inp["w_branch_c"][l]], 0))
    d["w_merge"] = np.ascontiguousarray(inp["w_merge"][l])
    d["bm"] = np.ascontiguousarray(inp["b_merge"][l].reshape(24, 128).T)
    d["w_out"] = np.ascontiguousarray(inp["w_out"][l])
    return d


def prep_c2(l, inp):
    d = {}
    d["gF"] = _col8(inp["ffn_norm"][l])
    d["gZ"] = _col8(inp["final_norm"])
    d["w1"] = np.ascontiguousarray(inp["w_ffn1"][l])
    d["w3"] = np.ascontiguousarray(inp["w_ffn3"][l])
    d["w2"] = np.ascontiguousarray(inp["w_ffn2"][l])
    return d


NFM = 17
FM_ROPE = {0: 9, 1: 10, 3: 11, 4: 12, 5: 13, 6: 14, 7: 15, 8: 16}
TWO_PI = 2.0 * math.pi
C1 = 6.28125
C2 = TWO_PI - C1
BIGS = 1.0e9


def build_ab(S, stop=None, kb=None, io=None, sfx=""):
    own = kb is None
    if own:
        kb = KB()
    kb.io = io or {}
    kb.begin_phase(sfx)
    PG = 256
    NPG = S // PG
    QG = 512
    NQ = S // QG
    NT = S // 128
    NCP = S // 16
    ncmp = NCP - 1
    NCT = (NCP + 127) // 128
    NCW = NCT * 128

    xT = kb.din("xT", [D, S])
    posB_d = kb.din("posB", [128, S], I32)
    gA_d = kb.din("gA", [128, 8])
    wfm_d = kb.din("w_fm", [D, NFM * 128])
    wtm_d = kb.din("w_tm", [D, 396])
    identb_d = kb.din("identb", [128, 128], BF16)
    identN_d = kb.din("identN", [128, 128], BF16)
    colc_d = kb.din("colc", [128, 4])
    cmpbias_d = kb.din("cmpbias", [128, 5, 512], BF16)
    cdiag_d = kb.din("cdiag", [128, 2, 128], BF16)
    EE_d = kb.din("EE", [128, NT, 128], BF16)
    eaW_d = kb.din("eaW", [128, 2, 254])
    ovl_d = kb.din("ovl", [128, NCT, 128], BF16)
    w1kv_d = kb.din("w1kv", [128, 32, 256])
    posT_d = kb.din("posT", [128, 32])
    w2k_d = kb.din("w2k", [128, 2, 128])
    w2v_d = kb.din("w2v", [128, 2, 64])
    lqk_d = kb.din("lqk", [128, 4, 64])
    sublnB_d = kb.din("sublnB", [128, 128])
    oT = kb.dout("oT", [512, S], BF16)
    qscr = kb.dscr("qscr", [7, 128, S], BF16)

    ident = kb.sb("ident", [128, 128], BF16)
    identN = kb.sb("identN", [128, 128], BF16)
    ones = kb.sb("ones", [128, 128], BF16)
    gA = kb.sb("gA_s", [128, 8], F32)
    epsT = kb.sb("epsT", [128, 1], F32)
    eps128 = kb.sb("eps128", [128, 1], F32)
    colc = kb.sb("colc_s", [128, 4], F32)
    cmpbias = kb.sb("cmpbias_s", [128, 5, 512], BF16)
    cdiag = kb.sb("cdiag_s", [128, 2, 128], BF16)
    eaW = kb.sb("eaW_s", [128, 2, 254], F32)
    sublnB = kb.sb("sublnB_s", [128, 128], F32)
    lqk = kb.sb("lqk_s", [128, 4, 64], F32)
    lsc = kb.sb("lsc", [128, 8], F32)
    kslc = kb.sb("kslc", [128, S], BF16)
    kb0 = kb.sb("kb0", [128, S], BF16)
    kb1 = kb.sb("kb1", [128, S], BF16)
    Vnsa = kb.sb("Vnsa", [128, NT, 130], BF16)
    Vd = kb.sb("Vd", [128, NT, 258], BF16)
    gates = kb.sb("gates", [128, NT, 12], F32)
    kcT = kb.sb("kcT", [128, NCW], BF16)
    Vc = kb.sb("Vc", [128, NCT, 193], BF16)
    kvT1 = kb.sb("kvT1", [128, S], BF16)
    EE = Tn(kb.nc.alloc_sbuf_tensor_at("EE_s" + kb.sfx, [128, NT, 128], BF16, offset=kb.off - 2 * S), "EE_s")
    PS = [kb.psum("ps%d" % i) for i in range(7)]
    PSB = kb.psum("psb", [128, 1024], BF16)
    mark = kb.off

    Wfm = kb.sb("Wfm", [128, 8, NFM * 128], BF16)
    Wtm = kb.sb("Wtm", [128, 8, 396], BF16)
    xg = kb.sb("xg", [128, 8, PG], F32)
    hT = kb.sb("hT", [128, 8, PG], BF16)
    tmpA = kb.sb("tmpA", [128, PG], F32)
    rstdB = kb.sb("rstdB", [128, PG], F32)
    posi = kb.sb("posi", [128, PG], I32)
    ang = kb.sb("ang", [128, PG], F32)
    ra = kb.sb("ra", [128, PG], F32)
    rk = kb.sb("rk", [128, PG], F32)
    rki = kb.sb("rki", [128, PG], I32)
    rfix = kb.sb("rfix", [128, PG], F32)
    cosT = kb.sb("cosT", [128, PG], F32)
    sinT = kb.sb("sinT", [128, PG], F32)
    t1 = kb.sb("t1", [128, PG], F32)
    t2 = kb.sb("t2", [128, PG], F32)
    qst = [kb.sb("qstP%d" % i, [128, 7, PG], BF16) for i in range(2)]
    P_list = [Wfm, Wtm, xg, hT, tmpA, rstdB, posi, ang, ra, rk, rki, rfix, cosT, sinT, t1, t2] + qst
    kb.alloc_log.append(EE)
    kb.phase_barrier()

    kb.memset("dve", ones[:, :], 1.0, [ones])
    kb.memset("dve", epsT[:, :], EPS, [epsT])
    kb.memset("dve", eps128[:, :], EPS, [eps128])
    kb.memset("pool", Vnsa[:, :, :], 1.0, [Vnsa])
    kb.memset("pool", Vd[:, :, :], 1.0, [Vd])
    for (dst, src) in ((ident, identb_d), (identN, identN_d), (gA, gA_d), (colc, colc_d), (cmpbias, cmpbias_d),
                       (cdiag, cdiag_d), (eaW, eaW_d), (sublnB, sublnB_d), (lqk, lqk_d)):
        kb.dma("q_sp", dst.t[:], src.t, [src], [dst])
    load_w(kb, "q_pool", Wfm, wfm_d, 8)
    load_w(kb, "q_pool", Wtm, wtm_d, 8)
    invf = colc[:, 0:1]
    sgn = colc[:, 1:2]
    kb.tt("dve", lqk[:, 0, :], lqk[:, 0, :], lqk[:, 1, :], ALU.mult, [lqk], [lqk])
    kb.tt("dve", lqk[:, 2, :], lqk[:, 2, :], lqk[:, 3, :], ALU.mult, [lqk], [lqk])
    kb.pr.add("dve", lambda e: e.reduce_sum(out=lsc[:, 0:1], in_=lqk[:, 0, :], axis=AX.X), [lqk], [lsc])
    kb.pr.add("dve", lambda e: e.reduce_sum(out=lsc[:, 1:2], in_=lqk[:, 2, :], axis=AX.X), [lqk], [lsc])
    kb.act(lsc[:, 2:4], lsc[:, 0:2], AF.Exp, [lsc], [lsc])
    kb.tt("dve", lsc[:, 4:5], lsc[:, 3:4], lsc[:, 2:3], ALU.subtract, [lsc], [lsc])
    kb.tt("dve", lsc[:, 4:5], lsc[:, 4:5], colc[:, 2:3], ALU.subtract, [lsc, colc], [lsc])
    neglam = lsc[:, 4:5]
    kb.ts("dve", sublnB[:, :], sublnB[:, :], colc[:, 3:4], None, ALU.mult, None, [sublnB, colc], [sublnB])

    if stop == 'C':
        return _fin(kb, own)
    xTv = xT.t.rearrange("(c p) t -> p c t", p=128)
    pi = [0]

    def nps():
        p = PS[pi[0] % 7]
        pi[0] += 1
        return p

    for gi in range(NPG):
        t0 = gi * PG
        for c in range(8):
            kb.dma("q_sp", xg[:, c, :], xTv[:, c, t0:t0 + PG], [xT], [(xg, c)])
        kb.dma("q_sp", posi[:, :], posB_d[:, t0:t0 + PG], [posB_d], [posi])
        rmsnorm_fm(kb, xg, hT, gA, ones, epsT, nps(), tmpA, rstdB, PG)
        if stop == 'P1':
            return _fin(kb, own)
        kb.cp("dve", ang[:, :], posi[:, :], [posi], [ang])
        kb.ts("dve", ang[:, :], ang[:, :], invf, None, ALU.mult, None, [ang, colc], [ang])
        for which in range(2):
            dst = sinT if which == 0 else cosT
            if which == 0:
                src = ang
            else:
                kb.ts("dve", ra[:, :], ang[:, :], math.pi / 2, None, ALU.add, None, [ang], [ra])
                src = ra
            kb.ts("dve", rk[:, :], src[:, :], 1.0 / TWO_PI, None, ALU.mult, None, [src], [rk])
            kb.cp("dve", rki[:, :], rk[:, :], [rk], [rki])
            kb.cp("dve", rk[:, :], rki[:, :], [rki], [rk])
            kb.pr.add("dve", lambda e, src=src: e.scalar_tensor_tensor(out=rfix[:, :], in0=rk[:, :], scalar=-C1, in1=src[:, :],
                                                                       op0=ALU.mult, op1=ALU.add), [rk, src], [rfix])
            kb.pr.add("dve", lambda e: e.scalar_tensor_tensor(out=rfix[:, :], in0=rk[:, :], scalar=-C2, in1=rfix[:, :],
                                                              op0=ALU.mult, op1=ALU.add), [rk, rfix], [rfix])
            kb.ts("dve", rk[:, :], rfix[:, :], math.pi, -TWO_PI, ALU.is_gt, ALU.mult, [rfix], [rk])
            kb.tt("dve", rfix[:, :], rfix[:, :], rk[:, :], ALU.add, [rfix, rk], [rfix])
            kb.ts("dve", rk[:, :], rfix[:, :], -math.pi, TWO_PI, ALU.is_lt, ALU.mult, [rfix], [rk])
            kb.tt("dve", rfix[:, :], rfix[:, :], rk[:, :], ALU.add, [rfix, rk], [rfix])
            kb.ts("dve", rfix[:, :], rfix[:, :], math.pi, -math.pi, ALU.min, ALU.max, [rfix], [rfix])
            if which == 0:
                kb.act(dst[:, :], rfix[:, :], AF.Sin, [rfix, colc], [dst], scale=sgn)
            else:
                kb.act(dst[:, :], rfix[:, :], AF.Sin, [rfix], [dst])
        if stop == 'P2':
            return _fin(kb, own)
        qs = qst[gi % 2]
        for ch in range(9):
            pp = nps()
            for k in range(8):
                kb.mm(pp[:, :PG], Wfm[:, k, ch * 128:(ch + 1) * 128], hT[:, k, :], k == 0, k == 7, [(Wfm, k), hT], [pp])
            if ch in (0, 1):
                kb.cp("act", qs[:, ch, :], pp[:, :PG], [pp], [(qs, ch)])
            if ch == 2:
                kb.cp("act", kvT1[:, t0:t0 + PG], pp[:, :PG], [pp], [(kvT1, gi)])
                continue
            sw = FM_ROPE[ch]
            psw = nps()
            for k in range(8):
                kb.mm(psw[:, :PG], Wfm[:, k, sw * 128:(sw + 1) * 128], hT[:, k, :], k == 0, k == 7, [(Wfm, k), hT], [psw])
            kb.tt("dve", t1[:, :], pp[:, :PG], cosT[:, :], ALU.mult, [pp, cosT], [t1])
            kb.tt("dve", t2[:, :], psw[:, :PG], sinT[:, :], ALU.mult, [psw, sinT], [t2])
            if ch in (0, 1):
                dst, dk, dt_ = qs[:, 2 + ch, :], (qs, 2 + ch), qs
            elif ch == 3:
                dst, dk, dt_ = kslc[:, t0:t0 + PG], (kslc, gi), kslc
            elif ch == 4:
                dst, dk, dt_ = qs[:, 6, :], (qs, 6), qs
            elif ch in (5, 6):
                dst, dk, dt_ = qs[:, ch - 1, :], (qs, ch - 1), qs
            elif ch == 7:
                dst, dk, dt_ = kb0[:, t0:t0 + PG], (kb0, gi), kb0
            else:
                dst, dk, dt_ = kb1[:, t0:t0 + PG], (kb1, gi), kb1
            kb.tt("pool", dst, t1[:, :], t2[:, :], ALU.add, [t1, t2], [dk])
        if stop == 'P3':
            return _fin(kb, own)
        for j in range(7):
            kb.dma("q_sp", qscr[j, :, t0:t0 + PG], qs[:, j, :], [(qs, j)], [qscr])
        if stop == 'P4':
            return _fin(kb, own)
        for tt in range(PG // 128):
            T_ = gi * (PG // 128) + tt
            p = nps()
            for k in range(8):
                kb.mm(p[:, :396], hT[:, k, tt * 128:(tt + 1) * 128], Wtm[:, k, :], k == 0, k == 7, [hT, (Wtm, k)], [p])
            kb.cp("act", Vnsa[:, T_, 0:64], p[:, 0:64], [p], [(Vnsa, T_)])
            kb.cp("act", Vnsa[:, T_, 65:129], p[:, 64:128], [p], [(Vnsa, T_)])
            kb.act(gates[:, T_, :], p[:, 128:140], AF.Sigmoid, [p], [(gates, T_)])
            kb.cp("dve", Vd[:, T_, 0:128], p[:, 140:268], [p], [(Vd, T_)])
            kb.cp("dve", Vd[:, T_, 129:257], p[:, 268:396], [p], [(Vd, T_)])
        if stop == 'P5' or (stop == 'P6' and gi == 1):
            return _fin(kb, own)

    if stop == 'P':
        return _fin(kb, own)
    kb.off = mark
    w1kv = kb.sb("w1kv", [128, 32, 256], BF16)
    posT = kb.sb("posT", [128, 32], BF16)
    w2k = kb.sb("w2k", [128, 2, 128], BF16)
    w2v = kb.sb("w2v", [128, 2, 64], BF16)
    hidT = kb.sb("hidT", [128, 2, 2, NCW], BF16)
    posb = kb.sb("posb", [128, 4], F32)
    X_list = [w1kv, posT, w2k, w2v, hidT, posb]
    kb.barrier(P_list, X_list)
    for l4 in range(4):
        kb.dma("q_pool", w1kv[:, l4 * 8:(l4 + 1) * 8, :], w1kv_d[:, l4 * 8:(l4 + 1) * 8, :], [w1kv_d], [(w1kv, l4)])
    kb.dma("q_pool", posT[:, :], posT_d.t, [posT_d], [posT])
    kb.dma("q_pool", w2k[:, :, :], w2k_d.t, [w2k_d], [w2k])
    kb.dma("q_pool", w2v[:, :, :], w2v_d.t, [w2v_d], [w2v])
    kb.memset("pool", hidT[:, :, :, :], 0.0, [hidT])
    kvv = kvT1.t.reshape([128, NCP, 16])
    for which in range(2):
        r0 = 64 * which
        for half in range(2):
            ph = nps()
            for l in range(32):
                kb.mm(ph[:, :ncmp], w1kv[r0:r0 + 64, l, half * 128:(half + 1) * 128],
                      kvv[r0:r0 + 64, (l // 16):(l // 16) + ncmp, l % 16], l == 0, l == 31, [w1kv, kvT1], [ph])
            pb = nps()
            for l in range(32):
                kb.mm(pb[:, 0:1], w1kv[r0:r0 + 64, l, half * 128:(half + 1) * 128], posT[r0:r0 + 64, l:l + 1], l == 0, l == 31,
                      [w1kv, posT], [pb])
            idx = which * 2 + half
            kb.cp("dve", posb[:, idx:idx + 1], pb[:, 0:1], [pb], [(posb, idx)])
            kb.act(hidT[:, which, half, :ncmp], ph[:, :ncmp], AF.Gelu_apprx_tanh, [ph, (posb, idx)], [(hidT, idx)],
                   bias=posb[:, idx:idx + 1])
    pk = nps()
    for half in range(2):
        kb.mm(pk[:, :NCW], w2k[:, half, :], hidT[:, 0, half, :], half == 0, half == 1, [w2k, hidT], [pk])
    kb.cp("act", kcT[:, :], pk[:, :NCW], [pk], [kcT])
    kb.memset("pool", Vc[:, :, :], 1.0, [Vc])
    for nt in range(NCT):
        pv = nps()
        for half in range(2):
            kb.mm(pv[:, 0:64], hidT[:, 1, half, nt * 128:(nt + 1) * 128], w2v[:, half, :], half == 0, half == 1, [hidT, w2v], [pv])
        kb.cp("dve", Vc[:, nt, 0:64], pv[:, 0:64], [pv], [Vc])
    kb.dma("q_sp", Vc[:, :, 65:193], ovl_d.t, [ovl_d], [Vc])

    if stop == 'X':
        return _fin(kb, own)
    kb.off = mark
    qstA = [kb.sb("qstA%d" % i, [128, 6, QG], BF16) for i in range(2)]
    kwst = [kb.sb("kwst%d" % i, [128, 1024], BF16) for i in range(2)]
    PT = [kb.sb("PT%d" % i, [128, 512], BF16) for i in range(3)]
    negT = kb.sb("negT", [128, 512], BF16)
    Oev = [kb.sb("Oev%d" % i, [128, 4, 193], F32) for i in range(2)]
    rc = [kb.sb("rc%d" % i, [128, 4], F32) for i in range(2)]
    coef = [kb.sb("coef%d" % i, [128, 4], F32) for i in range(2)]
    ocmp = kb.sb("ocmp", [128, 4, 4, 64], F32)
    oacc = kb.sb("oacc", [128, 4, 256], F32)
    obt = kb.sb("obt", [128, 4, 256], F32)
    imp = kb.sb("imp", [128, 4, 128], F32)
    score = kb.sb("score", [128, 4, 128], F32)
    sc2 = kb.sb("sc2", [128, 4, 128], F32)
    m8 = kb.sb("m8", [128, 4, 8], F32)
    neg01 = kb.sb("neg01", [128, 4, 128], BF16)
    od0 = kb.sb("od0", [128, 4, 128], F32)
    od1 = kb.sb("od1", [128, 4, 128], F32)
    djunk = kb.sb("djunk", [128, 128], F32)
    dss = kb.sb("dss", [128, 4], F32)
    o16 = kb.sb("o16", [128, 4, 512], BF16)
    oTs = kb.sb("oTs", [128, 4, 512], BF16)
    A_list = qstA + kwst + PT + [negT, ocmp, oacc, obt, imp, score, sc2, m8, neg01, od0, od1, djunk, dss, o16, oTs] + Oev + rc + coef
    kb.barrier(X_list + P_list + [kvT1], A_list + [EE])
    kb.dma("q_sp", EE.t[:], EE_d.t, [EE_d], [EE])
    LB = [PS[0], PS[1]]
    ACC = [(PS[2], PS[3]), (PS[4], PS[5])]
    PSM = PS[6]
    li = [0]
    ai = [0]
    pti = [0]
    ei = [0]
    oTv = oT.t.rearrange("(c p) t -> p c t", p=128)

    def nL():
        li[0] += 1
        return LB[li[0] % 2]

    def nA():
        ai[0] += 1
        return ACC[ai[0] % 2]

    def nPT():
        pti[0] += 1
        return PT[pti[0] % 3]

    def nE():
        ei[0] += 1
        return Oev[ei[0] % 2], rc[ei[0] % 2], coef[ei[0] % 2]

    def evac(acc, w, nbank_q):
        ev, r_, cf = nE()
        nb = 4 // nbank_q
        for bnk in range(nb):
            kb.cp("dve", ev[:, bnk * nbank_q:(bnk + 1) * nbank_q, 0:w],
                  acc[bnk][:, 0:nbank_q * w].rearrange("p (q w) -> p q w", w=w), [acc[bnk]], [(ev, bnk)])
        sumcol = 64 if w in (65, 193) else 128
        kb.ts("dve", r_[:, :], ev[:, :, sumcol], 1e-30, None, ALU.max, None, [ev], [r_])
        kb.recip(r_[:, :], r_[:, :], [r_], [r_])
        return ev, r_, cf

    for Q in range(NQ):
        q0 = Q * QG
        qs = qstA[Q % 2]
        kw = kwst[Q % 2]
        for j in range(6):
            kb.dma("q_sp", qs[:, j, :], qscr[j, :, q0:q0 + QG], [qscr], [(qs, j)])
        klo = max(0, q0 - 512)
        kb.dma("q_sp", kw[:, (klo - (q0 - 512)):1024], qscr[6, :, klo:q0 + 512], [qscr], [kw])
        for h in range(4):
            r0 = 64 * (h % 2)
            qa = qs[r0:r0 + 64, h // 2, :]
            nts = [nt for nt in range(NCT) if Q - 4 * nt >= 0]
            acc = nA()
            for ix, nt in enumerate(nts):
                Dd = Q - 4 * nt
                L = nL()
                kb.mm(L[:, :512], kcT[r0:r0 + 64, nt * 128:(nt + 1) * 128], qa, True, Dd > 4, [kcT, qs], [L])
                if Dd <= 4:
                    kb.mm(L[:, :512], identN[:, :], cmpbias[:, Dd, :], False, True, [identN, cmpbias], [L])
                pt = nPT()
                kb.act(pt[:, :], L[:, :512], AF.Exp, [L], [pt], scale=0.125)
                for qt in range(4):
                    kb.mm(acc[qt // 2][:, (qt % 2) * 193:(qt % 2) * 193 + 193], pt[:, qt * 128:(qt + 1) * 128], Vc[:, nt, :],
                          ix == 0 and qt % 2 == 0, ix == len(nts) - 1, [pt, Vc], [acc[qt // 2]])
            ev, r_, cf = evac(acc, 193, 2)
            for qt in range(4):
                kb.ts("dve", ocmp[:, h, qt, :], ev[:, qt, 0:64], r_[:, qt:qt + 1], None, ALU.mult, None, [ev, r_], [(ocmp, h)])
                if h == 0:
                    kb.ts("dve", imp[:, qt, :], ev[:, qt, 65:193], r_[:, qt:qt + 1], None, ALU.mult, None, [ev, r_], [imp])
                else:
                    kb.stt(imp[:, qt, :], ev[:, qt, 65:193], r_[:, qt:qt + 1], imp[:, qt, :], ALU.mult, ALU.add, [ev, r_, imp], [imp])
        for qt in range(4):
            qta = 4 * Q + qt
            off = 126 - 2 * qta
            kb.tt("dve", score[:, qt, :], imp[:, qt, :], eaW[:, 0, off:off + 128], ALU.mult, [imp, eaW], [score])
            kb.tt("dve", score[:, qt, :], score[:, qt, :], eaW[:, 1, off:off + 128], ALU.add, [score, eaW], [score])
            kb.memset("dve", score[:, qt, 0:1], BIGS, [score])
            kb.pr.add("dve", lambda e, qt=qt: e.max(out=m8[:, qt, :], in_=score[:, qt, :]), [score], [m8])
            kb.pr.add("dve", lambda e, qt=qt: e.match_replace(out=sc2[:, qt, :], in_to_replace=m8[:, qt, :],
                                                              in_values=score[:, qt, :], imm_value=-3.0e38), [score, m8], [sc2])
            kb.pr.add("dve", lambda e, qt=qt: e.max(out=m8[:, qt, :], in_=sc2[:, qt, :]), [sc2], [m8])
            kb.ts("dve", neg01[:, qt, :], score[:, qt, :], m8[:, qt, 7:8], 1.0, ALU.is_ge, ALU.subtract, [score, m8], [neg01])
            kb.tr(PSB[:, qt * 128:(qt + 1) * 128], neg01[:, qt, :], ident[:, :], [neg01, ident], [PSB])
        kb.cp("dve", negT[:, :], PSB[:, 0:512], [PSB], [negT])
        for h in range(4):
            r0 = 64 * (h % 2)
            qr = qs[r0:r0 + 64, 2 + h // 2, :]
            acc = nA()
            firstb = True
            ilist = [i for i in range(8) if 4 * Q - 4 + i >= 0]
            for i in ilist:
                kt = 4 * Q - 4 + i
                qts = [qt for qt in range(4) if 0 <= 4 - i + qt <= 4]
                c0, c1 = qts[0] * 128, (qts[-1] + 1) * 128
                L = nL()
                kb.mm(L[:, c0:c1], kw[r0:r0 + 64, i * 128:(i + 1) * 128], qr[:, c0:c1], True, False, [kw, qs], [L])
                for qt in qts:
                    dd = 4 - i + qt
                    if dd == 0:
                        kb.mm(L[:, qt * 128:(qt + 1) * 128], identN[:, :], cdiag[:, 0, :], False, True, [identN, cdiag], [L])
                    elif dd == 4:
                        kb.mm(L[:, qt * 128:(qt + 1) * 128], identN[:, :], cdiag[:, 1, :], False, True, [identN, cdiag], [L])
                pt = nPT()
                kb.act(pt[:, c0:c1], L[:, c0:c1], AF.Exp, [L], [pt], scale=0.125)
                for qt in qts:
                    kb.mm(acc[0][:, qt * 65:qt * 65 + 65], pt[:, qt * 128:(qt + 1) * 128], Vnsa[:, kt, 65:130],
                          firstb, i == qt + 4, [pt, Vnsa], [acc[0]])
                    firstb = False
            ev, r_, cf = evac(acc, 65, 4)
            for qt in range(4):
                qta = 4 * Q + qt
                kb.tt("dve", cf[:, qt:qt + 1], r_[:, qt:qt + 1], gates[:, qta, h * 3 + 2:h * 3 + 3], ALU.mult, [r_, gates], [cf])
                kb.ts("dve", oacc[:, qt, h * 64:(h + 1) * 64], ocmp[:, h, qt, :], gates[:, qta, h * 3:h * 3 + 1], None, ALU.mult, None,
                      [(ocmp, h), gates], [(oacc, h)])
                kb.stt(oacc[:, qt, h * 64:(h + 1) * 64], ev[:, qt, 0:64], cf[:, qt:qt + 1], oacc[:, qt, h * 64:(h + 1) * 64],
                       ALU.mult, ALU.add, [ev, cf, (oacc, h)], [(oacc, h)])

        def causal_unit(qap, kT, vfn, w, nbq, use_sel, qbuf):
            acc = nA()
            nkt = 4 * Q + 4
            for kt in range(nkt):
                i = kt - 4 * Q
                c0 = max(i, 0) * 128
                L = nL()
                kb.mm(L[:, c0:512], kT[:, kt * 128:(kt + 1) * 128], qap[:, c0:512], True, False, [kT_tn[0], qbuf], [L])
                if use_sel:
                    kb.mm(L[:, c0:512], EE[:, kt, :], negT[:, c0:512], False, False, [EE, negT], [L])
                if i >= 0:
                    kb.mm(L[:, c0:c0 + 128], identN[:, :], cdiag[:, 0, :], False, True, [identN, cdiag], [L])
                pt = nPT()
                kb.act(pt[:, c0:512], L[:, c0:512], AF.Exp, [L], [pt], scale=0.125)
                for qt in range(max(i, 0), 4):
                    bnk, sl = qt // nbq, (qt % nbq) * w
                    kb.mm(acc[bnk][:, sl:sl + w], pt[:, qt * 128:(qt + 1) * 128], vfn(kt),
                          kt == 0 and qt % nbq == 0, kt == 4 * Q + qt, [pt, vT_tn[0]], [acc[bnk]])
            return evac(acc, w, nbq)

        kT_tn = [None]
        vT_tn = [None]
        for hh in range(2):
            for m in range(2):
                mi = hh * 2 + m
                kT_tn[0] = kb0 if mi < 2 else kb1
                vT_tn[0] = Vd
                r0 = 64 * (mi % 2)
                qap = qs[r0:r0 + 64, 4 + mi // 2, :]
                ev, r_, cf = causal_unit(qap, kT_tn[0][r0:r0 + 64, :], lambda kt, hh=hh: Vd[:, kt, hh * 129:hh * 129 + 129], 129, 2, False, qs)
                if m == 0:
                    for qt in range(4):
                        kb.ts("dve", od0[:, qt, :], ev[:, qt, 0:128], r_[:, qt:qt + 1], None, ALU.mult, None, [ev, r_], [od0])
                else:
                    for qt in range(4):
                        kb.ts("dve", od1[:, qt, :], ev[:, qt, 0:128], r_[:, qt:qt + 1], neglam, ALU.mult, ALU.mult, [ev, r_, lsc], [od1])
                        kb.tt("dve", od0[:, qt, :], od0[:, qt, :], od1[:, qt, :], ALU.add, [od0, od1], [od0])
                        kb.stt(djunk[:, :], od0[:, qt, :], 1.0, od0[:, qt, :], ALU.mult, ALU.mult, [od0], [djunk, dss],
                               accum_out=dss[:, qt:qt + 1])
                    kb.act(dss[:, :], dss[:, :], AF.Sqrt, [dss, eps128], [dss], bias=eps128[:, 0:1], scale=1.0 / 128)
                    kb.recip(dss[:, :], dss[:, :], [dss], [dss])
                    for qt in range(4):
                        kb.stt(obt[:, qt, hh * 128:(hh + 1) * 128], od0[:, qt, :], dss[:, qt:qt + 1], sublnB[:, :], ALU.mult, ALU.mult,
                               [od0, dss, sublnB], [(obt, hh)])
        for h in range(4):
            r0 = 64 * (h % 2)
            kT_tn[0] = kslc
            vT_tn[0] = Vnsa
            qap = qs[r0:r0 + 64, 2 + h // 2, :]
            ev, r_, cf = causal_unit(qap, kslc[r0:r0 + 64, :], lambda kt: Vnsa[:, kt, 0:65], 65, 4, True, qs)
            for qt in range(4):
                qta = 4 * Q + qt
                kb.tt("dve", cf[:, qt:qt + 1], r_[:, qt:qt + 1], gates[:, qta, h * 3 + 1:h * 3 + 2], ALU.mult, [r_, gates], [cf])
                kb.stt(oacc[:, qt, h * 64:(h + 1) * 64], ev[:, qt, 0:64], cf[:, qt:qt + 1], oacc[:, qt, h * 64:(h + 1) * 64],
                       ALU.mult, ALU.add, [ev, cf, (oacc, h)], [(oacc, h)])
        kb.cp("act", o16[:, :, 0:256], oacc[:, :, :], [oacc], [o16])
        kb.cp("act", o16[:, :, 256:512], obt[:, :, :], [obt], [o16])
        for c4 in range(4):
            for qt in range(4):
                kb.tr(PSB[:, 512 + qt * 128:512 + (qt + 1) * 128], o16[:, qt, c4 * 128:(c4 + 1) * 128], ident[:, :], [o16, ident], [(PSB, "o")])
            kb.cp("dve", oTs[:, c4, :], PSB[:, 512:1024], [(PSB, "o")], [(oTs, c4)])
            kb.dma("q_sp", oTv[:, c4, q0:q0 + QG], oTs[:, c4, :], [(oTs, c4)], [oT])
        if stop is not None and stop.startswith('A') and int(stop[1:]) == Q:
            return _fin(kb, own)
    return _fin(kb, own)


def _swap64(cols):
    cols = np.asarray(cols).reshape(-1, 64)
    return np.concatenate([cols[:, 32:], cols[:, :32]], axis=1).reshape(-1)


def ab_consts(S):
    NT = S // 128
    NCP = S // 16
    ncmp = NCP - 1
    NCT = (NCP + 127) // 128
    bf = ml_dtypes.bfloat16
    c = {}
    c["identb"] = np.eye(128, dtype=np.float32).astype(bf)
    c["identN"] = (np.eye(128, dtype=np.float32) * NEGM).astype(bf)
    n_ = np.arange(128)[:, None, None]
    D_ = np.arange(5)[None, :, None]
    q_ = np.arange(512)[None, None, :]
    c["cmpbias"] = np.where(16 * n_ + 31 - q_ <= 512 * D_, 0.0, -1.0).astype(np.float32).astype(bf)
    k_ = np.arange(128)[:, None]
    qq = np.arange(128)[None, :]
    cd = np.stack([np.where(k_ <= qq, 0.0, -1.0), np.where(k_ > qq, 0.0, -1.0)], axis=1)
    c["cdiag"] = cd.astype(np.float32).astype(bf)
    j_ = np.arange(128)[:, None, None]
    kt_ = np.arange(NT)[None, :, None]
    kk = np.arange(128)[None, None, :]
    c["EE"] = np.where(j_ == 2 * kt_ + kk // 64, NEGM, 0.0).astype(np.float32).astype(bf)
    qp = np.arange(128)[:, None]
    rel = np.arange(254)[None, :] - 126
    cur = qp // 64
    elig = (rel <= cur).astype(np.float32)
    addw = np.where((rel == cur) | (rel == cur - 1), BIGS, np.where(rel > cur, -BIGS, 0.0)).astype(np.float32)
    c["eaW"] = np.ascontiguousarray(np.stack([elig, addw], axis=1))
    n = np.arange(NCT * 128)[:, None]
    j = np.arange(128)[None, :]
    ov = ((16 * n < 64 * j + 64) & (16 * n + 32 > 64 * j) & (n < ncmp)).astype(np.float32)
    c["ovl"] = np.ascontiguousarray(ov.reshape(NCT, 128, 128).transpose(1, 0, 2)).astype(bf)
    return c


def prep_ab(l, inp, b, g, S):
    w_in = inp["w_in"][l]
    d = {}
    d["posB"] = np.ascontiguousarray(np.broadcast_to(inp["positions"][b][None, :S], (128, S))).astype(np.int32)
    d["gA"] = _col8(inp["attn_norm"][l])
    r64 = np.arange(64)
    r128 = np.arange(128)
    plain = [256 * g + r128, 256 * g + 128 + r128,
             np.concatenate([512 + 64 * g + r64, 512 + 128 + 64 * g + r64]),
             np.concatenate([512 + 256 + 64 * g + r64] * 2),
             np.concatenate([512 + 512 + 64 * g + r64] * 2),
             1304 + 256 * g + r128, 1304 + 256 * g + 128 + r128,
             1816 + 256 * g + r128, 1816 + 256 * g + 128 + r128]
    swaps = [_swap64(plain[i]) for i in (0, 1, 3, 4, 5, 6, 7, 8)]
    cols = np.concatenate(plain + swaps)
    d["w_fm"] = np.ascontiguousarray(w_in[:, cols])
    tcols = np.concatenate([512 + 384 + 64 * g + r64, 512 + 640 + 64 * g + r64, 1280 + 12 * g + np.arange(12),
                            2328 + 256 * g + np.arange(256)])
    d["w_tm"] = np.ascontiguousarray(w_in[:, tcols])
    p = np.arange(128)
    lam_init = 0.8 - 0.6 * math.exp(-0.3 * l)
    colc = np.zeros((128, 4), np.float32)
    colc[:, 0] = (10000.0 ** (-(np.arange(32, dtype=np.float32)) / 32.0)).astype(np.float32)[p % 32]
    colc[:, 1] = np.where((p % 64) < 32, -1.0, 1.0)
    colc[:, 2] = lam_init
    colc[:, 3] = 1.0 - lam_init
    d["colc"] = colc
    w1k = inp["cmp_k_w1"][l].reshape(32, 64, 256).transpose(1, 0, 2)
    w1v = inp["cmp_v_w1"][l].reshape(32, 64, 256).transpose(1, 0, 2)
    d["w1kv"] = np.ascontiguousarray(np.concatenate([w1k, w1v], axis=0))
    d["posT"] = np.ascontiguousarray(np.concatenate([inp["cmp_pos_k"][l].T, inp["cmp_pos_v"][l].T], axis=0))
    w2k = inp["cmp_k_w2"][l].reshape(2, 128, 64).transpose(1, 0, 2)
    d["w2k"] = np.ascontiguousarray(np.concatenate([w2k, w2k], axis=2))
    d["w2v"] = np.ascontiguousarray(inp["cmp_v_w2"][l].reshape(2, 128, 64).transpose(1, 0, 2))
    lq = np.stack([inp["diff_lq1"][l], inp["diff_lk1"][l], inp["diff_lq2"][l], inp["diff_lk2"][l]], axis=0)
    d["lqk"] = np.ascontiguousarray(np.broadcast_to(lq[None], (128, 4, 64)))
    d["sublnB"] = np.ascontiguousarray(np.broadcast_to(inp["diff_subln"][l][None, :], (128, 128)))
    return d


_CACHE = {}


def _prog(name, fn, *args):
    key = (name,) + args
    if key not in _CACHE:
        _CACHE[key] = fn(*args)
    return _CACHE[key]


def kernel_unfused(**inputs):
    inp = {k: np.asarray(v) for k, v in inputs.items()}
    x = inp["x"].astype(np.float32, copy=False)
    B, S, _ = x.shape
    T = S // 2
    L = inp["w_in"].shape[0]
    cores = list(range(8))
    cst = ab_consts(S)
    xT = [np.ascontiguousarray(x[b].T) for b in range(B)]
    out = None
    for l in range(L):
        nc_ab = build_ab(S)
        maps = []
        for c in cores:
            b, g = c // 2, c % 2
            d = prep_ab(l, inp, b, g, S)
            d.update(cst)
            d["xT"] = xT[b]
            maps.append(d)
        res = run_bass_kernel_spmd(nc_ab, maps, core_ids=cores).results
        oaT = [np.concatenate([res[2 * b]["oT"][0:256], res[2 * b + 1]["oT"][0:256]], axis=0) for b in range(B)]
        obT = [np.concatenate([res[2 * b]["oT"][256:512], res[2 * b + 1]["oT"][256:512]], axis=0) for b in range(B)]
        del res, maps
        nc_c1 = build_c1(T)
        maps = []
        p1 = prep_c1(l, inp)
        for c in cores:
            b, g = c // 2, c % 2
            d = dict(p1)
            d["xT"] = np.ascontiguousarray(xT[b][:, g * T:(g + 1) * T])
            d["oaT"] = np.ascontiguousarray(oaT[b][:, g * T:(g + 1) * T])
            d["obT"] = np.ascontiguousarray(obT[b][:, g * T:(g + 1) * T])
            maps.append(d)
        res = run_bass_kernel_spmd(nc_c1, maps, core_ids=cores).results
        x1T = [res[c]["x1T"] for c in cores]
        del res, maps
        nc_c2 = build_c2(T)
        p2 = prep_c2(l, inp)
        maps = []
        for c in cores:
            d = dict(p2)
            d["x1T"] = x1T[c]
            maps.append(d)
        res = run_bass_kernel_spmd(nc_c2, maps, core_ids=cores).results
        xT = [np.concatenate([res[2 * b]["x2T"], res[2 * b + 1]["x2T"]], axis=1) for b in range(B)]
        if l == L - 1:
            out = np.stack([np.concatenate([res[2 * b]["x2nT"], res[2 * b + 1]["x2nT"]], axis=1).T for b in range(B)], axis=0)
        del res, maps
    return np.ascontiguousarray(out.astype(np.float32))


def build_fused(S, L=2):
    kb = KB()
    xin = kb.din("xT_in", [D, S])
    outT = kb.dout("outT", [D, S])
    xcur = xin
    for l in range(L):
        kb.sfx = "_l%d" % l
        oscr = kb.dscr("oscr", [2, 512, S], BF16)
        for g in range(2):
            ov = Tn(oscr.t[g], "oscr_g")
            ov.b = oscr.b
            build_ab(S, kb=kb, io={"xT": xcur, "oT": ov}, sfx="_l%dg%d" % (l, g))
        kb.sfx = "_l%d" % l
        x1 = kb.dscr("x1scr", [D, S], F32)

        def osrc_fn(which, c, t0, G, oscr=oscr):
            r0 = 256 * which + (c % 2) * 128
            return oscr.t[c // 2, r0:r0 + 128, t0:t0 + G]

        build_c1(S, kb=kb, io={"xT": xcur, "x1T": x1, "osrc_fn": osrc_fn, "osrc_tn": oscr}, sfx="_l%d" % l)
        if l < L - 1:
            kb.sfx = "_l%d" % l
            x2 = kb.dscr("x2scr", [D, S], F32)
            build_c2(S, kb=kb, io={"x1T": x1, "x2T": x2, "want_x2": True, "want_x2n": False}, sfx="_l%d" % l)
            xcur = x2
        else:
            build_c2(S, kb=kb, io={"x1T": x1, "x2nT": outT, "want_x2": False, "want_x2n": True}, sfx="_l%d" % l)
    return kb.finish()


def fused_inputs(inp, b, S, cst):
    L = inp["w_in"].shape[0]
    d = dict(cst)
    d["cm01"] = np.triu(np.ones((128, 128), np.float32))
    d["xT_in"] = np.ascontiguousarray(inp["x"][b].T)
    for l in range(L):
        for g in range(2):
            for k, v in prep_ab(l, inp, b, g, S).items():
                if k == "posB":
                    d["posB"] = v
                else:
                    d["%s_l%dg%d" % (k, l, g)] = v
        for k, v in prep_c1(l, inp).items():
            if k != "cm01":
                d["%s_l%d" % (k, l)] = v
        for k, v in prep_c2(l, inp).items():
            d["%s_l%d" % (k, l)] = v
    return d


def kernel(**inputs):
    inp = {k: np.asarray(v) for k, v in inputs.items()}
    B, S, _ = inp["x"].shape
    nc = build_fused(S)
    cst = ab_consts(S)
    per_b = [fused_inputs(inp, b, S, cst) for b in range(B)]
    maps = [per_b[c % B] for c in range(8)]
    res = run_bass_kernel_spmd(nc, maps, core_ids=list(range(8))).results
    out = np.stack([res[b]["outT"].T for b in range(B)], axis=0)
    return np.ascontiguousarray(out.astype(np.float32))
```

```python
import contextlib
import math
import numpy as np
import ml_dtypes
import concourse.bass as bass
import concourse.mybir as mybir
from concourse.bass_utils import run_bass_kernel_spmd

F32 = mybir.dt.float32
BF16 = mybir.dt.bfloat16
I32 = mybir.dt.int32
AF = mybir.ActivationFunctionType
ALU = mybir.AluOpType
AX = mybir.AxisListType

D = 1024
DFF = 2816
EPS = 1e-6
NEGM = 32768.0

COMPUTE = ("pe", "act", "dve", "pool")
DMAQ = {"q_sp": "sp", "q_pool": "pool", "q_act": "act", "q_cc": "pool"}
QINC = {"q_sp": 16, "q_pool": 16, "q_act": 16, "q_cc": 1}
NSEM_PER_Q = 10


class Buf:
    __slots__ = ("name", "w", "r", "excl")

    def __init__(self, name):
        self.name = name
        self.w = {}
        self.r = {}
        self.excl = False


class Tn:
    def __init__(self, t, name):
        self.t = t
        self.b = Buf(name)

    def __getitem__(self, idx):
        return self.t[idx]


def _norm(lst):
    out = []
    for x in lst:
        if isinstance(x, tuple):
            b, k = x
        else:
            b, k = x, None
        if isinstance(b, Tn):
            b = b.b
        out.append((b, k))
    return out


class Prog:
    def __init__(self, nc):
        self.nc = nc
        self.ops = []

    def add(self, stream, fn, reads=(), writes=()):
        i = len(self.ops)
        deps = {}

        def ck(d, k):
            if k is None:
                return list(d.keys())
            return [kk for kk in (k, None) if kk in d]

        reads = _norm(reads)
        writes = _norm(writes)
        writes = writes + [(b, k) for (b, k) in reads if b.excl and (b, k) not in writes]
        for (b, k) in reads:
            for kk in ck(b.w, k):
                deps[b.w[kk]] = True
        for (b, k) in writes:
            for kk in ck(b.w, k):
                deps.setdefault(b.w[kk], False)
            for kk in ck(b.r, k):
                for s, j in b.r[kk].items():
                    if isinstance(j, list):
                        for jj in j:
                            deps.setdefault(jj, False)
                    else:
                        deps.setdefault(j, False)
        for (b, k) in reads:
            d = b.r.setdefault(k, {})
            if stream in DMAQ:
                d.setdefault(stream, [])
                d[stream].append(i)
            else:
                d[stream] = i
        for (b, k) in writes:
            if k is None:
                b.w = {None: i}
                b.r = {}
            else:
                b.w[k] = i
                b.r.pop(k, None)
        deps.pop(i, None)
        self.ops.append(dict(stream=stream, fn=fn, deps=deps, sig=None))
        return i

    def emit(self, es):
        nc = self.nc
        ops = self.ops
        need = [[] for _ in ops]
        for c, o in enumerate(ops):
            cs = o["stream"]
            best = {}
            for p, raw in o["deps"].items():
                ps = ops[p]["stream"]
                if ps in DMAQ:
                    need[c].append(p)
                    continue
                if ps == cs:
                    if cs == "pe" or not raw:
                        continue
                if ps not in best or best[ps] < p:
                    best[ps] = p
            need[c].extend(best.values())
        signal = [False] * len(ops)
        for c in range(len(ops)):
            for p in need[c]:
                signal[p] = True
        sems = {}
        for s in COMPUTE:
            sems[s] = es.enter_context(nc.semaphore("s_" + s))
        qsems = {}
        for q in DMAQ:
            qsems[q] = [es.enter_context(nc.semaphore("s_%s_%d" % (q, j))) for j in range(NSEM_PER_Q)]
        cnt = {s: 0 for s in COMPUTE}
        qcnt = {q: 0 for q in DMAQ}
        qsemcnt = {q: [0] * NSEM_PER_Q for q in DMAQ}
        for i, o in enumerate(ops):
            s = o["stream"]
            if s in DMAQ:
                j = qcnt[s] % NSEM_PER_Q
                qcnt[s] += 1
                qsemcnt[s][j] += 1
                o["sig"] = (qsems[s][j], QINC[s] * qsemcnt[s][j], (s, j))
                o["prev"] = (qsems[s][j], QINC[s] * (qsemcnt[s][j] - 1), (s, j))
            elif signal[i]:
                cnt[s] += 1
                o["sig"] = (sems[s], cnt[s], s)
        per_eng = {e: [] for e in ("pe", "act", "dve", "pool", "sp")}
        for i, o in enumerate(ops):
            s = o["stream"]
            per_eng[DMAQ.get(s, s)].append(i)
        waited = {e: {} for e in per_eng}

        def run_engine(ename, eng):
            wd = waited[ename]
            for i in per_eng[ename]:
                o = ops[i]
                ws = []
                for p in need[i]:
                    ws.append(ops[p]["sig"])
                if o["stream"] in DMAQ:
                    if o["prev"][1] > 0:
                        ws.append(o["prev"])
                for sem, val, key in ws:
                    if wd.get(key, 0) >= val:
                        continue
                    wd[key] = val
                    eng.wait_ge(sem, val)
                ins = o["fn"](eng)
                if o["sig"] is not None:
                    sem, val, key = o["sig"]
                    ins.then_inc(sem, QINC.get(o["stream"], 1))
            for q, e in DMAQ.items():
                if e != ename:
                    continue
                for j in range(NSEM_PER_Q):
                    v = QINC[q] * qsemcnt[q][j]
                    if v > 0 and wd.get((q, j), 0) < v:
                        eng.wait_ge(qsems[q][j], v)

        with nc.Block() as block:
            @block.tensor
            def _(e):
                run_engine("pe", e)

            @block.scalar
            def _(e):
                run_engine("act", e)

            @block.vector
            def _(e):
                run_engine("dve", e)

            @block.gpsimd
            def _(e):
                run_engine("pool", e)

            @block.sync
            def _(e):
                run_engine("sp", e)


class KB:
    def __init__(self):
        self.nc = bass.Bass("TRN2", target_bir_lowering=False)
        self.es = contextlib.ExitStack()
        self.pr = Prog(self.nc)
        self.off = 16896
        self.maxoff = 0
        self.sfx = ""
        self.io = {}
        self.uid = 0
        self.alloc_log = []
        self.prev_list = []
        self.dram = {}
        self.psums = {}
        self.dummy = self.sb("dummy", [128, 16], F32)
        self.base = self.off

    SHARED = ("posB", "identb", "identN", "identf", "cmpbias", "cdiag", "EE", "eaW", "ovl", "cm01")

    def begin_phase(self, sfx):
        self.sfx = sfx
        self.off = self.base
        self.alloc_log = []

    def phase_barrier(self):
        if self.prev_list:
            self.barrier(self.prev_list, list(self.alloc_log))

    def end_phase(self):
        self.prev_list = list(self.alloc_log)

    def sb(self, name, shape, dt):
        nb = 4 if dt in (F32, I32) else 2
        size = nb
        for s_ in shape[1:]:
            size *= s_
        size = (size + 63) // 64 * 64
        off = self.off
        self.off += size
        self.maxoff = max(self.maxoff, self.off)
        assert self.off <= 229376 - 256, ("SBUF overflow", name, self.off)
        self.uid += 1
        name = "%s%s_%d" % (name, self.sfx, self.uid)
        t = Tn(self.nc.alloc_sbuf_tensor_at(name, list(shape), dt, offset=off), name)
        self.alloc_log.append(t)
        return t

    def barrier(self, old, new):
        d = self.dummy
        self.pr.add("dve", lambda e: e.memset(d[:, :], 0.0), list(old), list(new) + list(old))

    def psum(self, name, shape=(128, 512), dt=F32):
        if name in self.psums:
            return self.psums[name]
        t = Tn(self.es.enter_context(self.nc.psum_tensor(name, list(shape), dt)), name)
        t.b.excl = True
        self.psums[name] = t
        return t

    def din(self, name, shape, dt=F32):
        if name in self.io:
            return self.io[name]
        if name not in self.SHARED:
            name = name + self.sfx
        if name in self.dram:
            return self.dram[name]
        t = self.nc.dram_tensor(name, list(shape), dt, kind="ExternalInput")
        self.dram[name] = Tn(t.ap(), name)
        return self.dram[name]

    def dout(self, name, shape, dt=F32):
        if name in self.io:
            return self.io[name]
        t = self.nc.dram_tensor(name + self.sfx, list(shape), dt, kind="ExternalOutput")
        return Tn(t.ap(), name)

    def dscr(self, name, shape, dt):
        t = self.nc.dram_tensor(name + self.sfx, list(shape), dt, kind="Internal")
        return Tn(t.ap(), name)

    def dma(self, q, out, in_, r, w):
        self.pr.add(q, lambda e: e.dma_start(out=out, in_=in_), r, w)

    def mm(self, out, lhsT, rhs, start, stop, r, w):
        self.pr.add("pe", lambda e: e.matmul(out, lhsT, rhs, start=start, stop=stop, skip_group_check=True), r, w)

    def tr(self, out, in_, ident, r, w):
        self.pr.add("pe", lambda e: e.transpose(out, in_, ident), r, w)

    def act(self, out, in_, func, r, w, bias=None, scale=None):
        kw = {}
        if bias is not None:
            kw["bias"] = bias
        if scale is not None:
            kw["scale"] = scale
        self.pr.add("act", lambda e: e.activation(out=out, in_=in_, func=func, **kw), r, w)

    def ts(self, eng, out, in0, s1, s2, op0, op1, r, w, accum_out=None):
        if op1 is None:
            self.pr.add(eng, lambda e: e.tensor_scalar(out=out, in0=in0, scalar1=s1, scalar2=None, op0=op0), r, w)
        elif accum_out is not None:
            self.pr.add(eng, lambda e: e.tensor_scalar(out=out, in0=in0, scalar1=s1, scalar2=s2, op0=op0, op1=op1,
                                                      accum_out=accum_out), r, w)
        else:
            self.pr.add(eng, lambda e: e.tensor_scalar(out=out, in0=in0, scalar1=s1, scalar2=s2, op0=op0, op1=op1), r, w)

    def tt(self, eng, out, in0, in1, op, r, w):
        self.pr.add(eng, lambda e: e.tensor_tensor(out=out, in0=in0, in1=in1, op=op), r, w)

    def stt(self, out, in0, scalar, in1, op0, op1, r, w, accum_out=None):
        if accum_out is None:
            self.pr.add("dve", lambda e: e.scalar_tensor_tensor(out=out, in0=in0, scalar=scalar, in1=in1, op0=op0, op1=op1), r, w)
        else:
            self.pr.add("dve", lambda e: e.scalar_tensor_tensor(out=out, in0=in0, scalar=scalar, in1=in1, op0=op0, op1=op1,
                                                                accum_out=accum_out), r, w)

    def cp(self, eng, out, in_, r, w):
        if eng == "act":
            self.pr.add("act", lambda e: e.copy(out=out, in_=in_), r, w)
        else:
            self.pr.add(eng, lambda e: e.tensor_copy(out=out, in_=in_), r, w)

    def recip(self, out, in_, r, w):
        self.pr.add("dve", lambda e: e.reciprocal(out=out, in_=in_), r, w)

    def memset(self, eng, ap, val, w):
        self.pr.add(eng, lambda e: e.memset(ap, val), (), w)

    def finish(self):
        self.pr.emit(self.es)
        self.es.close()
        return self.nc


def _fin(kb, own):
    kb.end_phase()
    kb.io = {}
    if own:
        return kb.finish()
    return None


def load_w(kb, q, dst, src_ap, kchunks, r=(), extra_w=()):
    v = src_ap.t.rearrange("(c p) n -> p c n", p=128)
    for c in range(kchunks):
        kb.dma(q, dst[:, c, :], v[:, c, :], [src_ap] + list(r), [(dst, c)] + list(extra_w))


def rmsnorm_fm(kb, xg, hT, gcol, ones, epsT, ps_ss, tmpA, rstdB, G=512):
    kb.tt("pool", hT[:, :, :], xg[:, :, :], xg[:, :, :], ALU.mult, [xg], [hT])
    for c in range(8):
        kb.mm(ps_ss[:, :G], ones[:, :], hT[:, c, :], c == 0, c == 7, [ones, hT], [ps_ss])
    kb.act(tmpA[:, :G], ps_ss[:, :G], AF.Sqrt, [ps_ss, epsT], [tmpA], bias=epsT[:, 0:1], scale=1.0 / D)
    kb.recip(rstdB[:, :G], tmpA[:, :G], [tmpA], [rstdB])
    for c in range(8):
        kb.stt(hT[:, c, :], xg[:, c, :], gcol[:, c:c + 1], rstdB[:, :G], ALU.mult, ALU.mult, [xg, gcol, rstdB], [(hT, c)])


def build_c1(T, kb=None, io=None, sfx=""):
    own = kb is None
    if own:
        kb = KB()
    kb.io = io or {}
    kb.begin_phase(sfx)
    G = 512
    NG = T // G
    xT = kb.din("xT", [D, T])
    osrc_fn = kb.io.get("osrc_fn")
    osrc_tn = kb.io.get("osrc_tn")
    o_gath = kb.io.get("o_gath")
    msel_d = kb.din("msel", [128, 2]) if o_gath is not None else None
    if osrc_fn is None and o_gath is None:
        oaT = kb.din("oaT", [512, T], BF16)
        obT = kb.din("obT", [512, T], BF16)
        oaTv = oaT.t.rearrange("(c p) t -> p c t", p=128)
        obTv = obT.t.rearrange("(c p) t -> p c t", p=128)
    gA_d = kb.din("gA", [128, 8])
    wsgu_d = kb.din("w_sgu", [D, 1024])
    gB_d = kb.din("sgu_gB", [128, 512])
    swT_d = kb.din("sgu_wT", [128, 4, 128])
    sbB_d = kb.din("sgu_bB", [128, 4, 128])
    cm_d = kb.din("cm01", [128, 128])
    wbr_d = kb.din("w_br", [1536, D])
    wm_d = kb.din("w_merge", [D, 3072])
    bm_d = kb.din("bm", [128, 24])
    wo_d = kb.din("w_out", [D, D])
    x1T = kb.dout("x1T", [D, T])

    Wsgu = kb.sb("Wsgu", [128, 8, 1024], BF16)
    Wbr = kb.sb("Wbr", [128, 12, 1024], BF16)
    Wm = kb.sb("Wm", [128, 8, 3072], BF16)
    Wo = kb.sb("Wo", [128, 8, 1024], BF16)
    gA = kb.sb("gA_s", [128, 8], F32)
    gB = kb.sb("gB_s", [128, 512], F32)
    swT = kb.sb("swT_s", [128, 4, 128], F32)
    swTb = kb.sb("swTb", [128, 4, 128], BF16)
    sbB = kb.sb("sbB_s", [128, 4, 128], F32)
    cm = kb.sb("cm_s", [128, 128], F32)
    bm = kb.sb("bm_s", [128, 24], F32)
    ones = kb.sb("ones", [128, 128], BF16)
    epsT = kb.sb("epsT", [128, 1], F32)
    eps512 = kb.sb("eps512", [128, 1], F32)

    xg = kb.sb("xg", [128, 8, G], F32)
    hT = kb.sb("hT", [128, 8, G], BF16)
    tmpA = kb.sb("tmpA", [128, G], F32)
    rstdB = kb.sb("rstdB", [128, G], F32)
    uT = kb.sb("uT", [128, 4, G], BF16)
    vg = kb.sb("vg", [128, 512], F32)
    vjunk = kb.sb("vjunk", [128, 512], F32)
    vn = kb.sb("vn", [128, 512], BF16)
    ssv = kb.sb("ssv", [128, 4], F32)
    ocT = kb.sb("ocT", [128, 4, G], BF16)
    stmp = kb.sb("stmp", [128, G], F32)
    oa = kb.sb("oa", [128, 4, G], BF16)
    ob = kb.sb("ob", [128, 4, G], BF16)
    gate = [kb.sb("gate%d" % i, [128, G], F32) for i in range(2)]
    acc = kb.sb("acc", [128, G], F32)
    mixedT = kb.sb("mixedT", [128, 8, G], BF16)
    ost = [kb.sb("ost%d" % i, [128, 2, G], BF16) for i in range(2)]
    msel = kb.sb("msel_s", [128, 2], F32)
    sti = [0]
    PS = [kb.psum("ps%d" % i) for i in range(7)]
    kb.phase_barrier()
    if msel_d is not None:
        kb.dma("q_sp", msel.t[:], msel_d.t, [msel_d], [msel])

    kb.memset("dve", ones[:, :], 1.0, [ones])
    kb.memset("dve", epsT[:, :], EPS, [epsT])
    kb.memset("dve", eps512[:, :], EPS, [eps512])
    for (dst, src) in ((gA, gA_d), (gB, gB_d), (sbB, sbB_d), (cm, cm_d), (bm, bm_d), (swT, swT_d)):
        kb.dma("q_sp", dst.t[:], src.t, [src], [dst])
    for g4 in range(4):
        kb.tt("dve", swTb[:, g4, :], swT[:, g4, :], cm[:, :], ALU.mult, [swT, cm], [(swTb, g4)])
    load_w(kb, "q_pool", Wsgu, wsgu_d, 8)
    load_w(kb, "q_pool", Wm, wm_d, 8)
    load_w(kb, "q_pool", Wbr, wbr_d, 12)
    load_w(kb, "q_pool", Wo, wo_d, 8)

    xTv = xT.t.rearrange("(c p) t -> p c t", p=128)
    x1Tv = x1T.t.rearrange("(c p) t -> p c t", p=128)
    pi = [0]

    def nps():
        p = PS[pi[0] % 3]
        pi[0] += 1
        return p

    for gi in range(NG):
        t0 = gi * G
        for c in range(8):
            kb.dma("q_sp", xg[:, c, :], xTv[:, c, t0:t0 + G], [xT], [(xg, c)])
        for c in range(4):
            if osrc_fn is None and o_gath is None:
                kb.dma("q_sp", oa[:, c, :], oaTv[:, c, t0:t0 + G], [oaT], [(oa, c)])
                kb.dma("q_sp", ob[:, c, :], obTv[:, c, t0:t0 + G], [obT], [(ob, c)])
            elif o_gath is None:
                kb.dma("q_sp", oa[:, c, :], osrc_fn(0, c, t0, G), [osrc_tn], [(oa, c)])
                kb.dma("q_sp", ob[:, c, :], osrc_fn(1, c, t0, G), [osrc_tn], [(ob, c)])
            else:
                for which, dst in ((0, oa), (1, ob)):
                    c4 = 2 * which + c % 2
                    r = c // 2
                    st = ost[sti[0] % 2]
                    sti[0] += 1
                    for half in range(2):
                        kb.dma("q_sp", st[:, half, :], o_gath.t[c4, half, r * 128:(r + 1) * 128, t0:t0 + G], [o_gath], [(st, half)])
                    kb.ts("dve", dst[:, c, :], st[:, 0, :], msel[:, 0:1], None, ALU.mult, None, [st, msel], [(dst, c)])
                    kb.stt(dst[:, c, :], st[:, 1, :], msel[:, 1:2], dst[:, c, :], ALU.mult, ALU.add, [st, msel, (dst, c)], [(dst, c)])
        rmsnorm_fm(kb, xg, hT, gA, ones, epsT, nps(), tmpA, rstdB, G)
        for uc in range(4):
            p = nps()
            for k in range(8):
                kb.mm(p[:, :G], Wsgu[:, k, uc * 128:(uc + 1) * 128], hT[:, k, :], k == 0, k == 7, [(Wsgu, k), hT], [p])
            kb.act(uT[:, uc, :], p[:, :G], AF.Gelu_apprx_tanh, [p], [(uT, uc)])
        ps_s = PS[3:7]
        for tt in range(4):
            p = nps()
            for k in range(8):
                kb.mm(p[:, :512], hT[:, k, tt * 128:(tt + 1) * 128], Wsgu[:, k, 512:1024], k == 0, k == 7, [hT, (Wsgu, k)], [p])
            kb.act(vg[:, :], p[:, :512], AF.Gelu_apprx_tanh, [p], [vg])
            kb.stt(vjunk[:, :], vg[:, :], 1.0, vg[:, :], ALU.mult, ALU.mult, [vg], [vjunk, (ssv, tt)], accum_out=ssv[:, tt:tt + 1])
            kb.act(ssv[:, tt:tt + 1], ssv[:, tt:tt + 1], AF.Sqrt, [(ssv, tt), eps512], [(ssv, tt)], bias=eps512[:, 0:1], scale=1.0 / 512)
            kb.recip(ssv[:, tt:tt + 1], ssv[:, tt:tt + 1], [(ssv, tt)], [(ssv, tt)])
            kb.stt(vn[:, :], vg[:, :], ssv[:, tt:tt + 1], gB[:, :], ALU.mult, ALU.mult, [vg, (ssv, tt), gB], [vn])
            for g4 in range(4):
                kb.mm(ps_s[g4][:, tt * 128:(tt + 1) * 128], vn[:, g4 * 128:(g4 + 1) * 128], swTb[:, g4, :], True, True,
                      [vn, swTb], [(ps_s[g4], tt)])
        for g4 in range(4):
            for tt in range(4):
                kb.tt("dve", stmp[:, tt * 128:(tt + 1) * 128], ps_s[g4][:, tt * 128:(tt + 1) * 128], sbB[:, g4, :], ALU.add,
                      [ps_s[g4], sbB], [(stmp, tt)])
            kb.tt("dve", ocT[:, g4, :], stmp[:, :], uT[:, g4, :], ALU.mult, [stmp, (uT, g4)], [(ocT, g4)])
        osrc = (oa, ob, ocT)
        for oc in range(8):
            for br in range(3):
                pg = nps()
                for k in range(8):
                    kb.mm(pg[:, :G], Wm[:, k, br * 1024 + oc * 128: br * 1024 + (oc + 1) * 128], hT[:, k, :], k == 0, k == 7,
                          [(Wm, k), hT], [pg])
                gt = gate[br % 2]
                kb.act(gt[:, :], pg[:, :G], AF.Sigmoid, [pg, bm], [gt], bias=bm[:, br * 8 + oc: br * 8 + oc + 1])
                pb = nps()
                for k in range(4):
                    kb.mm(pb[:, :G], Wbr[:, br * 4 + k, oc * 128:(oc + 1) * 128], osrc[br][:, k, :], k == 0, k == 3,
                          [(Wbr, br * 4 + k), (osrc[br], k)], [pb])
                if br == 0:
                    kb.tt("dve", acc[:, :], gt[:, :], pb[:, :G], ALU.mult, [gt, pb], [acc])
                else:
                    kb.tt("dve", gt[:, :], gt[:, :], pb[:, :G], ALU.mult, [gt, pb], [gt])
                    if br == 1:
                        kb.tt("dve", acc[:, :], acc[:, :], gt[:, :], ALU.add, [acc, gt], [acc])
                    else:
                        kb.tt("dve", mixedT[:, oc, :], acc[:, :], gt[:, :], ALU.add, [acc, gt], [(mixedT, oc)])
        for oc in range(8):
            p = nps()
            for k in range(8):
                kb.mm(p[:, :G], Wo[:, k, oc * 128:(oc + 1) * 128], mixedT[:, k, :], k == 0, k == 7, [(Wo, k), (mixedT, k)], [p])
            kb.tt("dve", xg[:, oc, :], xg[:, oc, :], p[:, :G], ALU.add, [(xg, oc), p], [(xg, oc)])
            kb.dma("q_sp", x1Tv[:, oc, t0:t0 + G], xg[:, oc, :], [(xg, oc)], [x1T])
    return _fin(kb, own)


def build_c2(T, kb=None, io=None, sfx=""):
    own = kb is None
    if own:
        kb = KB()
    kb.io = io or {}
    kb.begin_phase(sfx)
    G = 512
    NG = T // G
    NF = DFF // 128
    x1T = kb.din("x1T", [D, T])
    gF_d = kb.din("gF", [128, 8])
    gZ_d = kb.din("gZ", [128, 8])
    w1_d = kb.din("w1", [D, DFF])
    w3_d = kb.din("w3", [D, DFF])
    w2_d = kb.din("w2", [DFF, D])
    x2T = kb.dout("x2T", [D, T]) if kb.io.get("want_x2", True) else None
    x2nT = kb.dout("x2nT", [D, T]) if kb.io.get("want_x2n", True) else None

    W1 = kb.sb("W1", [128, 8, DFF], BF16)
    W3 = kb.sb("W3", [128, 8, DFF], BF16)
    W2 = kb.sb("W2", [128, NF, D], BF16)
    gF = kb.sb("gF_s", [128, 8], F32)
    gZ = kb.sb("gZ_s", [128, 8], F32)
    ones = kb.sb("ones", [128, 128], BF16)
    epsT = kb.sb("epsT", [128, 1], F32)
    xg = kb.sb("xg", [128, 8, G], F32)
    hT = kb.sb("hT", [128, 8, G], BF16)
    tmpA = kb.sb("tmpA", [128, G], F32)
    rstdB = kb.sb("rstdB", [128, G], F32)
    aT = kb.sb("aT", [128, NF, G], BF16)
    sl = [kb.sb("sl%d" % i, [128, G], F32) for i in range(2)]
    gN = kb.sb("gN_s", [128, 8], F32)
    PS = [kb.psum("ps%d" % i) for i in range(7)]
    kb.phase_barrier()
    want_x2 = kb.io.get("want_x2", True)
    want_x2n = kb.io.get("want_x2n", True)
    h_next = kb.io.get("h_next")
    if h_next is not None:
        gN_d = kb.din("gN", [128, 8])
        kb.dma("q_sp", gN.t[:], gN_d.t, [gN_d], [gN])

    kb.memset("dve", ones[:, :], 1.0, [ones])
    kb.memset("dve", epsT[:, :], EPS, [epsT])
    kb.dma("q_sp", gF.t[:], gF_d.t, [gF_d], [gF])
    kb.dma("q_sp", gZ.t[:], gZ_d.t, [gZ_d], [gZ])
    load_w(kb, "q_pool", W1, w1_d, 8)
    load_w(kb, "q_pool", W3, w3_d, 8)
    load_w(kb, "q_pool", W2, w2_d, NF)
    x1Tv = x1T.t.rearrange("(c p) t -> p c t", p=128)
    x2Tv = x2T.t.rearrange("(c p) t -> p c t", p=128) if x2T is not None else None
    x2nTv = x2nT.t.rearrange("(c p) t -> p c t", p=128) if x2nT is not None else None
    pi = [0]

    def nps():
        p = PS[pi[0] % 7]
        pi[0] += 1
        return p

    for gi in range(NG):
        t0 = gi * G
        for c in range(8):
            kb.dma("q_sp", xg[:, c, :], x1Tv[:, c, t0:t0 + G], [x1T], [(xg, c)])
        rmsnorm_fm(kb, xg, hT, gF, ones, epsT, nps(), tmpA, rstdB, G)
        for fc in range(NF):
            p1 = nps()
            for k in range(8):
                kb.mm(p1[:, :G], W1[:, k, fc * 128:(fc + 1) * 128], hT[:, k, :], k == 0, k == 7, [(W1, k), hT], [p1])
            p3 = nps()
            for k in range(8):
                kb.mm(p3[:, :G], W3[:, k, fc * 128:(fc + 1) * 128], hT[:, k, :], k == 0, k == 7, [(W3, k), hT], [p3])
            s = sl[fc % 2]
            kb.act(s[:, :], p1[:, :G], AF.Silu, [p1], [s])
            kb.tt("dve", aT[:, fc, :], s[:, :], p3[:, :G], ALU.mult, [s, p3], [(aT, fc)])
        for oc in range(8):
            p = nps()
            for k in range(NF):
                kb.mm(p[:, :G], W2[:, k, oc * 128:(oc + 1) * 128], aT[:, k, :], k == 0, k == NF - 1, [(W2, k), (aT, k)], [p])
            kb.tt("dve", xg[:, oc, :], xg[:, oc, :], p[:, :G], ALU.add, [(xg, oc), p], [(xg, oc)])
            if want_x2:
                kb.dma("q_sp", x2Tv[:, oc, t0:t0 + G], xg[:, oc, :], [(xg, oc)], [x2T])
        if h_next is not None:
            rmsnorm_fm(kb, xg, hT, gN, ones, epsT, nps(), tmpA, rstdB, G)
            for c in range(8):
                kb.dma("q_sp", h_next.t[c, :, t0:t0 + G], hT[:, c, :], [(hT, c)], [(h_next, c)])
        if want_x2n:
            rmsnorm_fm_f32(kb, xg, hT, gZ, ones, epsT, nps(), tmpA, rstdB, G)
            for oc in range(8):
                kb.dma("q_sp", x2nTv[:, oc, t0:t0 + G], xg[:, oc, :], [(xg, oc)], [x2nT])
    return _fin(kb, own)


def rmsnorm_fm_f32(kb, xg, hT, gcol, ones, epsT, ps_ss, tmpA, rstdB, G=512):
    kb.tt("pool", hT[:, :, :], xg[:, :, :], xg[:, :, :], ALU.mult, [xg], [hT])
    for c in range(8):
        kb.mm(ps_ss[:, :G], ones[:, :], hT[:, c, :], c == 0, c == 7, [ones, hT], [ps_ss])
    kb.act(tmpA[:, :G], ps_ss[:, :G], AF.Sqrt, [ps_ss, epsT], [tmpA], bias=epsT[:, 0:1], scale=1.0 / D)
    kb.recip(rstdB[:, :G], tmpA[:, :G], [tmpA], [rstdB])
    for c in range(8):
        kb.stt(xg[:, c, :], xg[:, c, :], gcol[:, c:c + 1], rstdB[:, :G], ALU.mult, ALU.mult, [(xg, c), gcol, rstdB], [(xg, c)])


def _col8(v):
    return np.ascontiguousarray(v.reshape(8, 128).T)


def prep_c1(l, inp):
    w_in = inp["w_in"][l]
    d = {}
    d["gA"] = _col8(inp["attn_norm"][l])
    d["w_sgu"] = np.ascontiguousarray(w_in[:, 2840:3864])
    d["sgu_gB"] = np.ascontiguousarray(np.broadcast_to(inp["sgu_norm"][l][None, :], (128, 512)))
    d["sgu_wT"] = np.ascontiguousarray(inp["sgu_w"][l].transpose(2, 0, 1))
    d["sgu_bB"] = np.ascontiguousarray(np.broadcast_to(inp["sgu_b"][l][None], (128, 4, 128)))
    d["cm01"] = np.triu(np.ones((128, 128), np.float32))
    d["w_br"] = np.ascontiguousarray(np.concatenate([inp["w_branch_a"][l], inp["w_branch_b"][l], inp["w_branch_c"][l]], 0))
    d["w_merge"] = np.ascontiguousarray(inp["w_merge"][l])
    d["bm"] = np.ascontiguousarray(inp["b_merge"][l].reshape(24, 128).T)
    d["w_out"] = np.ascontiguousarray(inp["w_out"][l])
    return d


def prep_c2(l, inp):
    d = {}
    d["gF"] = _col8(inp["ffn_norm"][l])
    d["gZ"] = _col8(inp["final_norm"])
    d["w1"] = np.ascontiguousarray(inp["w_ffn1"][l])
    d["w3"] = np.ascontiguousarray(inp["w_ffn3"][l])
    d["w2"] = np.ascontiguousarray(inp["w_ffn2"][l])
    return d


NFM = 17
FM_ROPE = {0: 9, 1: 10, 3: 11, 4: 12, 5: 13, 6: 14, 7: 15, 8: 16}
TWO_PI = 2.0 * math.pi
C1 = 6.28125
C2 = TWO_PI - C1
BIGS = 1.0e9


def build_ab(S, stop=None, kb=None, io=None, sfx=""):
    own = kb is None
    if own:
        kb = KB()
    kb.io = io or {}
    kb.begin_phase(sfx)
    PG = 256
    NPG = S // PG
    QG = 512
    NQ = S // QG
    NT = S // 128
    NCP = S // 16
    ncmp = NCP - 1
    NCT = (NCP + 127) // 128
    NCW = NCT * 128

    h_src = kb.io.get("h_src")
    o_piece = kb.io.get("o_piece")
    xT = kb.din("xT", [D, S]) if h_src is None else None
    posB_d = kb.din("posB", [128, S], I32)
    gA_d = kb.din("gA", [128, 8])
    wfm_d = kb.din("w_fm", [D, NFM * 128])
    wtm_d = kb.din("w_tm", [D, 396])
    identb_d = kb.din("identb", [128, 128], BF16)
    identN_d = kb.din("identN", [128, 128], BF16)
    identF_d = kb.din("identf", [128, 128])
    colc_d = kb.din("colc", [128, 4])
    cmpbias_d = kb.din("cmpbias", [128, 5, 512], BF16)
    cdiag_d = kb.din("cdiag", [128, 2, 128], BF16)
    EE_d = kb.din("EE", [128, NT, 128], BF16)
    eaW_d = kb.din("eaW", [128, 2, 254])
    ovl_d = kb.din("ovl", [128, NCT, 128], BF16)
    w1kv_d = kb.din("w1kv", [128, 32, 256])
    posT_d = kb.din("posT", [128, 32])
    w2k_d = kb.din("w2k", [128, 2, 128])
    w2v_d = kb.din("w2v", [128, 2, 64])
    lqk_d = kb.din("lqk", [128, 4, 64])
    sublnB_d = kb.din("sublnB", [128, 128])
    oT = kb.dout("oT", [512, S], BF16) if o_piece is None else None
    qscr = kb.dscr("qscr", [7, 128, S], BF16)

    ident = kb.sb("ident", [128, 128], BF16)
    identN = kb.sb("identN", [128, 128], BF16)
    identF = kb.sb("identF", [128, 128], F32)
    ones = kb.sb("ones", [128, 128], BF16)
    gA = kb.sb("gA_s", [128, 8], F32)
    epsT = kb.sb("epsT", [128, 1], F32)
    eps128 = kb.sb("eps128", [128, 1], F32)
    colc = kb.sb("colc_s", [128, 4], F32)
    cmpbias = kb.sb("cmpbias_s", [128, 5, 512], BF16)
    cdiag = kb.sb("cdiag_s", [128, 2, 128], BF16)
    eaW = kb.sb("eaW_s", [128, 2, 254], F32)
    sublnB = kb.sb("sublnB_s", [128, 128], F32)
    lqk = kb.sb("lqk_s", [128, 4, 64], F32)
    lsc = kb.sb("lsc", [128, 8], F32)
    kslc = kb.sb("kslc", [128, S], BF16)
    kb0 = kb.sb("kb0", [128, S], BF16)
    kb1 = kb.sb("kb1", [128, S], BF16)
    Vnsa = kb.sb("Vnsa", [128, NT, 130], BF16)
    Vd = kb.sb("Vd", [128, NT, 258], BF16)
    gates = kb.sb("gates", [128, NT, 12], F32)
    kcT = kb.sb("kcT", [128, NCW], BF16)
    Vc = kb.sb("Vc", [128, NCT, 193], BF16)
    kvT1 = kb.sb("kvT1", [128, S], BF16)
    EE = Tn(kb.nc.alloc_sbuf_tensor_at("EE_s" + kb.sfx, [128, NT, 128], BF16, offset=kb.off - 2 * S), "EE_s")
    PS = [kb.psum("ps%d" % i) for i in range(7)]
    PSB = kb.psum("psb", [128, 1024], BF16)
    mark = kb.off

    Wfm = kb.sb("Wfm", [128, 8, NFM * 128], BF16)
    Wtm = kb.sb("Wtm", [128, 8, 396], BF16)
    xg = kb.sb("xg", [128, 8, PG], F32)
    hT = kb.sb("hT", [128, 8, PG], BF16)
    tmpA = kb.sb("tmpA", [128, PG], F32)
    rstdB = kb.sb("rstdB", [128, PG], F32)
    posi = kb.sb("posi", [128, PG], I32)
    ang = kb.sb("ang", [128, PG], F32)
    ra = kb.sb("ra", [128, PG], F32)
    rk = kb.sb("rk", [128, PG], F32)
    rki = kb.sb("rki", [128, PG], I32)
    rfix = kb.sb("rfix", [128, PG], F32)
    cosT = kb.sb("cosT", [128, PG], F32)
    sinT = kb.sb("sinT", [128, PG], F32)
    t1 = kb.sb("t1", [128, PG], F32)
    t2 = kb.sb("t2", [128, PG], F32)
    qst = [kb.sb("qstP%d" % i, [128, 7, PG], BF16) for i in range(2)]
    P_list = [Wfm, Wtm, xg, hT, tmpA, rstdB, posi, ang, ra, rk, rki, rfix, cosT, sinT, t1, t2] + qst
    kb.alloc_log.append(EE)
    kb.phase_barrier()

    kb.memset("dve", ones[:, :], 1.0, [ones])
    kb.memset("dve", epsT[:, :], EPS, [epsT])
    kb.memset("dve", eps128[:, :], EPS, [eps128])
    kb.memset("pool", Vnsa[:, :, :], 1.0, [Vnsa])
    kb.memset("pool", Vd[:, :, :], 1.0, [Vd])
    for (dst, src) in ((ident, identb_d), (identN, identN_d), (identF, identF_d), (gA, gA_d), (colc, colc_d), (cmpbias, cmpbias_d),
                       (cdiag, cdiag_d), (eaW, eaW_d), (sublnB, sublnB_d), (lqk, lqk_d)):
        kb.dma("q_sp", dst.t[:], src.t, [src], [dst])
    load_w(kb, "q_pool", Wfm, wfm_d, 8)
    load_w(kb, "q_pool", Wtm, wtm_d, 8)
    invf = colc[:, 0:1]
    sgn = colc[:, 1:2]
    kb.tt("dve", lqk[:, 0, :], lqk[:, 0, :], lqk[:, 1, :], ALU.mult, [lqk], [lqk])
    kb.tt("dve", lqk[:, 2, :], lqk[:, 2, :], lqk[:, 3, :], ALU.mult, [lqk], [lqk])
    kb.pr.add("dve", lambda e: e.reduce_sum(out=lsc[:, 0:1], in_=lqk[:, 0, :], axis=AX.X), [lqk], [lsc])
    kb.pr.add("dve", lambda e: e.reduce_sum(out=lsc[:, 1:2], in_=lqk[:, 2, :], axis=AX.X), [lqk], [lsc])
    kb.act(lsc[:, 2:4], lsc[:, 0:2], AF.Exp, [lsc], [lsc])
    kb.tt("dve", lsc[:, 4:5], lsc[:, 3:4], lsc[:, 2:3], ALU.subtract, [lsc], [lsc])
    kb.tt("dve", lsc[:, 4:5], lsc[:, 4:5], colc[:, 2:3], ALU.subtract, [lsc, colc], [lsc])
    neglam = lsc[:, 4:5]
    kb.ts("dve", sublnB[:, :], sublnB[:, :], colc[:, 3:4], None, ALU.mult, None, [sublnB, colc], [sublnB])

    if stop == 'C':
        return _fin(kb, own)
    xTv = xT.t.rearrange("(c p) t -> p c t", p=128) if xT is not None else None
    pi = [0]

    def nps():
        p = PS[pi[0] % 7]
        pi[0] += 1
        return p

    for gi in range(NPG):
        t0 = gi * PG
        if h_src is None:
            for c in range(8):
                kb.dma("q_sp", xg[:, c, :], xTv[:, c, t0:t0 + PG], [xT], [(xg, c)])
        kb.dma("q_sp", posi[:, :], posB_d[:, t0:t0 + PG], [posB_d], [posi])
        if h_src is None:
            rmsnorm_fm(kb, xg, hT, gA, ones, epsT, nps(), tmpA, rstdB, PG)
        else:
            Th = S // 2
            rr, col = t0 // Th, t0 % Th
            for c in range(8):
                kb.dma("q_sp", hT[:, c, :], h_src.t[c, rr * 128:(rr + 1) * 128, col:col + PG], [h_src], [(hT, c)])
        if stop == 'P1':
            return _fin(kb, own)
        kb.cp("dve", ang[:, :], posi[:, :], [posi], [ang])
        kb.ts("dve", ang[:, :], ang[:, :], invf, None, ALU.mult, None, [ang, colc], [ang])
        for which in range(2):
            dst = sinT if which == 0 else cosT
            if which == 0:
                src = ang
            else:
                kb.ts("dve", ra[:, :], ang[:, :], math.pi / 2, None, ALU.add, None, [ang], [ra])
                src = ra
            kb.ts("dve", rk[:, :], src[:, :], 1.0 / TWO_PI, None, ALU.mult, None, [src], [rk])
            kb.cp("dve", rki[:, :], rk[:, :], [rk], [rki])
            kb.cp("dve", rk[:, :], rki[:, :], [rki], [rk])
            kb.pr.add("dve", lambda e, src=src: e.scalar_tensor_tensor(out=rfix[:, :], in0=rk[:, :], scalar=-C1, in1=src[:, :],
                                                                       op0=ALU.mult, op1=ALU.add), [rk, src], [rfix])
            kb.pr.add("dve", lambda e: e.scalar_tensor_tensor(out=rfix[:, :], in0=rk[:, :], scalar=-C2, in1=rfix[:, :],
                                                              op0=ALU.mult, op1=ALU.add), [rk, rfix], [rfix])
            kb.ts("dve", rk[:, :], rfix[:, :], math.pi, -TWO_PI, ALU.is_gt, ALU.mult, [rfix], [rk])
            kb.tt("dve", rfix[:, :], rfix[:, :], rk[:, :], ALU.add, [rfix, rk], [rfix])
            kb.ts("dve", rk[:, :], rfix[:, :], -math.pi, TWO_PI, ALU.is_lt, ALU.mult, [rfix], [rk])
            kb.tt("dve", rfix[:, :], rfix[:, :], rk[:, :], ALU.add, [rfix, rk], [rfix])
            kb.ts("dve", rfix[:, :], rfix[:, :], math.pi, -math.pi, ALU.min, ALU.max, [rfix], [rfix])
            if which == 0:
                kb.act(dst[:, :], rfix[:, :], AF.Sin, [rfix, colc], [dst], scale=sgn)
            else:
                kb.act(dst[:, :], rfix[:, :], AF.Sin, [rfix], [dst])
        if stop == 'P2':
            return _fin(kb, own)
        qs = qst[gi % 2]
        for ch in range(9):
            pp = nps()
            for k in range(8):
                kb.mm(pp[:, :PG], Wfm[:, k, ch * 128:(ch + 1) * 128], hT[:, k, :], k == 0, k == 7, [(Wfm, k), hT], [pp])
            if ch in (0, 1):
                kb.cp("act", qs[:, ch, :], pp[:, :PG], [pp], [(qs, ch)])
            if ch == 2:
                kb.cp("act", kvT1[:, t0:t0 + PG], pp[:, :PG], [pp], [(kvT1, gi)])
                continue
            sw = FM_ROPE[ch]
            psw = nps()
            for k in range(8):
                kb.mm(psw[:, :PG], Wfm[:, k, sw * 128:(sw + 1) * 128], hT[:, k, :], k == 0, k == 7, [(Wfm, k), hT], [psw])
            kb.tt("dve", t1[:, :], pp[:, :PG], cosT[:, :], ALU.mult, [pp, cosT], [t1])
            kb.tt("dve", t2[:, :], psw[:, :PG], sinT[:, :], ALU.mult, [psw, sinT], [t2])
            if ch in (0, 1):
                dst, dk, dt_ = qs[:, 2 + ch, :], (qs, 2 + ch), qs
            elif ch == 3:
                dst, dk, dt_ = kslc[:, t0:t0 + PG], (kslc, gi), kslc
            elif ch == 4:
                dst, dk, dt_ = qs[:, 6, :], (qs, 6), qs
            elif ch in (5, 6):
                dst, dk, dt_ = qs[:, ch - 1, :], (qs, ch - 1), qs
            elif ch == 7:
                dst, dk, dt_ = kb0[:, t0:t0 + PG], (kb0, gi), kb0
            else:
                dst, dk, dt_ = kb1[:, t0:t0 + PG], (kb1, gi), kb1
            kb.tt("pool", dst, t1[:, :], t2[:, :], ALU.add, [t1, t2], [dk])
        if stop == 'P3':
            return _fin(kb, own)
        for j in range(7):
            kb.dma("q_sp", qscr[j, :, t0:t0 + PG], qs[:, j, :], [(qs, j)], [qscr])
        if stop == 'P4':
            return _fin(kb, own)
        for tt in range(PG // 128):
            T_ = gi * (PG // 128) + tt
            p = nps()
            for k in range(8):
                kb.mm(p[:, :396], hT[:, k, tt * 128:(tt + 1) * 128], Wtm[:, k, :], k == 0, k == 7, [hT, (Wtm, k)], [p])
            kb.cp("act", Vnsa[:, T_, 0:64], p[:, 0:64], [p], [(Vnsa, T_)])
            kb.cp("act", Vnsa[:, T_, 65:129], p[:, 64:128], [p], [(Vnsa, T_)])
            kb.act(gates[:, T_, :], p[:, 128:140], AF.Sigmoid, [p], [(gates, T_)])
            kb.cp("dve", Vd[:, T_, 0:128], p[:, 140:268], [p], [(Vd, T_)])
            kb.cp("dve", Vd[:, T_, 129:257], p[:, 268:396], [p], [(Vd, T_)])
        if stop == 'P5' or (stop == 'P6' and gi == 1):
            return _fin(kb, own)

    if stop == 'P':
        return _fin(kb, own)
    kb.off = mark
    w1kv = kb.sb("w1kv", [128, 32, 256], BF16)
    posT = kb.sb("posT", [128, 32], BF16)
    w2k = kb.sb("w2k", [128, 2, 128], BF16)
    w2v = kb.sb("w2v", [128, 2, 64], BF16)
    hidT = kb.sb("hidT", [128, 2, 2, NCW], BF16)
    posb = kb.sb("posb", [128, 4], F32)
    X_list = [w1kv, posT, w2k, w2v, hidT, posb]
    kb.barrier(P_list, X_list)
    for l4 in range(4):
        kb.dma("q_pool", w1kv[:, l4 * 8:(l4 + 1) * 8, :], w1kv_d[:, l4 * 8:(l4 + 1) * 8, :], [w1kv_d], [(w1kv, l4)])
    kb.dma("q_pool", posT[:, :], posT_d.t, [posT_d], [posT])
    kb.dma("q_pool", w2k[:, :, :], w2k_d.t, [w2k_d], [w2k])
    kb.dma("q_pool", w2v[:, :, :], w2v_d.t, [w2v_d], [w2v])
    kb.memset("pool", hidT[:, :, :, :], 0.0, [hidT])
    kvv = kvT1.t.reshape([128, NCP, 16])
    for which in range(2):
        r0 = 64 * which
        for half in range(2):
            ph = nps()
            for l in range(32):
                kb.mm(ph[:, :ncmp], w1kv[r0:r0 + 64, l, half * 128:(half + 1) * 128],
                      kvv[r0:r0 + 64, (l // 16):(l // 16) + ncmp, l % 16], l == 0, l == 31, [w1kv, kvT1], [ph])
            pb = nps()
            for l in range(32):
                kb.mm(pb[:, 0:1], w1kv[r0:r0 + 64, l, half * 128:(half + 1) * 128], posT[r0:r0 + 64, l:l + 1], l == 0, l == 31,
                      [w1kv, posT], [pb])
            idx = which * 2 + half
            kb.cp("dve", posb[:, idx:idx + 1], pb[:, 0:1], [pb], [(posb, idx)])
            kb.act(hidT[:, which, half, :ncmp], ph[:, :ncmp], AF.Gelu_apprx_tanh, [ph, (posb, idx)], [(hidT, idx)],
                   bias=posb[:, idx:idx + 1])
    pk = nps()
    for half in range(2):
        kb.mm(pk[:, :NCW], w2k[:, half, :], hidT[:, 0, half, :], half == 0, half == 1, [w2k, hidT], [pk])
    kb.cp("act", kcT[:, :], pk[:, :NCW], [pk], [kcT])
    kb.memset("pool", Vc[:, :, :], 1.0, [Vc])
    for nt in range(NCT):
        pv = nps()
        for half in range(2):
            kb.mm(pv[:, 0:64], hidT[:, 1, half, nt * 128:(nt + 1) * 128], w2v[:, half, :], half == 0, half == 1, [hidT, w2v], [pv])
        kb.cp("dve", Vc[:, nt, 0:64], pv[:, 0:64], [pv], [Vc])
    kb.dma("q_sp", Vc[:, :, 65:193], ovl_d.t, [ovl_d], [Vc])

    if stop == 'X':
        return _fin(kb, own)
    kb.off = mark
    qstA = [kb.sb("qstA%d" % i, [128, 6, QG], BF16) for i in range(2)]
    kwst = [kb.sb("kwst%d" % i, [128, 1024], BF16) for i in range(2)]
    PT = [kb.sb("PT%d" % i, [128, 512], BF16) for i in range(3)]
    negT = kb.sb("negT", [128, 512], BF16)
    Oev = [kb.sb("Oev%d" % i, [128, 4, 193], F32) for i in range(2)]
    rc = [kb.sb("rc%d" % i, [128, 4], F32) for i in range(2)]
    coef = [kb.sb("coef%d" % i, [128, 4], F32) for i in range(2)]
    ocmp = kb.sb("ocmp", [128, 4, 4, 64], F32)
    oacc = kb.sb("oacc", [128, 4, 256], F32)
    obt = kb.sb("obt", [128, 4, 256], F32)
    imp = kb.sb("imp", [128, 4, 128], F32)
    score = kb.sb("score", [128, 4, 128], F32)
    sc2 = kb.sb("sc2", [128, 4, 128], F32)
    m8 = kb.sb("m8", [128, 4, 8], F32)
    neg01 = kb.sb("neg01", [128, 4, 128], BF16)
    od0 = kb.sb("od0", [128, 4, 128], F32)
    od1 = kb.sb("od1", [128, 4, 128], F32)
    djunk = kb.sb("djunk", [128, 128], F32)
    dss = kb.sb("dss", [128, 4], F32)
    o16 = kb.sb("o16", [128, 4, 512], BF16)
    oTs = kb.sb("oTs", [128, 4, 512], BF16)
    accS = [kb.sb("accS%d" % i, [128, 512], F32) for i in range(2)]
    sumS = [kb.sb("sumS%d" % i, [1, 512], F32) for i in range(2)]
    A_list = accS + sumS + qstA + kwst + PT + [negT, ocmp, oacc, obt, imp, score, sc2, m8, neg01, od0, od1, djunk, dss, o16, oTs] + Oev + rc + coef
    kb.barrier(X_list + P_list + [kvT1], A_list + [EE])
    kb.dma("q_sp", EE.t[:], EE_d.t, [EE_d], [EE])
    LB = [PS[0], PS[1]]
    ACC = [(PS[4], PS[5]), (PS[4], PS[5])]
    ACCT = [PS[2], PS[3]]
    PSM = PS[6]
    fi = [0]

    def flip_finish(accT, wm, acc, w, nbq, sums_row):
        fi[0] += 1
        aS = accS[fi[0] % 2]
        kb.cp("dve", aS[0:wm, :], accT[0:wm, :512], [accT], [aS])
        if sums_row is not None:
            sS = sumS[fi[0] % 2]
            kb.cp("dve", sS[0:1, :], sums_row, [PSM], [sS])
        for qt in range(4):
            bnk, sl = qt // nbq, (qt % nbq) * w
            kb.tr(acc[bnk][:, sl:sl + wm], aS[0:wm, qt * 128:(qt + 1) * 128], identF[0:wm, 0:wm], [aS, identF], [acc[bnk]])
            if sums_row is not None:
                kb.mm(acc[bnk][:, sl + wm:sl + wm + 1], sS[0:1, qt * 128:(qt + 1) * 128], identF[0:1, 0:1], True, True, [sS, identF], [acc[bnk]])
    li = [0]
    ai = [0]
    pti = [0]
    ei = [0]
    oTv = oT.t.rearrange("(c p) t -> p c t", p=128) if oT is not None else None

    def nL():
        li[0] += 1
        return LB[li[0] % 2]

    def nA():
        ai[0] += 1
        return ACC[ai[0] % 2]

    def nPT():
        pti[0] += 1
        return PT[pti[0] % 3]

    def nE():
        ei[0] += 1
        return Oev[ei[0] % 2], rc[ei[0] % 2], coef[ei[0] % 2]

    def evac(acc, w, nbank_q):
        ev, r_, cf = nE()
        nb = 4 // nbank_q
        for bnk in range(nb):
            kb.cp("dve", ev[:, bnk * nbank_q:(bnk + 1) * nbank_q, 0:w],
                  acc[bnk][:, 0:nbank_q * w].rearrange("p (q w) -> p q w", w=w), [acc[bnk]], [(ev, bnk)])
        sumcol = 64 if w in (65, 193) else 128
        kb.ts("dve", r_[:, :], ev[:, :, sumcol], 1e-30, None, ALU.max, None, [ev], [r_])
        kb.recip(r_[:, :], r_[:, :], [r_], [r_])
        return ev, r_, cf

    for Q in range(NQ):
        q0 = Q * QG
        qs = qstA[Q % 2]
        kw = kwst[Q % 2]
        for j in range(6):
            kb.dma("q_sp", qs[:, j, :], qscr[j, :, q0:q0 + QG], [qscr], [(qs, j)])
        klo = max(0, q0 - 512)
        kb.dma("q_sp", kw[:, (klo - (q0 - 512)):1024], qscr[6, :, klo:q0 + 512], [qscr], [kw])
        for h in range(4):
            r0 = 64 * (h % 2)
            qa = qs[r0:r0 + 64, h // 2, :]
            nts = [nt for nt in range(NCT) if Q - 4 * nt >= 0]
            acc = nA()
            def c_qk(ix, nt):
                Dd = Q - 4 * nt
                L = nL()
                kb.mm(L[:, :512], kcT[r0:r0 + 64, nt * 128:(nt + 1) * 128], qa, True, Dd > 4, [kcT, qs], [L])
                if Dd <= 4:
                    kb.mm(L[:, :512], identN[:, :], cmpbias[:, Dd, :], False, True, [identN, cmpbias], [L])
                pt = nPT()
                kb.act(pt[:, :], L[:, :512], AF.Exp, [L], [pt], scale=0.125)
                return pt

            def c_pv(ix, nt, pt):
                for qt in range(4):
                    kb.mm(acc[qt // 2][:, (qt % 2) * 193:(qt % 2) * 193 + 193], pt[:, qt * 128:(qt + 1) * 128], Vc[:, nt, :],
                          ix == 0 and qt % 2 == 0, ix == len(nts) - 1, [pt, Vc], [acc[qt // 2]])

            prev = None
            for ix, nt in enumerate(nts):
                pt = c_qk(ix, nt)
                if prev is not None:
                    c_pv(*prev)
                prev = (ix, nt, pt)
            c_pv(*prev)
            ev, r_, cf = evac(acc, 193, 2)
            for qt in range(4):
                kb.ts("dve", ocmp[:, h, qt, :], ev[:, qt, 0:64], r_[:, qt:qt + 1], None, ALU.mult, None, [ev, r_], [(ocmp, h)])
                if h == 0:
                    kb.ts("dve", imp[:, qt, :], ev[:, qt, 65:193], r_[:, qt:qt + 1], None, ALU.mult, None, [ev, r_], [imp])
                else:
                    kb.stt(imp[:, qt, :], ev[:, qt, 65:193], r_[:, qt:qt + 1], imp[:, qt, :], ALU.mult, ALU.add, [ev, r_, imp], [imp])
        for qt in range(4):
            qta = 4 * Q + qt
            off = 126 - 2 * qta
            kb.tt("dve", score[:, qt, :], imp[:, qt, :], eaW[:, 0, off:off + 128], ALU.mult, [imp, eaW], [score])
            kb.tt("dve", score[:, qt, :], score[:, qt, :], eaW[:, 1, off:off + 128], ALU.add, [score, eaW], [score])
            kb.memset("dve", score[:, qt, 0:1], BIGS, [score])
            kb.pr.add("dve", lambda e, qt=qt: e.max(out=m8[:, qt, :], in_=score[:, qt, :]), [score], [m8])
            kb.pr.add("dve", lambda e, qt=qt: e.match_replace(out=sc2[:, qt, :], in_to_replace=m8[:, qt, :],
                                                              in_values=score[:, qt, :], imm_value=-3.0e38), [score, m8], [sc2])
            kb.pr.add("dve", lambda e, qt=qt: e.max(out=m8[:, qt, :], in_=sc2[:, qt, :]), [sc2], [m8])
            kb.ts("dve", neg01[:, qt, :], score[:, qt, :], m8[:, qt, 7:8], 1.0, ALU.is_ge, ALU.subtract, [score, m8], [neg01])
            kb.tr(PSB[:, qt * 128:(qt + 1) * 128], neg01[:, qt, :], ident[:, :], [neg01, ident], [PSB])
        kb.cp("dve", negT[:, :], PSB[:, 0:512], [PSB], [negT])
        for h in range(4):
            r0 = 64 * (h % 2)
            qr = qs[r0:r0 + 64, 2 + h // 2, :]
            acc = nA()
            firstb = True
            ilist = [i for i in range(8) if 4 * Q - 4 + i >= 0]
            def w_qk(i):
                qts = [qt for qt in range(4) if 0 <= 4 - i + qt <= 4]
                c0, c1 = qts[0] * 128, (qts[-1] + 1) * 128
                L = nL()
                kb.mm(L[:, c0:c1], kw[r0:r0 + 64, i * 128:(i + 1) * 128], qr[:, c0:c1], True, False, [kw, qs], [L])
                for qt in qts:
                    dd = 4 - i + qt
                    if dd == 0:
                        kb.mm(L[:, qt * 128:(qt + 1) * 128], identN[:, :], cdiag[:, 0, :], False, True, [identN, cdiag], [L])
                    elif dd == 4:
                        kb.mm(L[:, qt * 128:(qt + 1) * 128], identN[:, :], cdiag[:, 1, :], False, True, [identN, cdiag], [L])
                pt = nPT()
                kb.act(pt[:, c0:c1], L[:, c0:c1], AF.Exp, [L], [pt], scale=0.125)
                return qts, pt

            fb = [True]
            fi[0] += 0
            accT = ACCT[(ai[0]) % 2]

            def w_pv(i, qts, pt):
                kt = 4 * Q - 4 + i
                c0, c1 = qts[0] * 128, (qts[-1] + 1) * 128
                kb.mm(accT[0:65, c0:c1], Vnsa[:, kt, 65:130], pt[:, c0:c1], fb[0], i == ilist[-1], [pt, Vnsa], [accT])
                fb[0] = False

            prev = None
            for i in ilist:
                qts, pt = w_qk(i)
                if prev is not None:
                    w_pv(*prev)
                prev = (i, qts, pt)
            w_pv(*prev)
            flip_finish(accT, 65, acc, 65, 4, None)
            ev, r_, cf = evac(acc, 65, 4)
            for qt in range(4):
                qta = 4 * Q + qt
                kb.tt("dve", cf[:, qt:qt + 1], r_[:, qt:qt + 1], gates[:, qta, h * 3 + 2:h * 3 + 3], ALU.mult, [r_, gates], [cf])
                kb.ts("dve", oacc[:, qt, h * 64:(h + 1) * 64], ocmp[:, h, qt, :], gates[:, qta, h * 3:h * 3 + 1], None, ALU.mult, None,
                      [(ocmp, h), gates], [(oacc, h)])
                kb.stt(oacc[:, qt, h * 64:(h + 1) * 64], ev[:, qt, 0:64], cf[:, qt:qt + 1], oacc[:, qt, h * 64:(h + 1) * 64],
                       ALU.mult, ALU.add, [ev, cf, (oacc, h)], [(oacc, h)])

        def causal_unit(qap, kT, vfn, w, nbq, use_sel, qbuf):
            acc = nA()
            nkt = 4 * Q + 4
            ktn, vtn = kT_tn[0], vT_tn[0]

            def u_qk(kt):
                i = kt - 4 * Q
                c0 = max(i, 0) * 128
                L = nL()
                kb.mm(L[:, c0:512], kT[:, kt * 128:(kt + 1) * 128], qap[:, c0:512], True, False, [ktn, qbuf], [L])
                if use_sel:
                    kb.mm(L[:, c0:512], EE[:, kt, :], negT[:, c0:512], False, False, [EE, negT], [L])
                if i >= 0:
                    kb.mm(L[:, c0:c0 + 128], identN[:, :], cdiag[:, 0, :], False, True, [identN, cdiag], [L])
                pt = nPT()
                kb.act(pt[:, c0:512], L[:, c0:512], AF.Exp, [L], [pt], scale=0.125)
                return pt

            accT = ACCT[ai[0] % 2]
            wm = 65 if w == 65 else 128
            srow = None
            if w == 129:
                rb = 32 * (ai[0] % 2)
                srow = PSM[rb:rb + 1, 0:512]

            def u_pv(kt, pt):
                i = kt - 4 * Q
                c0 = max(i, 0) * 128
                vap = vfn(kt)
                kb.mm(accT[0:wm, c0:512], vap[:, 0:wm], pt[:, c0:512], kt == 0, kt == nkt - 1, [pt, vtn], [accT])
                if w == 129:
                    kb.mm(PSM[rb:rb + 1, c0:512], ones[:, 0:1], pt[:, c0:512], kt == 0, kt == nkt - 1, [pt, ones], [PSM])

            prev = None
            for kt in range(nkt):
                pt = u_qk(kt)
                if prev is not None:
                    u_pv(*prev)
                prev = (kt, pt)
            u_pv(*prev)
            flip_finish(accT, wm, acc, w, nbq, srow)
            return evac(acc, w, nbq)

        kT_tn = [None]
        vT_tn = [None]
        for hh in range(2):
            for m in range(2):
                mi = hh * 2 + m
                kT_tn[0] = kb0 if mi < 2 else kb1
                vT_tn[0] = Vd
                r0 = 64 * (mi % 2)
                qap = qs[r0:r0 + 64, 4 + mi // 2, :]
                ev, r_, cf = causal_unit(qap, kT_tn[0][r0:r0 + 64, :], lambda kt, hh=hh: Vd[:, kt, hh * 129:hh * 129 + 129], 129, 2, False, qs)
                if m == 0:
                    for qt in range(4):
                        kb.ts("dve", od0[:, qt, :], ev[:, qt, 0:128], r_[:, qt:qt + 1], None, ALU.mult, None, [ev, r_], [od0])
                else:
                    for qt in range(4):
                        kb.ts("dve", od1[:, qt, :], ev[:, qt, 0:128], r_[:, qt:qt + 1], neglam, ALU.mult, ALU.mult, [ev, r_, lsc], [od1])
                        kb.tt("dve", od0[:, qt, :], od0[:, qt, :], od1[:, qt, :], ALU.add, [od0, od1], [od0])
                        kb.stt(djunk[:, :], od0[:, qt, :], 1.0, od0[:, qt, :], ALU.mult, ALU.mult, [od0], [djunk, dss],
                               accum_out=dss[:, qt:qt + 1])
                    kb.act(dss[:, :], dss[:, :], AF.Sqrt, [dss, eps128], [dss], bias=eps128[:, 0:1], scale=1.0 / 128)
                    kb.recip(dss[:, :], dss[:, :], [dss], [dss])
                    for qt in range(4):
                        kb.stt(obt[:, qt, hh * 128:(hh + 1) * 128], od0[:, qt, :], dss[:, qt:qt + 1], sublnB[:, :], ALU.mult, ALU.mult,
                               [od0, dss, sublnB], [(obt, hh)])
        for h in range(4):
            r0 = 64 * (h % 2)
            kT_tn[0] = kslc
            vT_tn[0] = Vnsa
            qap = qs[r0:r0 + 64, 2 + h // 2, :]
            ev, r_, cf = causal_unit(qap, kslc[r0:r0 + 64, :], lambda kt: Vnsa[:, kt, 0:65], 65, 4, True, qs)
            for qt in range(4):
                qta = 4 * Q + qt
                kb.tt("dve", cf[:, qt:qt + 1], r_[:, qt:qt + 1], gates[:, qta, h * 3 + 1:h * 3 + 2], ALU.mult, [r_, gates], [cf])
                kb.stt(oacc[:, qt, h * 64:(h + 1) * 64], ev[:, qt, 0:64], cf[:, qt:qt + 1], oacc[:, qt, h * 64:(h + 1) * 64],
                       ALU.mult, ALU.add, [ev, cf, (oacc, h)], [(oacc, h)])
        kb.cp("act", o16[:, :, 0:256], oacc[:, :, :], [oacc], [o16])
        kb.cp("act", o16[:, :, 256:512], obt[:, :, :], [obt], [o16])
        for c4 in range(4):
            for qt in range(4):
                kb.tr(PSB[:, 512 + qt * 128:512 + (qt + 1) * 128], o16[:, qt, c4 * 128:(c4 + 1) * 128], ident[:, :], [o16, ident], [(PSB, "o")])
            kb.cp("dve", oTs[:, c4, :], PSB[:, 512:1024], [(PSB, "o")], [(oTs, c4)])
            if o_piece is None:
                kb.dma("q_sp", oTv[:, c4, q0:q0 + QG], oTs[:, c4, :], [(oTs, c4)], [oT])
            else:
                hq = NQ // 2
                kb.dma("q_sp", o_piece.t[c4, Q // hq, :, (Q % hq) * QG:(Q % hq + 1) * QG], oTs[:, c4, :], [(oTs, c4)],
                       [(o_piece, (c4, Q // hq))])
        if stop is not None and stop.startswith('A') and int(stop[1:]) == Q:
            return _fin(kb, own)
    return _fin(kb, own)


def _swap64(cols):
    cols = np.asarray(cols).reshape(-1, 64)
    return np.concatenate([cols[:, 32:], cols[:, :32]], axis=1).reshape(-1)


def ab_consts(S):
    NT = S // 128
    NCP = S // 16
    ncmp = NCP - 1
    NCT = (NCP + 127) // 128
    bf = ml_dtypes.bfloat16
    c = {}
    c["identb"] = np.eye(128, dtype=np.float32).astype(bf)
    c["identN"] = (np.eye(128, dtype=np.float32) * NEGM).astype(bf)
    c["identf"] = np.eye(128, dtype=np.float32)
    n_ = np.arange(128)[:, None, None]
    D_ = np.arange(5)[None, :, None]
    q_ = np.arange(512)[None, None, :]
    c["cmpbias"] = np.where(16 * n_ + 31 - q_ <= 512 * D_, 0.0, -1.0).astype(np.float32).astype(bf)
    k_ = np.arange(128)[:, None]
    qq = np.arange(128)[None, :]
    cd = np.stack([np.where(k_ <= qq, 0.0, -1.0), np.where(k_ > qq, 0.0, -1.0)], axis=1)
    c["cdiag"] = cd.astype(np.float32).astype(bf)
    j_ = np.arange(128)[:, None, None]
    kt_ = np.arange(NT)[None, :, None]
    kk = np.arange(128)[None, None, :]
    c["EE"] = np.where(j_ == 2 * kt_ + kk // 64, NEGM, 0.0).astype(np.float32).astype(bf)
    qp = np.arange(128)[:, None]
    rel = np.arange(254)[None, :] - 126
    cur = qp // 64
    elig = (rel <= cur).astype(np.float32)
    addw = np.where((rel == cur) | (rel == cur - 1), BIGS, np.where(rel > cur, -BIGS, 0.0)).astype(np.float32)
    c["eaW"] = np.ascontiguousarray(np.stack([elig, addw], axis=1))
    n = np.arange(NCT * 128)[:, None]
    j = np.arange(128)[None, :]
    ov = ((16 * n < 64 * j + 64) & (16 * n + 32 > 64 * j) & (n < ncmp)).astype(np.float32)
    c["ovl"] = np.ascontiguousarray(ov.reshape(NCT, 128, 128).transpose(1, 0, 2)).astype(bf)
    return c


def prep_ab(l, inp, b, g, S):
    w_in = inp["w_in"][l]
    d = {}
    d["posB"] = np.ascontiguousarray(np.broadcast_to(inp["positions"][b][None, :S], (128, S))).astype(np.int32)
    d["gA"] = _col8(inp["attn_norm"][l])
    r64 = np.arange(64)
    r128 = np.arange(128)
    plain = [256 * g + r128, 256 * g + 128 + r128,
             np.concatenate([512 + 64 * g + r64, 512 + 128 + 64 * g + r64]),
             np.concatenate([512 + 256 + 64 * g + r64] * 2),
             np.concatenate([512 + 512 + 64 * g + r64] * 2),
             1304 + 256 * g + r128, 1304 + 256 * g + 128 + r128,
             1816 + 256 * g + r128, 1816 + 256 * g + 128 + r128]
    swaps = [_swap64(plain[i]) for i in (0, 1, 3, 4, 5, 6, 7, 8)]
    cols = np.concatenate(plain + swaps)
    d["w_fm"] = np.ascontiguousarray(w_in[:, cols])
    tcols = np.concatenate([512 + 384 + 64 * g + r64, 512 + 640 + 64 * g + r64, 1280 + 12 * g + np.arange(12),
                            2328 + 256 * g + np.arange(256)])
    d["w_tm"] = np.ascontiguousarray(w_in[:, tcols])
    p = np.arange(128)
    lam_init = 0.8 - 0.6 * math.exp(-0.3 * l)
    colc = np.zeros((128, 4), np.float32)
    colc[:, 0] = (10000.0 ** (-(np.arange(32, dtype=np.float32)) / 32.0)).astype(np.float32)[p % 32]
    colc[:, 1] = np.where((p % 64) < 32, -1.0, 1.0)
    colc[:, 2] = lam_init
    colc[:, 3] = 1.0 - lam_init
    d["colc"] = colc
    w1k = inp["cmp_k_w1"][l].reshape(32, 64, 256).transpose(1, 0, 2)
    w1v = inp["cmp_v_w1"][l].reshape(32, 64, 256).transpose(1, 0, 2)
    d["w1kv"] = np.ascontiguousarray(np.concatenate([w1k, w1v], axis=0))
    d["posT"] = np.ascontiguousarray(np.concatenate([inp["cmp_pos_k"][l].T, inp["cmp_pos_v"][l].T], axis=0))
    w2k = inp["cmp_k_w2"][l].reshape(2, 128, 64).transpose(1, 0, 2)
    d["w2k"] = np.ascontiguousarray(np.concatenate([w2k, w2k], axis=2))
    d["w2v"] = np.ascontiguousarray(inp["cmp_v_w2"][l].reshape(2, 128, 64).transpose(1, 0, 2))
    lq = np.stack([inp["diff_lq1"][l], inp["diff_lk1"][l], inp["diff_lq2"][l], inp["diff_lk2"][l]], axis=0)
    d["lqk"] = np.ascontiguousarray(np.broadcast_to(lq[None], (128, 4, 64)))
    d["sublnB"] = np.ascontiguousarray(np.broadcast_to(inp["diff_subln"][l][None, :], (128, 128)))
    return d


_CACHE = {}


def _prog(name, fn, *args):
    key = (name,) + args
    if key not in _CACHE:
        _CACHE[key] = fn(*args)
    return _CACHE[key]


def kernel_unfused(**inputs):
    inp = {k: np.asarray(v) for k, v in inputs.items()}
    x = inp["x"].astype(np.float32, copy=False)
    B, S, _ = x.shape
    T = S // 2
    L = inp["w_in"].shape[0]
    cores = list(range(8))
    cst = ab_consts(S)
    xT = [np.ascontiguousarray(x[b].T) for b in range(B)]
    out = None
    for l in range(L):
        nc_ab = build_ab(S)
        maps = []
        for c in cores:
            b, g = c // 2, c % 2
            d = prep_ab(l, inp, b, g, S)
            d.update(cst)
            d["xT"] = xT[b]
            maps.append(d)
        res = run_bass_kernel_spmd(nc_ab, maps, core_ids=cores).results
        oaT = [np.concatenate([res[2 * b]["oT"][0:256], res[2 * b + 1]["oT"][0:256]], axis=0) for b in range(B)]
        obT = [np.concatenate([res[2 * b]["oT"][256:512], res[2 * b + 1]["oT"][256:512]], axis=0) for b in range(B)]
        del res, maps
        nc_c1 = build_c1(T)
        maps = []
        p1 = prep_c1(l, inp)
        for c in cores:
            b, g = c // 2, c % 2
            d = dict(p1)
            d["xT"] = np.ascontiguousarray(xT[b][:, g * T:(g + 1) * T])
            d["oaT"] = np.ascontiguousarray(oaT[b][:, g * T:(g + 1) * T])
            d["obT"] = np.ascontiguousarray(obT[b][:, g * T:(g + 1) * T])
            maps.append(d)
        res = run_bass_kernel_spmd(nc_c1, maps, core_ids=cores).results
        x1T = [res[c]["x1T"] for c in cores]
        del res, maps
        nc_c2 = build_c2(T)
        p2 = prep_c2(l, inp)
        maps = []
        for c in cores:
            d = dict(p2)
            d["x1T"] = x1T[c]
            maps.append(d)
        res = run_bass_kernel_spmd(nc_c2, maps, core_ids=cores).results
        xT = [np.concatenate([res[2 * b]["x2T"], res[2 * b + 1]["x2T"]], axis=1) for b in range(B)]
        if l == L - 1:
            out = np.stack([np.concatenate([res[2 * b]["x2nT"], res[2 * b + 1]["x2nT"]], axis=1).T for b in range(B)], axis=0)
        del res, maps
    return np.ascontiguousarray(out.astype(np.float32))


def build_fused(S, L=2):
    kb = KB()
    xin = kb.din("xT_in", [D, S])
    outT = kb.dout("outT", [D, S])
    xcur = xin
    for l in range(L):
        kb.sfx = "_l%d" % l
        oscr = kb.dscr("oscr", [2, 512, S], BF16)
        for g in range(2):
            ov = Tn(oscr.t[g], "oscr_g")
            ov.b = oscr.b
            build_ab(S, kb=kb, io={"xT": xcur, "oT": ov}, sfx="_l%dg%d" % (l, g))
        kb.sfx = "_l%d" % l
        x1 = kb.dscr("x1scr", [D, S], F32)

        def osrc_fn(which, c, t0, G, oscr=oscr):
            r0 = 256 * which + (c % 2) * 128
            return oscr.t[c // 2, r0:r0 + 128, t0:t0 + G]

        build_c1(S, kb=kb, io={"xT": xcur, "x1T": x1, "osrc_fn": osrc_fn, "osrc_tn": oscr}, sfx="_l%d" % l)
        if l < L - 1:
            kb.sfx = "_l%d" % l
            x2 = kb.dscr("x2scr", [D, S], F32)
            build_c2(S, kb=kb, io={"x1T": x1, "x2T": x2, "want_x2": True, "want_x2n": False}, sfx="_l%d" % l)
            xcur = x2
        else:
            build_c2(S, kb=kb, io={"x1T": x1, "x2nT": outT, "want_x2": False, "want_x2n": True}, sfx="_l%d" % l)
    return kb.finish()


def fused_inputs(inp, b, S, cst):
    L = inp["w_in"].shape[0]
    d = dict(cst)
    d["cm01"] = np.triu(np.ones((128, 128), np.float32))
    d["xT_in"] = np.ascontiguousarray(inp["x"][b].T)
    for l in range(L):
        for g in range(2):
            for k, v in prep_ab(l, inp, b, g, S).items():
                if k == "posB":
                    d["posB"] = v
                else:
                    d["%s_l%dg%d" % (k, l, g)] = v
        for k, v in prep_c1(l, inp).items():
            if k != "cm01":
                d["%s_l%d" % (k, l)] = v
        for k, v in prep_c2(l, inp).items():
            d["%s_l%d" % (k, l)] = v
    return d


def kernel_fused4(**inputs):
    inp = {k: np.asarray(v) for k, v in inputs.items()}
    B, S, _ = inp["x"].shape
    nc = build_fused(S)
    cst = ab_consts(S)
    per_b = [fused_inputs(inp, b, S, cst) for b in range(B)]
    maps = [per_b[c % B] for c in range(8)]
    res = run_bass_kernel_spmd(nc, maps, core_ids=list(range(8))).results
    out = np.stack([res[b]["outT"].T for b in range(B)], axis=0)
    return np.ascontiguousarray(out.astype(np.float32))


RG_PAIRS = [[0, 1], [2, 3], [4, 5], [6, 7]]


def _allgather(kb, src, dst, key):
    kb.pr.add("q_cc", lambda e: e.collective_compute("AllGather", ALU.bypass, replica_groups=RG_PAIRS,
                                                     ins=[src.opt()], outs=[dst.opt()]), [key[0]], [key[1]])


def build_fused8(S, L=2):
    kb = KB()
    T = S // 2
    xfull = kb.din("xT_in", [D, S])
    xhalf = kb.din("xT_half", [D, T])
    outT = kb.dout("outT", [D, T])
    hgath = None
    xown = xhalf
    for l in range(L):
        kb.sfx = "_l%d" % l
        opc = kb.dscr("opc", [4, 2, 128, T], BF16)
        ogath = kb.dscr("ogath", [4, 2, 256, T], BF16)
        io = {"o_piece": opc}
        if hgath is None:
            io["xT"] = xfull
        else:
            io["h_src"] = hgath
        build_ab(S, kb=kb, io=io, sfx="_l%d" % l)
        for c4 in range(4):
            for half in range(2):
                _allgather(kb, opc.t[c4, half], ogath.t[c4, half], ((opc, (c4, half)), (ogath, (c4, half))))
        kb.sfx = "_l%d" % l
        x1 = kb.dscr("x1scr", [D, T], F32)
        build_c1(T, kb=kb, io={"xT": xown, "x1T": x1, "o_gath": ogath}, sfx="_l%d" % l)
        if l < L - 1:
            kb.sfx = "_l%d" % l
            x2 = kb.dscr("x2scr", [D, T], F32)
            hpc = kb.dscr("hpc", [8, 128, T], BF16)
            hg = kb.dscr("hgath", [8, 256, T], BF16)
            build_c2(T, kb=kb, io={"x1T": x1, "x2T": x2, "want_x2": True, "want_x2n": False, "h_next": hpc}, sfx="_l%d" % l)
            for c in range(8):
                _allgather(kb, hpc.t[c], hg.t[c], ((hpc, c), (hg, c)))
            hgath = hg
            xown = x2
        else:
            build_c2(T, kb=kb, io={"x1T": x1, "x2nT": outT, "want_x2": False, "want_x2n": True}, sfx="_l%d" % l)
    return kb.finish()


def fused8_inputs(inp, b, g, S, cst):
    L = inp["w_in"].shape[0]
    T = S // 2
    d = dict(cst)
    d["cm01"] = np.triu(np.ones((128, 128), np.float32))
    xT = np.ascontiguousarray(inp["x"][b].T)
    d["xT_in"] = xT
    d["xT_half"] = np.ascontiguousarray(xT[:, g * T:(g + 1) * T])
    m = np.zeros((128, 2), np.float32)
    m[:, g] = 1.0
    for l in range(L):
        for k, v in prep_ab(l, inp, b, g, S).items():
            if k == "posB":
                d["posB"] = v
            elif not (l > 0 and k == "gA"):
                d["%s_l%d" % (k, l)] = v
        for k, v in prep_c1(l, inp).items():
            if k != "cm01":
                d["%s_l%d" % (k, l)] = v
        d["msel_l%d" % l] = m
        for k, v in prep_c2(l, inp).items():
            d["%s_l%d" % (k, l)] = v
        if l < L - 1:
            d["gN_l%d" % l] = _col8(inp["attn_norm"][l + 1])
    return d


def kernel(**inputs):
    inp = {k: np.asarray(v) for k, v in inputs.items()}
    B, S, _ = inp["x"].shape
    T = S // 2
    nc = build_fused8(S)
    cst = ab_consts(S)
    maps = [fused8_inputs(inp, c // 2, c % 2, S, cst) for c in range(8)]
    res = run_bass_kernel_spmd(nc, maps, core_ids=list(range(8))).results
    out = np.stack([np.concatenate([res[2 * b]["outT"].T, res[2 * b + 1]["outT"].T], axis=0) for b in range(B)], axis=0)
    return np.ascontiguousarray(out.astype(np.float32))
```

```python
import contextlib
import math
import numpy as np
import ml_dtypes
import concourse.bass as bass
import concourse.mybir as mybir
from concourse.bass_utils import run_bass_kernel_spmd

F32 = mybir.dt.float32
BF16 = mybir.dt.bfloat16
I32 = mybir.dt.int32
AF = mybir.ActivationFunctionType
ALU = mybir.AluOpType
AX = mybir.AxisListType

D = 1024
DFF = 2816
EPS = 1e-6
NEGM = 32768.0

COMPUTE = ("pe", "act", "dve", "pool")
DMAQ = {"q_sp": "sp", "q_pool": "pool", "q_act": "act", "q_cc": "pool"}
QINC = {"q_sp": 16, "q_pool": 16, "q_act": 16, "q_cc": 1}
NSEM_PER_Q = 10


class Buf:
    __slots__ = ("name", "w", "r", "excl")

    def __init__(self, name):
        self.name = name
        self.w = {}
        self.r = {}
        self.excl = False


class Tn:
    def __init__(self, t, name):
        self.t = t
        self.b = Buf(name)

    def __getitem__(self, idx):
        return self.t[idx]


def _norm(lst):
    out = []
    for x in lst:
        if isinstance(x, tuple):
            b, k = x
        else:
            b, k = x, None
        if isinstance(b, Tn):
            b = b.b
        out.append((b, k))
    return out


class Prog:
    def __init__(self, nc):
        self.nc = nc
        self.ops = []

    def add(self, stream, fn, reads=(), writes=()):
        i = len(self.ops)
        deps = {}

        def ck(d, k):
            if k is None:
                return list(d.keys())
            return [kk for kk in (k, None) if kk in d]

        reads = _norm(reads)
        writes = _norm(writes)
        writes = writes + [(b, k) for (b, k) in reads if b.excl and (b, k) not in writes]
        for (b, k) in reads:
            for kk in ck(b.w, k):
                deps[b.w[kk]] = True
        for (b, k) in writes:
            for kk in ck(b.w, k):
                deps.setdefault(b.w[kk], False)
            for kk in ck(b.r, k):
                for s, j in b.r[kk].items():
                    if isinstance(j, list):
                        for jj in j:
                            deps.setdefault(jj, False)
                    else:
                        deps.setdefault(j, False)
        for (b, k) in reads:
            d = b.r.setdefault(k, {})
            if stream in DMAQ:
                d.setdefault(stream, [])
                d[stream].append(i)
            else:
                d[stream] = i
        for (b, k) in writes:
            if k is None:
                b.w = {None: i}
                b.r = {}
            else:
                b.w[k] = i
                b.r.pop(k, None)
        deps.pop(i, None)
        self.ops.append(dict(stream=stream, fn=fn, deps=deps, sig=None))
        return i

    def emit(self, es):
        nc = self.nc
        ops = self.ops
        need = [[] for _ in ops]
        for c, o in enumerate(ops):
            cs = o["stream"]
            best = {}
            for p, raw in o["deps"].items():
                ps = ops[p]["stream"]
                if ps in DMAQ:
                    need[c].append(p)
                    continue
                if ps == cs:
                    if cs == "pe" or not raw:
                        continue
                if ps not in best or best[ps] < p:
                    best[ps] = p
            need[c].extend(best.values())
        signal = [False] * len(ops)
        for c in range(len(ops)):
            for p in need[c]:
                signal[p] = True
        sems = {}
        for s in COMPUTE:
            sems[s] = es.enter_context(nc.semaphore("s_" + s))
        qsems = {}
        for q in DMAQ:
            qsems[q] = [es.enter_context(nc.semaphore("s_%s_%d" % (q, j))) for j in range(NSEM_PER_Q)]
        cnt = {s: 0 for s in COMPUTE}
        qcnt = {q: 0 for q in DMAQ}
        qsemcnt = {q: [0] * NSEM_PER_Q for q in DMAQ}
        for i, o in enumerate(ops):
            s = o["stream"]
            if s in DMAQ:
                j = qcnt[s] % NSEM_PER_Q
                qcnt[s] += 1
                qsemcnt[s][j] += 1
                o["sig"] = (qsems[s][j], QINC[s] * qsemcnt[s][j], (s, j))
                o["prev"] = (qsems[s][j], QINC[s] * (qsemcnt[s][j] - 1), (s, j))
            elif signal[i]:
                cnt[s] += 1
                o["sig"] = (sems[s], cnt[s], s)
        per_eng = {e: [] for e in ("pe", "act", "dve", "pool", "sp")}
        for i, o in enumerate(ops):
            s = o["stream"]
            per_eng[DMAQ.get(s, s)].append(i)
        waited = {e: {} for e in per_eng}

        def run_engine(ename, eng):
            wd = waited[ename]
            for i in per_eng[ename]:
                o = ops[i]
                ws = []
                for p in need[i]:
                    ws.append(ops[p]["sig"])
                if o["stream"] in DMAQ:
                    if o["prev"][1] > 0:
                        ws.append(o["prev"])
                for sem, val, key in ws:
                    if wd.get(key, 0) >= val:
                        continue
                    wd[key] = val
                    eng.wait_ge(sem, val)
                ins = o["fn"](eng)
                if o["sig"] is not None:
                    sem, val, key = o["sig"]
                    ins.then_inc(sem, QINC.get(o["stream"], 1))
            for q, e in DMAQ.items():
                if e != ename:
                    continue
                for j in range(NSEM_PER_Q):
                    v = QINC[q] * qsemcnt[q][j]
                    if v > 0 and wd.get((q, j), 0) < v:
                        eng.wait_ge(qsems[q][j], v)

        with nc.Block() as block:
            @block.tensor
            def _(e):
                run_engine("pe", e)

            @block.scalar
            def _(e):
                run_engine("act", e)

            @block.vector
            def _(e):
                run_engine("dve", e)

            @block.gpsimd
            def _(e):
                run_engine("pool", e)

            @block.sync
            def _(e):
                run_engine("sp", e)


class KB:
    def __init__(self):
        self.nc = bass.Bass("TRN2", target_bir_lowering=False)
        self.es = contextlib.ExitStack()
        self.pr = Prog(self.nc)
        self.off = 16896
        self.maxoff = 0
        self.sfx = ""
        self.io = {}
        self.uid = 0
        self.alloc_log = []
        self.prev_list = []
        self.dram = {}
        self.psums = {}
        self.dummy = self.sb("dummy", [128, 16], F32)
        self.base = self.off

    SHARED = ("posB", "identb", "identN", "cmpbias", "cdiag", "EE", "eaW", "ovl", "cm01")

    def begin_phase(self, sfx):
        self.sfx = sfx
        self.off = self.base
        self.alloc_log = []

    def phase_barrier(self):
        if self.prev_list:
            self.barrier(self.prev_list, list(self.alloc_log))

    def end_phase(self):
        self.prev_list = list(self.alloc_log)

    def sb(self, name, shape, dt):
        nb = 4 if dt in (F32, I32) else 2
        size = nb
        for s_ in shape[1:]:
            size *= s_
        size = (size + 63) // 64 * 64
        off = self.off
        self.off += size
        self.maxoff = max(self.maxoff, self.off)
        assert self.off <= 229376 - 256, ("SBUF overflow", name, self.off)
        self.uid += 1
        name = "%s%s_%d" % (name, self.sfx, self.uid)
        t = Tn(self.nc.alloc_sbuf_tensor_at(name, list(shape), dt, offset=off), name)
        self.alloc_log.append(t)
        return t

    def barrier(self, old, new):
        d = self.dummy
        self.pr.add("dve", lambda e: e.memset(d[:, :], 0.0), list(old), list(new) + list(old))

    def psum(self, name, shape=(128, 512), dt=F32):
        if name in self.psums:
            return self.psums[name]
        t = Tn(self.es.enter_context(self.nc.psum_tensor(name, list(shape), dt)), name)
        t.b.excl = True
        self.psums[name] = t
        return t

    def din(self, name, shape, dt=F32):
        if name in self.io:
            return self.io[name]
        if name not in self.SHARED:
            name = name + self.sfx
        if name in self.dram:
            return self.dram[name]
        t = self.nc.dram_tensor(name, list(shape), dt, kind="ExternalInput")
        self.dram[name] = Tn(t.ap(), name)
        return self.dram[name]

    def dout(self, name, shape, dt=F32):
        if name in self.io:
            return self.io[name]
        t = self.nc.dram_tensor(name + self.sfx, list(shape), dt, kind="ExternalOutput")
        return Tn(t.ap(), name)

    def dscr(self, name, shape, dt):
        t = self.nc.dram_tensor(name + self.sfx, list(shape), dt, kind="Internal")
        return Tn(t.ap(), name)

    def dma(self, q, out, in_, r, w):
        self.pr.add(q, lambda e: e.dma_start(out=out, in_=in_), r, w)

    def mm(self, out, lhsT, rhs, start, stop, r, w):
        self.pr.add("pe", lambda e: e.matmul(out, lhsT, rhs, start=start, stop=stop, skip_group_check=True), r, w)

    def tr(self, out, in_, ident, r, w):
        self.pr.add("pe", lambda e: e.transpose(out, in_, ident), r, w)

    def act(self, out, in_, func, r, w, bias=None, scale=None):
        kw = {}
        if bias is not None:
            kw["bias"] = bias
        if scale is not None:
            kw["scale"] = scale
        self.pr.add("act", lambda e: e.activation(out=out, in_=in_, func=func, **kw), r, w)

    def ts(self, eng, out, in0, s1, s2, op0, op1, r, w, accum_out=None):
        if op1 is None:
            self.pr.add(eng, lambda e: e.tensor_scalar(out=out, in0=in0, scalar1=s1, scalar2=None, op0=op0), r, w)
        elif accum_out is not None:
            self.pr.add(eng, lambda e: e.tensor_scalar(out=out, in0=in0, scalar1=s1, scalar2=s2, op0=op0, op1=op1,
                                                      accum_out=accum_out), r, w)
        else:
            self.pr.add(eng, lambda e: e.tensor_scalar(out=out, in0=in0, scalar1=s1, scalar2=s2, op0=op0, op1=op1), r, w)

    def tt(self, eng, out, in0, in1, op, r, w):
        self.pr.add(eng, lambda e: e.tensor_tensor(out=out, in0=in0, in1=in1, op=op), r, w)

    def stt(self, out, in0, scalar, in1, op0, op1, r, w, accum_out=None):
        if accum_out is None:
            self.pr.add("dve", lambda e: e.scalar_tensor_tensor(out=out, in0=in0, scalar=scalar, in1=in1, op0=op0, op1=op1), r, w)
        else:
            self.pr.add("dve", lambda e: e.scalar_tensor_tensor(out=out, in0=in0, scalar=scalar, in1=in1, op0=op0, op1=op1,
                                                                accum_out=accum_out), r, w)

    def cp(self, eng, out, in_, r, w):
        if eng == "act":
            self.pr.add("act", lambda e: e.copy(out=out, in_=in_), r, w)
        else:
            self.pr.add(eng, lambda e: e.tensor_copy(out=out, in_=in_), r, w)

    def recip(self, out, in_, r, w):
        self.pr.add("dve", lambda e: e.reciprocal(out=out, in_=in_), r, w)

    def memset(self, eng, ap, val, w):
        self.pr.add(eng, lambda e: e.memset(ap, val), (), w)

    def finish(self):
        self.pr.emit(self.es)
        self.es.close()
        return self.nc


def _fin(kb, own):
    kb.end_phase()
    kb.io = {}
    if own:
        return kb.finish()
    return None


def load_w(kb, q, dst, src_ap, kchunks, r=(), extra_w=()):
    v = src_ap.t.rearrange("(c p) n -> p c n", p=128)
    for c in range(kchunks):
        kb.dma(q, dst[:, c, :], v[:, c, :], [src_ap] + list(r), [(dst, c)] + list(extra_w))


def rmsnorm_fm(kb, xg, hT, gcol, ones, epsT, ps_ss, tmpA, rstdB, G=512):
    kb.tt("pool", hT[:, :, :], xg[:, :, :], xg[:, :, :], ALU.mult, [xg], [hT])
    for c in range(8):
        kb.mm(ps_ss[:, :G], ones[:, :], hT[:, c, :], c == 0, c == 7, [ones, hT], [ps_ss])
    kb.act(tmpA[:, :G], ps_ss[:, :G], AF.Sqrt, [ps_ss, epsT], [tmpA], bias=epsT[:, 0:1], scale=1.0 / D)
    kb.recip(rstdB[:, :G], tmpA[:, :G], [tmpA], [rstdB])
    for c in range(8):
        kb.stt(hT[:, c, :], xg[:, c, :], gcol[:, c:c + 1], rstdB[:, :G], ALU.mult, ALU.mult, [xg, gcol, rstdB], [(hT, c)])


def build_c1(T, kb=None, io=None, sfx=""):
    own = kb is None
    if own:
        kb = KB()
    kb.io = io or {}
    kb.begin_phase(sfx)
    G = 512
    NG = T // G
    xT = kb.din("xT", [D, T])
    osrc_fn = kb.io.get("osrc_fn")
    osrc_tn = kb.io.get("osrc_tn")
    o_gath = kb.io.get("o_gath")
    msel_d = kb.din("msel", [128, 2]) if o_gath is not None else None
    if osrc_fn is None and o_gath is None:
        oaT = kb.din("oaT", [512, T], BF16)
        obT = kb.din("obT", [512, T], BF16)
        oaTv = oaT.t.rearrange("(c p) t -> p c t", p=128)
        obTv = obT.t.rearrange("(c p) t -> p c t", p=128)
    gA_d = kb.din("gA", [128, 8])
    wsgu_d = kb.din("w_sgu", [D, 1024])
    gB_d = kb.din("sgu_gB", [128, 512])
    swT_d = kb.din("sgu_wT", [128, 4, 128])
    sbB_d = kb.din("sgu_bB", [128, 4, 128])
    cm_d = kb.din("cm01", [128, 128])
    wbr_d = kb.din("w_br", [1536, D])
    wm_d = kb.din("w_merge", [D, 3072])
    bm_d = kb.din("bm", [128, 24])
    wo_d = kb.din("w_out", [D, D])
    x1T = kb.dout("x1T", [D, T])

    Wsgu = kb.sb("Wsgu", [128, 8, 1024], BF16)
    Wbr = kb.sb("Wbr", [128, 12, 1024], BF16)
    Wm = kb.sb("Wm", [128, 8, 3072], BF16)
    Wo = kb.sb("Wo", [128, 8, 1024], BF16)
    gA = kb.sb("gA_s", [128, 8], F32)
    gB = kb.sb("gB_s", [128, 512], F32)
    swT = kb.sb("swT_s", [128, 4, 128], F32)
    swTb = kb.sb("swTb", [128, 4, 128], BF16)
    sbB = kb.sb("sbB_s", [128, 4, 128], F32)
    cm = kb.sb("cm_s", [128, 128], F32)
    bm = kb.sb("bm_s", [128, 24], F32)
    ones = kb.sb("ones", [128, 128], BF16)
    epsT = kb.sb("epsT", [128, 1], F32)
    eps512 = kb.sb("eps512", [128, 1], F32)

    xg = kb.sb("xg", [128, 8, G], F32)
    hT = kb.sb("hT", [128, 8, G], BF16)
    tmpA = kb.sb("tmpA", [128, G], F32)
    rstdB = kb.sb("rstdB", [128, G], F32)
    uT = kb.sb("uT", [128, 4, G], BF16)
    vg = kb.sb("vg", [128, 512], F32)
    vjunk = kb.sb("vjunk", [128, 512], F32)
    vn = kb.sb("vn", [128, 512], BF16)
    ssv = kb.sb("ssv", [128, 4], F32)
    ocT = kb.sb("ocT", [128, 4, G], BF16)
    stmp = kb.sb("stmp", [128, G], F32)
    oa = kb.sb("oa", [128, 4, G], BF16)
    ob = kb.sb("ob", [128, 4, G], BF16)
    gate = [kb.sb("gate%d" % i, [128, G], F32) for i in range(2)]
    acc = kb.sb("acc", [128, G], F32)
    mixedT = kb.sb("mixedT", [128, 8, G], BF16)
    ost = [kb.sb("ost%d" % i, [128, 2, G], BF16) for i in range(2)]
    msel = kb.sb("msel_s", [128, 2], F32)
    sti = [0]
    PS = [kb.psum("ps%d" % i) for i in range(7)]
    kb.phase_barrier()
    if msel_d is not None:
        kb.dma("q_sp", msel.t[:], msel_d.t, [msel_d], [msel])

    kb.memset("dve", ones[:, :], 1.0, [ones])
    kb.memset("dve", epsT[:, :], EPS, [epsT])
    kb.memset("dve", eps512[:, :], EPS, [eps512])
    for (dst, src) in ((gA, gA_d), (gB, gB_d), (sbB, sbB_d), (cm, cm_d), (bm, bm_d), (swT, swT_d)):
        kb.dma("q_sp", dst.t[:], src.t, [src], [dst])
    for g4 in range(4):
        kb.tt("dve", swTb[:, g4, :], swT[:, g4, :], cm[:, :], ALU.mult, [swT, cm], [(swTb, g4)])
    load_w(kb, "q_pool", Wsgu, wsgu_d, 8)
    load_w(kb, "q_pool", Wm, wm_d, 8)
    load_w(kb, "q_pool", Wbr, wbr_d, 12)
    load_w(kb, "q_pool", Wo, wo_d, 8)

    xTv = xT.t.rearrange("(c p) t -> p c t", p=128)
    x1Tv = x1T.t.rearrange("(c p) t -> p c t", p=128)
    pi = [0]

    def nps():
        p = PS[pi[0] % 3]
        pi[0] += 1
        return p

    for gi in range(NG):
        t0 = gi * G
        for c in range(8):
            kb.dma("q_sp", xg[:, c, :], xTv[:, c, t0:t0 + G], [xT], [(xg, c)])
        for c in range(4):
            if osrc_fn is None and o_gath is None:
                kb.dma("q_sp", oa[:, c, :], oaTv[:, c, t0:t0 + G], [oaT], [(oa, c)])
                kb.dma("q_sp", ob[:, c, :], obTv[:, c, t0:t0 + G], [obT], [(ob, c)])
            elif o_gath is None:
                kb.dma("q_sp", oa[:, c, :], osrc_fn(0, c, t0, G), [osrc_tn], [(oa, c)])
                kb.dma("q_sp", ob[:, c, :], osrc_fn(1, c, t0, G), [osrc_tn], [(ob, c)])
            else:
                for which, dst in ((0, oa), (1, ob)):
                    c4 = 2 * which + c % 2
                    r = c // 2
                    st = ost[sti[0] % 2]
                    sti[0] += 1
                    for half in range(2):
                        kb.dma("q_sp", st[:, half, :], o_gath.t[c4, half, r * 128:(r + 1) * 128, t0:t0 + G], [o_gath], [(st, half)])
                    kb.ts("dve", dst[:, c, :], st[:, 0, :], msel[:, 0:1], None, ALU.mult, None, [st, msel], [(dst, c)])
                    kb.stt(dst[:, c, :], st[:, 1, :], msel[:, 1:2], dst[:, c, :], ALU.mult, ALU.add, [st, msel, (dst, c)], [(dst, c)])
        rmsnorm_fm(kb, xg, hT, gA, ones, epsT, nps(), tmpA, rstdB, G)
        for uc in range(4):
            p = nps()
            for k in range(8):
                kb.mm(p[:, :G], Wsgu[:, k, uc * 128:(uc + 1) * 128], hT[:, k, :], k == 0, k == 7, [(Wsgu, k), hT], [p])
            kb.act(uT[:, uc, :], p[:, :G], AF.Gelu_apprx_tanh, [p], [(uT, uc)])
        ps_s = PS[3:7]
        for tt in range(4):
            p = nps()
            for k in range(8):
                kb.mm(p[:, :512], hT[:, k, tt * 128:(tt + 1) * 128], Wsgu[:, k, 512:1024], k == 0, k == 7, [hT, (Wsgu, k)], [p])
            kb.act(vg[:, :], p[:, :512], AF.Gelu_apprx_tanh, [p], [vg])
            kb.stt(vjunk[:, :], vg[:, :], 1.0, vg[:, :], ALU.mult, ALU.mult, [vg], [vjunk, (ssv, tt)], accum_out=ssv[:, tt:tt + 1])
            kb.act(ssv[:, tt:tt + 1], ssv[:, tt:tt + 1], AF.Sqrt, [(ssv, tt), eps512], [(ssv, tt)], bias=eps512[:, 0:1], scale=1.0 / 512)
            kb.recip(ssv[:, tt:tt + 1], ssv[:, tt:tt + 1], [(ssv, tt)], [(ssv, tt)])
            kb.stt(vn[:, :], vg[:, :], ssv[:, tt:tt + 1], gB[:, :], ALU.mult, ALU.mult, [vg, (ssv, tt), gB], [vn])
            for g4 in range(4):
                kb.mm(ps_s[g4][:, tt * 128:(tt + 1) * 128], vn[:, g4 * 128:(g4 + 1) * 128], swTb[:, g4, :], True, True,
                      [vn, swTb], [(ps_s[g4], tt)])
        for g4 in range(4):
            for tt in range(4):
                kb.tt("dve", stmp[:, tt * 128:(tt + 1) * 128], ps_s[g4][:, tt * 128:(tt + 1) * 128], sbB[:, g4, :], ALU.add,
                      [ps_s[g4], sbB], [(stmp, tt)])
            kb.tt("dve", ocT[:, g4, :], stmp[:, :], uT[:, g4, :], ALU.mult, [stmp, (uT, g4)], [(ocT, g4)])
        osrc = (oa, ob, ocT)
        for oc in range(8):
            for br in range(3):
                pg = nps()
                for k in range(8):
                    kb.mm(pg[:, :G], Wm[:, k, br * 1024 + oc * 128: br * 1024 + (oc + 1) * 128], hT[:, k, :], k == 0, k == 7,
                          [(Wm, k), hT], [pg])
                gt = gate[br % 2]
                kb.act(gt[:, :], pg[:, :G], AF.Sigmoid, [pg, bm], [gt], bias=bm[:, br * 8 + oc: br * 8 + oc + 1])
                pb = nps()
                for k in range(4):
                    kb.mm(pb[:, :G], Wbr[:, br * 4 + k, oc * 128:(oc + 1) * 128], osrc[br][:, k, :], k == 0, k == 3,
                          [(Wbr, br * 4 + k), (osrc[br], k)], [pb])
                if br == 0:
                    kb.tt("dve", acc[:, :], gt[:, :], pb[:, :G], ALU.mult, [gt, pb], [acc])
                else:
                    kb.tt("dve", gt[:, :], gt[:, :], pb[:, :G], ALU.mult, [gt, pb], [gt])
                    if br == 1:
                        kb.tt("dve", acc[:, :], acc[:, :], gt[:, :], ALU.add, [acc, gt], [acc])
                    else:
                        kb.tt("dve", mixedT[:, oc, :], acc[:, :], gt[:, :], ALU.add, [acc, gt], [(mixedT, oc)])
        for oc in range(8):
            p = nps()
            for k in range(8):
                kb.mm(p[:, :G], Wo[:, k, oc * 128:(oc + 1) * 128], mixedT[:, k, :], k == 0, k == 7, [(Wo, k), (mixedT, k)], [p])
            kb.tt("dve", xg[:, oc, :], xg[:, oc, :], p[:, :G], ALU.add, [(xg, oc), p], [(xg, oc)])
            kb.dma("q_sp", x1Tv[:, oc, t0:t0 + G], xg[:, oc, :], [(xg, oc)], [x1T])
    return _fin(kb, own)


def build_c2(T, kb=None, io=None, sfx=""):
    own = kb is None
    if own:
        kb = KB()
    kb.io = io or {}
    kb.begin_phase(sfx)
    G = 512
    NG = T // G
    NF = DFF // 128
    x1T = kb.din("x1T", [D, T])
    gF_d = kb.din("gF", [128, 8])
    gZ_d = kb.din("gZ", [128, 8])
    w1_d = kb.din("w1", [D, DFF])
    w3_d = kb.din("w3", [D, DFF])
    w2_d = kb.din("w2", [DFF, D])
    x2T = kb.dout("x2T", [D, T]) if kb.io.get("want_x2", True) else None
    x2nT = kb.dout("x2nT", [D, T]) if kb.io.get("want_x2n", True) else None

    W1 = kb.sb("W1", [128, 8, DFF], BF16)
    W3 = kb.sb("W3", [128, 8, DFF], BF16)
    W2 = kb.sb("W2", [128, NF, D], BF16)
    gF = kb.sb("gF_s", [128, 8], F32)
    gZ = kb.sb("gZ_s", [128, 8], F32)
    ones = kb.sb("ones", [128, 128], BF16)
    epsT = kb.sb("epsT", [128, 1], F32)
    xg = kb.sb("xg", [128, 8, G], F32)
    hT = kb.sb("hT", [128, 8, G], BF16)
    tmpA = kb.sb("tmpA", [128, G], F32)
    rstdB = kb.sb("rstdB", [128, G], F32)
    aT = kb.sb("aT", [128, NF, G], BF16)
    sl = [kb.sb("sl%d" % i, [128, G], F32) for i in range(2)]
    gN = kb.sb("gN_s", [128, 8], F32)
    PS = [kb.psum("ps%d" % i) for i in range(7)]
    kb.phase_barrier()
    want_x2 = kb.io.get("want_x2", True)
    want_x2n = kb.io.get("want_x2n", True)
    h_next = kb.io.get("h_next")
    if h_next is not None:
        gN_d = kb.din("gN", [128, 8])
        kb.dma("q_sp", gN.t[:], gN_d.t, [gN_d], [gN])

    kb.memset("dve", ones[:, :], 1.0, [ones])
    kb.memset("dve", epsT[:, :], EPS, [epsT])
    kb.dma("q_sp", gF.t[:], gF_d.t, [gF_d], [gF])
    kb.dma("q_sp", gZ.t[:], gZ_d.t, [gZ_d], [gZ])
    load_w(kb, "q_pool", W1, w1_d, 8)
    load_w(kb, "q_pool", W3, w3_d, 8)
    load_w(kb, "q_pool", W2, w2_d, NF)
    x1Tv = x1T.t.rearrange("(c p) t -> p c t", p=128)
    x2Tv = x2T.t.rearrange("(c p) t -> p c t", p=128) if x2T is not None else None
    x2nTv = x2nT.t.rearrange("(c p) t -> p c t", p=128) if x2nT is not None else None
    pi = [0]

    def nps():
        p = PS[pi[0] % 7]
        pi[0] += 1
        return p

    for gi in range(NG):
        t0 = gi * G
        for c in range(8):
            kb.dma("q_sp", xg[:, c, :], x1Tv[:, c, t0:t0 + G], [x1T], [(xg, c)])
        rmsnorm_fm(kb, xg, hT, gF, ones, epsT, nps(), tmpA, rstdB, G)
        for fc in range(NF):
            p1 = nps()
            for k in range(8):
                kb.mm(p1[:, :G], W1[:, k, fc * 128:(fc + 1) * 128], hT[:, k, :], k == 0, k == 7, [(W1, k), hT], [p1])
            p3 = nps()
            for k in range(8):
                kb.mm(p3[:, :G], W3[:, k, fc * 128:(fc + 1) * 128], hT[:, k, :], k == 0, k == 7, [(W3, k), hT], [p3])
            s = sl[fc % 2]
            kb.act(s[:, :], p1[:, :G], AF.Silu, [p1], [s])
            kb.tt("dve", aT[:, fc, :], s[:, :], p3[:, :G], ALU.mult, [s, p3], [(aT, fc)])
        for oc in range(8):
            p = nps()
            for k in range(NF):
                kb.mm(p[:, :G], W2[:, k, oc * 128:(oc + 1) * 128], aT[:, k, :], k == 0, k == NF - 1, [(W2, k), (aT, k)], [p])
            kb.tt("dve", xg[:, oc, :], xg[:, oc, :], p[:, :G], ALU.add, [(xg, oc), p], [(xg, oc)])
            if want_x2:
                kb.dma("q_sp", x2Tv[:, oc, t0:t0 + G], xg[:, oc, :], [(xg, oc)], [x2T])
        if h_next is not None:
            rmsnorm_fm(kb, xg, hT, gN, ones, epsT, nps(), tmpA, rstdB, G)
            for c in range(8):
                kb.dma("q_sp", h_next.t[c, :, t0:t0 + G], hT[:, c, :], [(hT, c)], [(h_next, c)])
        if want_x2n:
            rmsnorm_fm_f32(kb, xg, hT, gZ, ones, epsT, nps(), tmpA, rstdB, G)
            for oc in range(8):
                kb.dma("q_sp", x2nTv[:, oc, t0:t0 + G], xg[:, oc, :], [(xg, oc)], [x2nT])
    return _fin(kb, own)


def rmsnorm_fm_f32(kb, xg, hT, gcol, ones, epsT, ps_ss, tmpA, rstdB, G=512):
    kb.tt("pool", hT[:, :, :], xg[:, :, :], xg[:, :, :], ALU.mult, [xg], [hT])
    for c in range(8):
        kb.mm(ps_ss[:, :G], ones[:, :], hT[:, c, :], c == 0, c == 7, [ones, hT], [ps_ss])
    kb.act(tmpA[:, :G], ps_ss[:, :G], AF.Sqrt, [ps_ss, epsT], [tmpA], bias=epsT[:, 0:1], scale=1.0 / D)
    kb.recip(rstdB[:, :G], tmpA[:, :G], [tmpA], [rstdB])
    for c in range(8):
        kb.stt(xg[:, c, :], xg[:, c, :], gcol[:, c:c + 1], rstdB[:, :G], ALU.mult, ALU.mult, [(xg, c), gcol, rstdB], [(xg, c)])


def _col8(v):
    return np.ascontiguousarray(v.reshape(8, 128).T)


def prep_c1(l, inp):
    w_in = inp["w_in"][l]
    d = {}
    d["gA"] = _col8(inp["attn_norm"][l])
    d["w_sgu"] = np.ascontiguousarray(w_in[:, 2840:3864])
    d["sgu_gB"] = np.ascontiguousarray(np.broadcast_to(inp["sgu_norm"][l][None, :], (128, 512)))
    d["sgu_wT"] = np.ascontiguousarray(inp["sgu_w"][l].transpose(2, 0, 1))
    d["sgu_bB"] = np.ascontiguousarray(np.broadcast_to(inp["sgu_b"][l][None], (128, 4, 128)))
    d["cm01"] = np.triu(np.ones((128, 128), np.float32))
    d["w_br"] = np.ascontiguousarray(np.concatenate([inp["w_branch_a"][l], inp["w_branch_b"][l], inp["w_branch_c"][l]], 0))
    d["w_merge"] = np.ascontiguousarray(inp["w_merge"][l])
    d["bm"] = np.ascontiguousarray(inp["b_merge"][l].reshape(24, 128).T)
    d["w_out"] = np.ascontiguousarray(inp["w_out"][l])
    return d


def prep_c2(l, inp):
    d = {}
    d["gF"] = _col8(inp["ffn_norm"][l])
    d["gZ"] = _col8(inp["final_norm"])
    d["w1"] = np.ascontiguousarray(inp["w_ffn1"][l])
    d["w3"] = np.ascontiguousarray(inp["w_ffn3"][l])
    d["w2"] = np.ascontiguousarray(inp["w_ffn2"][l])
    return d


NFM = 17
FM_ROPE = {0: 9, 1: 10, 3: 11, 4: 12, 5: 13, 6: 14, 7: 15, 8: 16}
TWO_PI = 2.0 * math.pi
C1 = 6.28125
C2 = TWO_PI - C1
BIGS = 1.0e9


def build_ab(S, stop=None, kb=None, io=None, sfx=""):
    own = kb is None
    if own:
        kb = KB()
    kb.io = io or {}
    kb.begin_phase(sfx)
    PG = 256
    NPG = S // PG
    QG = 512
    NQ = S // QG
    NT = S // 128
    NCP = S // 16
    ncmp = NCP - 1
    NCT = (NCP + 127) // 128
    NCW = NCT * 128

    h_src = kb.io.get("h_src")
    o_piece = kb.io.get("o_piece")
    xT = kb.din("xT", [D, S]) if h_src is None else None
    posB_d = kb.din("posB", [128, S], I32)
    gA_d = kb.din("gA", [128, 8])
    wfm_d = kb.din("w_fm", [D, NFM * 128])
    wtm_d = kb.din("w_tm", [D, 396])
    identb_d = kb.din("identb", [128, 128], BF16)
    identN_d = kb.din("identN", [128, 128], BF16)
    colc_d = kb.din("colc", [128, 4])
    cmpbias_d = kb.din("cmpbias", [128, 5, 512], BF16)
    cdiag_d = kb.din("cdiag", [128, 2, 128], BF16)
    EE_d = kb.din("EE", [128, NT, 128], BF16)
    eaW_d = kb.din("eaW", [128, 2, 254])
    ovl_d = kb.din("ovl", [128, NCT, 128], BF16)
    w1kv_d = kb.din("w1kv", [128, 32, 256])
    posT_d = kb.din("posT", [128, 32])
    w2k_d = kb.din("w2k", [128, 2, 128])
    w2v_d = kb.din("w2v", [128, 2, 64])
    lqk_d = kb.din("lqk", [128, 4, 64])
    sublnB_d = kb.din("sublnB", [128, 128])
    oT = kb.dout("oT", [512, S], BF16) if o_piece is None else None
    qscr = kb.dscr("qscr", [7, 128, S], BF16)

    ident = kb.sb("ident", [128, 128], BF16)
    identN = kb.sb("identN", [128, 128], BF16)
    ones = kb.sb("ones", [128, 128], BF16)
    gA = kb.sb("gA_s", [128, 8], F32)
    epsT = kb.sb("epsT", [128, 1], F32)
    eps128 = kb.sb("eps128", [128, 1], F32)
    colc = kb.sb("colc_s", [128, 4], F32)
    cmpbias = kb.sb("cmpbias_s", [128, 5, 512], BF16)
    cdiag = kb.sb("cdiag_s", [128, 2, 128], BF16)
    eaW = kb.sb("eaW_s", [128, 2, 254], F32)
    sublnB = kb.sb("sublnB_s", [128, 128], F32)
    lqk = kb.sb("lqk_s", [128, 4, 64], F32)
    lsc = kb.sb("lsc", [128, 8], F32)
    kslc = kb.sb("kslc", [128, S], BF16)
    kb0 = kb.sb("kb0", [128, S], BF16)
    kb1 = kb.sb("kb1", [128, S], BF16)
    Vnsa = kb.sb("Vnsa", [128, NT, 130], BF16)
    Vd = kb.sb("Vd", [128, NT, 258], BF16)
    gates = kb.sb("gates", [128, NT, 12], F32)
    kcT = kb.sb("kcT", [128, NCW], BF16)
    Vc = kb.sb("Vc", [128, NCT, 193], BF16)
    kvT1 = kb.sb("kvT1", [128, S], BF16)
    EE = Tn(kb.nc.alloc_sbuf_tensor_at("EE_s" + kb.sfx, [128, NT, 128], BF16, offset=kb.off - 2 * S), "EE_s")
    PS = [kb.psum("ps%d" % i) for i in range(7)]
    PSB = kb.psum("psb", [128, 1024], BF16)
    mark = kb.off

    Wfm = kb.sb("Wfm", [128, 8, NFM * 128], BF16)
    Wtm = kb.sb("Wtm", [128, 8, 396], BF16)
    xg = kb.sb("xg", [128, 8, PG], F32)
    hT = kb.sb("hT", [128, 8, PG], BF16)
    tmpA = kb.sb("tmpA", [128, PG], F32)
    rstdB = kb.sb("rstdB", [128, PG], F32)
    posi = kb.sb("posi", [128, PG], I32)
    ang = kb.sb("ang", [128, PG], F32)
    ra = kb.sb("ra", [128, PG], F32)
    rk = kb.sb("rk", [128, PG], F32)
    rki = kb.sb("rki", [128, PG], I32)
    rfix = kb.sb("rfix", [128, PG], F32)
    cosT = kb.sb("cosT", [128, PG], F32)
    sinT = kb.sb("sinT", [128, PG], F32)
    t1 = kb.sb("t1", [128, PG], F32)
    t2 = kb.sb("t2", [128, PG], F32)
    qst = [kb.sb("qstP%d" % i, [128, 7, PG], BF16) for i in range(2)]
    P_list = [Wfm, Wtm, xg, hT, tmpA, rstdB, posi, ang, ra, rk, rki, rfix, cosT, sinT, t1, t2] + qst
    kb.alloc_log.append(EE)
    kb.phase_barrier()

    kb.memset("dve", ones[:, :], 1.0, [ones])
    kb.memset("dve", epsT[:, :], EPS, [epsT])
    kb.memset("dve", eps128[:, :], EPS, [eps128])
    kb.memset("pool", Vnsa[:, :, :], 1.0, [Vnsa])
    kb.memset("pool", Vd[:, :, :], 1.0, [Vd])
    for (dst, src) in ((ident, identb_d), (identN, identN_d), (gA, gA_d), (colc, colc_d), (cmpbias, cmpbias_d),
                       (cdiag, cdiag_d), (eaW, eaW_d), (sublnB, sublnB_d), (lqk, lqk_d)):
        kb.dma("q_sp", dst.t[:], src.t, [src], [dst])
    load_w(kb, "q_pool", Wfm, wfm_d, 8)
    load_w(kb, "q_pool", Wtm, wtm_d, 8)
    invf = colc[:, 0:1]
    sgn = colc[:, 1:2]
    kb.tt("dve", lqk[:, 0, :], lqk[:, 0, :], lqk[:, 1, :], ALU.mult, [lqk], [lqk])
    kb.tt("dve", lqk[:, 2, :], lqk[:, 2, :], lqk[:, 3, :], ALU.mult, [lqk], [lqk])
    kb.pr.add("dve", lambda e: e.reduce_sum(out=lsc[:, 0:1], in_=lqk[:, 0, :], axis=AX.X), [lqk], [lsc])
    kb.pr.add("dve", lambda e: e.reduce_sum(out=lsc[:, 1:2], in_=lqk[:, 2, :], axis=AX.X), [lqk], [lsc])
    kb.act(lsc[:, 2:4], lsc[:, 0:2], AF.Exp, [lsc], [lsc])
    kb.tt("dve", lsc[:, 4:5], lsc[:, 3:4], lsc[:, 2:3], ALU.subtract, [lsc], [lsc])
    kb.tt("dve", lsc[:, 4:5], lsc[:, 4:5], colc[:, 2:3], ALU.subtract, [lsc, colc], [lsc])
    neglam = lsc[:, 4:5]
    kb.ts("dve", sublnB[:, :], sublnB[:, :], colc[:, 3:4], None, ALU.mult, None, [sublnB, colc], [sublnB])

    if stop == 'C':
        return _fin(kb, own)
    xTv = xT.t.rearrange("(c p) t -> p c t", p=128) if xT is not None else None
    pi = [0]

    def nps():
        p = PS[pi[0] % 7]
        pi[0] += 1
        return p

    for gi in range(NPG):
        t0 = gi * PG
        if h_src is None:
            for c in range(8):
                kb.dma("q_sp", xg[:, c, :], xTv[:, c, t0:t0 + PG], [xT], [(xg, c)])
        kb.dma("q_sp", posi[:, :], posB_d[:, t0:t0 + PG], [posB_d], [posi])
        if h_src is None:
            rmsnorm_fm(kb, xg, hT, gA, ones, epsT, nps(), tmpA, rstdB, PG)
        else:
            Th = S // 2
            rr, col = t0 // Th, t0 % Th
            for c in range(8):
                kb.dma("q_sp", hT[:, c, :], h_src.t[c, rr * 128:(rr + 1) * 128, col:col + PG], [h_src], [(hT, c)])
        if stop == 'P1':
            return _fin(kb, own)
        kb.cp("dve", ang[:, :], posi[:, :], [posi], [ang])
        kb.ts("dve", ang[:, :], ang[:, :], invf, None, ALU.mult, None, [ang, colc], [ang])
        for which in range(2):
            dst = sinT if which == 0 else cosT
            if which == 0:
                src = ang
            else:
                kb.ts("dve", ra[:, :], ang[:, :], math.pi / 2, None, ALU.add, None, [ang], [ra])
                src = ra
            kb.ts("dve", rk[:, :], src[:, :], 1.0 / TWO_PI, None, ALU.mult, None, [src], [rk])
            kb.cp("dve", rki[:, :], rk[:, :], [rk], [rki])
            kb.cp("dve", rk[:, :], rki[:, :], [rki], [rk])
            kb.pr.add("dve", lambda e, src=src: e.scalar_tensor_tensor(out=rfix[:, :], in0=rk[:, :], scalar=-C1, in1=src[:, :],
                                                                       op0=ALU.mult, op1=ALU.add), [rk, src], [rfix])
            kb.pr.add("dve", lambda e: e.scalar_tensor_tensor(out=rfix[:, :], in0=rk[:, :], scalar=-C2, in1=rfix[:, :],
                                                              op0=ALU.mult, op1=ALU.add), [rk, rfix], [rfix])
            kb.ts("dve", rk[:, :], rfix[:, :], math.pi, -TWO_PI, ALU.is_gt, ALU.mult, [rfix], [rk])
            kb.tt("dve", rfix[:, :], rfix[:, :], rk[:, :], ALU.add, [rfix, rk], [rfix])
            kb.ts("dve", rk[:, :], rfix[:, :], -math.pi, TWO_PI, ALU.is_lt, ALU.mult, [rfix], [rk])
            kb.tt("dve", rfix[:, :], rfix[:, :], rk[:, :], ALU.add, [rfix, rk], [rfix])
            kb.ts("dve", rfix[:, :], rfix[:, :], math.pi, -math.pi, ALU.min, ALU.max, [rfix], [rfix])
            if which == 0:
                kb.act(dst[:, :], rfix[:, :], AF.Sin, [rfix, colc], [dst], scale=sgn)
            else:
                kb.act(dst[:, :], rfix[:, :], AF.Sin, [rfix], [dst])
        if stop == 'P2':
            return _fin(kb, own)
        qs = qst[gi % 2]
        for ch in range(9):
            pp = nps()
            for k in range(8):
                kb.mm(pp[:, :PG], Wfm[:, k, ch * 128:(ch + 1) * 128], hT[:, k, :], k == 0, k == 7, [(Wfm, k), hT], [pp])
            if ch in (0, 1):
                kb.cp("act", qs[:, ch, :], pp[:, :PG], [pp], [(qs, ch)])
            if ch == 2:
                kb.cp("act", kvT1[:, t0:t0 + PG], pp[:, :PG], [pp], [(kvT1, gi)])
                continue
            sw = FM_ROPE[ch]
            psw = nps()
            for k in range(8):
                kb.mm(psw[:, :PG], Wfm[:, k, sw * 128:(sw + 1) * 128], hT[:, k, :], k == 0, k == 7, [(Wfm, k), hT], [psw])
            kb.tt("dve", t1[:, :], pp[:, :PG], cosT[:, :], ALU.mult, [pp, cosT], [t1])
            kb.tt("dve", t2[:, :], psw[:, :PG], sinT[:, :], ALU.mult, [psw, sinT], [t2])
            if ch in (0, 1):
                dst, dk, dt_ = qs[:, 2 + ch, :], (qs, 2 + ch), qs
            elif ch == 3:
                dst, dk, dt_ = kslc[:, t0:t0 + PG], (kslc, gi), kslc
            elif ch == 4:
                dst, dk, dt_ = qs[:, 6, :], (qs, 6), qs
            elif ch in (5, 6):
                dst, dk, dt_ = qs[:, ch - 1, :], (qs, ch - 1), qs
            elif ch == 7:
                dst, dk, dt_ = kb0[:, t0:t0 + PG], (kb0, gi), kb0
            else:
                dst, dk, dt_ = kb1[:, t0:t0 + PG], (kb1, gi), kb1
            kb.tt("pool", dst, t1[:, :], t2[:, :], ALU.add, [t1, t2], [dk])
        if stop == 'P3':
            return _fin(kb, own)
        for j in range(7):
            kb.dma("q_sp", qscr[j, :, t0:t0 + PG], qs[:, j, :], [(qs, j)], [qscr])
        if stop == 'P4':
            return _fin(kb, own)
        for tt in range(PG // 128):
            T_ = gi * (PG // 128) + tt
            p = nps()
            for k in range(8):
                kb.mm(p[:, :396], hT[:, k, tt * 128:(tt + 1) * 128], Wtm[:, k, :], k == 0, k == 7, [hT, (Wtm, k)], [p])
            kb.cp("act", Vnsa[:, T_, 0:64], p[:, 0:64], [p], [(Vnsa, T_)])
            kb.cp("act", Vnsa[:, T_, 65:129], p[:, 64:128], [p], [(Vnsa, T_)])
            kb.act(gates[:, T_, :], p[:, 128:140], AF.Sigmoid, [p], [(gates, T_)])
            kb.cp("dve", Vd[:, T_, 0:128], p[:, 140:268], [p], [(Vd, T_)])
            kb.cp("dve", Vd[:, T_, 129:257], p[:, 268:396], [p], [(Vd, T_)])
        if stop == 'P5' or (stop == 'P6' and gi == 1):
            return _fin(kb, own)

    if stop == 'P':
        return _fin(kb, own)
    kb.off = mark
    w1kv = kb.sb("w1kv", [128, 32, 256], BF16)
    posT = kb.sb("posT", [128, 32], BF16)
    w2k = kb.sb("w2k", [128, 2, 128], BF16)
    w2v = kb.sb("w2v", [128, 2, 64], BF16)
    hidT = kb.sb("hidT", [128, 2, 2, NCW], BF16)
    posb = kb.sb("posb", [128, 4], F32)
    X_list = [w1kv, posT, w2k, w2v, hidT, posb]
    kb.barrier(P_list, X_list)
    for l4 in range(4):
        kb.dma("q_pool", w1kv[:, l4 * 8:(l4 + 1) * 8, :], w1kv_d[:, l4 * 8:(l4 + 1) * 8, :], [w1kv_d], [(w1kv, l4)])
    kb.dma("q_pool", posT[:, :], posT_d.t, [posT_d], [posT])
    kb.dma("q_pool", w2k[:, :, :], w2k_d.t, [w2k_d], [w2k])
    kb.dma("q_pool", w2v[:, :, :], w2v_d.t, [w2v_d], [w2v])
    kb.memset("pool", hidT[:, :, :, :], 0.0, [hidT])
    kvv = kvT1.t.reshape([128, NCP, 16])
    for which in range(2):
        r0 = 64 * which
        for half in range(2):
            ph = nps()
            for l in range(32):
                kb.mm(ph[:, :ncmp], w1kv[r0:r0 + 64, l, half * 128:(half + 1) * 128],
                      kvv[r0:r0 + 64, (l // 16):(l // 16) + ncmp, l % 16], l == 0, l == 31, [w1kv, kvT1], [ph])
            pb = nps()
            for l in range(32):
                kb.mm(pb[:, 0:1], w1kv[r0:r0 + 64, l, half * 128:(half + 1) * 128], posT[r0:r0 + 64, l:l + 1], l == 0, l == 31,
                      [w1kv, posT], [pb])
            idx = which * 2 + half
            kb.cp("dve", posb[:, idx:idx + 1], pb[:, 0:1], [pb], [(posb, idx)])
            kb.act(hidT[:, which, half, :ncmp], ph[:, :ncmp], AF.Gelu_apprx_tanh, [ph, (posb, idx)], [(hidT, idx)],
                   bias=posb[:, idx:idx + 1])
    pk = nps()
    for half in range(2):
        kb.mm(pk[:, :NCW], w2k[:, half, :], hidT[:, 0, half, :], half == 0, half == 1, [w2k, hidT], [pk])
    kb.cp("act", kcT[:, :], pk[:, :NCW], [pk], [kcT])
    kb.memset("pool", Vc[:, :, :], 1.0, [Vc])
    for nt in range(NCT):
        pv = nps()
        for half in range(2):
            kb.mm(pv[:, 0:64], hidT[:, 1, half, nt * 128:(nt + 1) * 128], w2v[:, half, :], half == 0, half == 1, [hidT, w2v], [pv])
        kb.cp("dve", Vc[:, nt, 0:64], pv[:, 0:64], [pv], [Vc])
    kb.dma("q_sp", Vc[:, :, 65:193], ovl_d.t, [ovl_d], [Vc])

    if stop == 'X':
        return _fin(kb, own)
    kb.off = mark
    qstA = [kb.sb("qstA%d" % i, [128, 6, QG], BF16) for i in range(2)]
    qzA = [kb.sb("qzA%d" % i, [128, 12, QG], BF16) for i in range(1)]
    kwst = [kb.sb("kwst%d" % i, [128, 1024], BF16) for i in range(2)]
    PT = [kb.sb("PT%d" % i, [128, 512], BF16) for i in range(3)]
    negT = kb.sb("negT", [128, 512], BF16)
    Oev = [kb.sb("Oev%d" % i, [128, 4, 193], F32) for i in range(2)]
    rc = [kb.sb("rc%d" % i, [128, 4], F32) for i in range(2)]
    coef = [kb.sb("coef%d" % i, [128, 4], F32) for i in range(2)]
    ocmp = kb.sb("ocmp", [128, 4, 4, 64], F32)
    oacc = kb.sb("oacc", [128, 4, 256], F32)
    obt = kb.sb("obt", [128, 4, 256], F32)
    imp = kb.sb("imp", [128, 4, 128], F32)
    score = kb.sb("score", [128, 4, 128], F32)
    sc2 = kb.sb("sc2", [128, 4, 128], F32)
    m8 = kb.sb("m8", [128, 4, 8], F32)
    neg01 = kb.sb("neg01", [128, 4, 128], BF16)
    od0 = kb.sb("od0", [128, 4, 128], F32)
    od1 = kb.sb("od1", [128, 4, 128], F32)
    djunk = kb.sb("djunk", [128, 128], F32)
    dss = kb.sb("dss", [128, 4], F32)
    o16 = kb.sb("o16", [128, 4, 512], BF16)
    oTs = kb.sb("oTs", [128, 4, 512], BF16)
    A_list = qzA + qstA + kwst + PT + [negT, ocmp, oacc, obt, imp, score, sc2, m8, neg01, od0, od1, djunk, dss, o16, oTs] + Oev + rc + coef
    kb.barrier(X_list + P_list + [kvT1], A_list + [EE])
    kb.dma("q_sp", EE.t[:], EE_d.t, [EE_d], [EE])
    for qzb in qzA:
        kb.memset("pool", qzb[:, :, :], 0.0, [qzb])
    LB = [PS[0], PS[1]]
    ACC = [(PS[2], PS[3]), (PS[4], PS[5])]
    PSM = PS[6]
    li = [0]
    ai = [0]
    pti = [0]
    ei = [0]
    oTv = oT.t.rearrange("(c p) t -> p c t", p=128) if oT is not None else None

    def nL():
        li[0] += 1
        return LB[li[0] % 2]

    def nA():
        ai[0] += 1
        return ACC[ai[0] % 2]

    def nPT():
        pti[0] += 1
        return PT[pti[0] % 3]

    def nE():
        ei[0] += 1
        return Oev[ei[0] % 2], rc[ei[0] % 2], coef[ei[0] % 2]

    def evac(acc, w, nbank_q):
        ev, r_, cf = nE()
        nb = 4 // nbank_q
        for bnk in range(nb):
            kb.cp("dve", ev[:, bnk * nbank_q:(bnk + 1) * nbank_q, 0:w],
                  acc[bnk][:, 0:nbank_q * w].rearrange("p (q w) -> p q w", w=w), [acc[bnk]], [(ev, bnk)])
        sumcol = 64 if w in (65, 193) else 128
        kb.ts("dve", r_[:, :], ev[:, :, sumcol], 1e-30, None, ALU.max, None, [ev], [r_])
        kb.recip(r_[:, :], r_[:, :], [r_], [r_])
        return ev, r_, cf

    for Q in range(NQ):
        q0 = Q * QG
        qs = qstA[Q % 2]
        kw = kwst[Q % 2]
        for j in range(6):
            kb.dma("q_sp", qs[:, j, :], qscr[j, :, q0:q0 + QG], [qscr], [(qs, j)])
        qz = qzA[0]
        for j in range(6):
            for hf in range(2):
                kb.cp("pool", qz[64 * hf:64 * hf + 64, 2 * j + hf, :], qs[64 * hf:64 * hf + 64, j, :], [(qs, j)], [(qz, 2 * j + hf)])
        klo = max(0, q0 - 512)
        kb.dma("q_sp", kw[:, (klo - (q0 - 512)):1024], qscr[6, :, klo:q0 + 512], [qscr], [kw])
        for h in range(4):
            r0 = 64 * (h % 2)
            qa = qz[:, 2 * (h // 2) + h % 2, :]
            nts = [nt for nt in range(NCT) if Q - 4 * nt >= 0]
            acc = nA()
            def c_qk(ix, nt):
                Dd = Q - 4 * nt
                L = nL()
                kb.mm(L[:, :512], kcT[:, nt * 128:(nt + 1) * 128], qa, True, Dd > 4, [kcT, qz], [L])
                if Dd <= 4:
                    kb.mm(L[:, :512], identN[:, :], cmpbias[:, Dd, :], False, True, [identN, cmpbias], [L])
                pt = nPT()
                kb.act(pt[:, :], L[:, :512], AF.Exp, [L], [pt], scale=0.125)
                return pt

            def c_pv(ix, nt, pt):
                for qt in range(4):
                    kb.mm(acc[qt // 2][:, (qt % 2) * 193:(qt % 2) * 193 + 193], pt[:, qt * 128:(qt + 1) * 128], Vc[:, nt, :],
                          ix == 0 and qt % 2 == 0, ix == len(nts) - 1, [pt, Vc], [acc[qt // 2]])

            prev = None
            for ix, nt in enumerate(nts):
                pt = c_qk(ix, nt)
                if prev is not None:
                    c_pv(*prev)
                prev = (ix, nt, pt)
            c_pv(*prev)
            ev, r_, cf = evac(acc, 193, 2)
            for qt in range(4):
                kb.ts("dve", ocmp[:, h, qt, :], ev[:, qt, 0:64], r_[:, qt:qt + 1], None, ALU.mult, None, [ev, r_], [(ocmp, h)])
                if h == 0:
                    kb.ts("dve", imp[:, qt, :], ev[:, qt, 65:193], r_[:, qt:qt + 1], None, ALU.mult, None, [ev, r_], [imp])
                else:
                    kb.stt(imp[:, qt, :], ev[:, qt, 65:193], r_[:, qt:qt + 1], imp[:, qt, :], ALU.mult, ALU.add, [ev, r_, imp], [imp])
        for qt in range(4):
            qta = 4 * Q + qt
            off = 126 - 2 * qta
            kb.tt("dve", score[:, qt, :], imp[:, qt, :], eaW[:, 0, off:off + 128], ALU.mult, [imp, eaW], [score])
            kb.tt("dve", score[:, qt, :], score[:, qt, :], eaW[:, 1, off:off + 128], ALU.add, [score, eaW], [score])
            kb.memset("dve", score[:, qt, 0:1], BIGS, [score])
            kb.pr.add("dve", lambda e, qt=qt: e.max(out=m8[:, qt, :], in_=score[:, qt, :]), [score], [m8])
            kb.pr.add("dve", lambda e, qt=qt: e.match_replace(out=sc2[:, qt, :], in_to_replace=m8[:, qt, :],
                                                              in_values=score[:, qt, :], imm_value=-3.0e38), [score, m8], [sc2])
            kb.pr.add("dve", lambda e, qt=qt: e.max(out=m8[:, qt, :], in_=sc2[:, qt, :]), [sc2], [m8])
            kb.ts("dve", neg01[:, qt, :], score[:, qt, :], m8[:, qt, 7:8], 1.0, ALU.is_ge, ALU.subtract, [score, m8], [neg01])
            kb.tr(PSB[:, qt * 128:(qt + 1) * 128], neg01[:, qt, :], ident[:, :], [neg01, ident], [PSB])
        kb.cp("dve", negT[:, :], PSB[:, 0:512], [PSB], [negT])
        for h in range(4):
            r0 = 64 * (h % 2)
            qr = qz[:, 2 * (2 + h // 2) + h % 2, :]
            acc = nA()
            firstb = True
            ilist = [i for i in range(8) if 4 * Q - 4 + i >= 0]
            def w_qk(i):
                qts = [qt for qt in range(4) if 0 <= 4 - i + qt <= 4]
                c0, c1 = qts[0] * 128, (qts[-1] + 1) * 128
                L = nL()
                kb.mm(L[:, c0:c1], kw[:, i * 128:(i + 1) * 128], qr[:, c0:c1], True, False, [kw, qz], [L])
                for qt in qts:
                    dd = 4 - i + qt
                    if dd == 0:
                        kb.mm(L[:, qt * 128:(qt + 1) * 128], identN[:, :], cdiag[:, 0, :], False, True, [identN, cdiag], [L])
                    elif dd == 4:
                        kb.mm(L[:, qt * 128:(qt + 1) * 128], identN[:, :], cdiag[:, 1, :], False, True, [identN, cdiag], [L])
                pt = nPT()
                kb.act(pt[:, c0:c1], L[:, c0:c1], AF.Exp, [L], [pt], scale=0.125)
                return qts, pt

            fb = [True]

            def w_pv(i, qts, pt):
                kt = 4 * Q - 4 + i
                for qt in qts:
                    kb.mm(acc[0][:, qt * 65:qt * 65 + 65], pt[:, qt * 128:(qt + 1) * 128], Vnsa[:, kt, 65:130],
                          fb[0], i == qt + 4, [pt, Vnsa], [acc[0]])
                    fb[0] = False

            prev = None
            for i in ilist:
                qts, pt = w_qk(i)
                if prev is not None:
                    w_pv(*prev)
                prev = (i, qts, pt)
            w_pv(*prev)
            ev, r_, cf = evac(acc, 65, 4)
            for qt in range(4):
                qta = 4 * Q + qt
                kb.tt("dve", cf[:, qt:qt + 1], r_[:, qt:qt + 1], gates[:, qta, h * 3 + 2:h * 3 + 3], ALU.mult, [r_, gates], [cf])
                kb.ts("dve", oacc[:, qt, h * 64:(h + 1) * 64], ocmp[:, h, qt, :], gates[:, qta, h * 3:h * 3 + 1], None, ALU.mult, None,
                      [(ocmp, h), gates], [(oacc, h)])
                kb.stt(oacc[:, qt, h * 64:(h + 1) * 64], ev[:, qt, 0:64], cf[:, qt:qt + 1], oacc[:, qt, h * 64:(h + 1) * 64],
                       ALU.mult, ALU.add, [ev, cf, (oacc, h)], [(oacc, h)])

        def causal_unit(qap, kT, vfn, w, nbq, use_sel, qbuf):
            acc = nA()
            nkt = 4 * Q + 4
            ktn, vtn = kT_tn[0], vT_tn[0]

            def u_qk(kt):
                i = kt - 4 * Q
                c0 = max(i, 0) * 128
                L = nL()
                kb.mm(L[:, c0:512], kT[:, kt * 128:(kt + 1) * 128], qap[:, c0:512], True, False, [ktn, qbuf], [L])
                if use_sel:
                    kb.mm(L[:, c0:512], EE[:, kt, :], negT[:, c0:512], False, False, [EE, negT], [L])
                if i >= 0:
                    kb.mm(L[:, c0:c0 + 128], identN[:, :], cdiag[:, 0, :], False, True, [identN, cdiag], [L])
                pt = nPT()
                kb.act(pt[:, c0:512], L[:, c0:512], AF.Exp, [L], [pt], scale=0.125)
                return pt

            def u_pv(kt, pt):
                i = kt - 4 * Q
                for qt in range(max(i, 0), 4):
                    bnk, sl = qt // nbq, (qt % nbq) * w
                    kb.mm(acc[bnk][:, sl:sl + w], pt[:, qt * 128:(qt + 1) * 128], vfn(kt),
                          kt == 0 and qt % nbq == 0, kt == 4 * Q + qt, [pt, vtn], [acc[bnk]])

            prev = None
            for kt in range(nkt):
                pt = u_qk(kt)
                if prev is not None:
                    u_pv(*prev)
                prev = (kt, pt)
            u_pv(*prev)
            return evac(acc, w, nbq)

        kT_tn = [None]
        vT_tn = [None]
        for hh in range(2):
            for m in range(2):
                mi = hh * 2 + m
                kT_tn[0] = kb0 if mi < 2 else kb1
                vT_tn[0] = Vd
                r0 = 64 * (mi % 2)
                qap = qz[:, 2 * (4 + mi // 2) + mi % 2, :]
                ev, r_, cf = causal_unit(qap, kT_tn[0][:, :], lambda kt, hh=hh: Vd[:, kt, hh * 129:hh * 129 + 129], 129, 2, False, qz)
                if m == 0:
                    for qt in range(4):
                        kb.ts("dve", od0[:, qt, :], ev[:, qt, 0:128], r_[:, qt:qt + 1], None, ALU.mult, None, [ev, r_], [od0])
                else:
                    for qt in range(4):
                        kb.ts("dve", od1[:, qt, :], ev[:, qt, 0:128], r_[:, qt:qt + 1], neglam, ALU.mult, ALU.mult, [ev, r_, lsc], [od1])
                        kb.tt("dve", od0[:, qt, :], od0[:, qt, :], od1[:, qt, :], ALU.add, [od0, od1], [od0])
                        kb.stt(djunk[:, :], od0[:, qt, :], 1.0, od0[:, qt, :], ALU.mult, ALU.mult, [od0], [djunk, dss],
                               accum_out=dss[:, qt:qt + 1])
                    kb.act(dss[:, :], dss[:, :], AF.Sqrt, [dss, eps128], [dss], bias=eps128[:, 0:1], scale=1.0 / 128)
                    kb.recip(dss[:, :], dss[:, :], [dss], [dss])
                    for qt in range(4):
                        kb.stt(obt[:, qt, hh * 128:(hh + 1) * 128], od0[:, qt, :], dss[:, qt:qt + 1], sublnB[:, :], ALU.mult, ALU.mult,
                               [od0, dss, sublnB], [(obt, hh)])
        for h in range(4):
            r0 = 64 * (h % 2)
            kT_tn[0] = kslc
            vT_tn[0] = Vnsa
            qap = qz[:, 2 * (2 + h // 2) + h % 2, :]
            ev, r_, cf = causal_unit(qap, kslc[:, :], lambda kt: Vnsa[:, kt, 0:65], 65, 4, True, qz)
            for qt in range(4):
                qta = 4 * Q + qt
                kb.tt("dve", cf[:, qt:qt + 1], r_[:, qt:qt + 1], gates[:, qta, h * 3 + 1:h * 3 + 2], ALU.mult, [r_, gates], [cf])
                kb.stt(oacc[:, qt, h * 64:(h + 1) * 64], ev[:, qt, 0:64], cf[:, qt:qt + 1], oacc[:, qt, h * 64:(h + 1) * 64],
                       ALU.mult, ALU.add, [ev, cf, (oacc, h)], [(oacc, h)])
        kb.cp("act", o16[:, :, 0:256], oacc[:, :, :], [oacc], [o16])
        kb.cp("act", o16[:, :, 256:512], obt[:, :, :], [obt], [o16])
        for c4 in range(4):
            for qt in range(4):
                kb.tr(PSB[:, 512 + qt * 128:512 + (qt + 1) * 128], o16[:, qt, c4 * 128:(c4 + 1) * 128], ident[:, :], [o16, ident], [(PSB, "o")])
            kb.cp("dve", oTs[:, c4, :], PSB[:, 512:1024], [(PSB, "o")], [(oTs, c4)])
            if o_piece is None:
                kb.dma("q_sp", oTv[:, c4, q0:q0 + QG], oTs[:, c4, :], [(oTs, c4)], [oT])
            else:
                hq = NQ // 2
                kb.dma("q_sp", o_piece.t[c4, Q // hq, :, (Q % hq) * QG:(Q % hq + 1) * QG], oTs[:, c4, :], [(oTs, c4)],
                       [(o_piece, (c4, Q // hq))])
        if stop is not None and stop.startswith('A') and int(stop[1:]) == Q:
            return _fin(kb, own)
    return _fin(kb, own)


def _swap64(cols):
    cols = np.asarray(cols).reshape(-1, 64)
    return np.concatenate([cols[:, 32:], cols[:, :32]], axis=1).reshape(-1)


def ab_consts(S):
    NT = S // 128
    NCP = S // 16
    ncmp = NCP - 1
    NCT = (NCP + 127) // 128
    bf = ml_dtypes.bfloat16
    c = {}
    c["identb"] = np.eye(128, dtype=np.float32).astype(bf)
    c["identN"] = (np.eye(128, dtype=np.float32) * NEGM).astype(bf)
    n_ = np.arange(128)[:, None, None]
    D_ = np.arange(5)[None, :, None]
    q_ = np.arange(512)[None, None, :]
    c["cmpbias"] = np.where(16 * n_ + 31 - q_ <= 512 * D_, 0.0, -1.0).astype(np.float32).astype(bf)
    k_ = np.arange(128)[:, None]
    qq = np.arange(128)[None, :]
    cd = np.stack([np.where(k_ <= qq, 0.0, -1.0), np.where(k_ > qq, 0.0, -1.0)], axis=1)
    c["cdiag"] = cd.astype(np.float32).astype(bf)
    j_ = np.arange(128)[:, None, None]
    kt_ = np.arange(NT)[None, :, None]
    kk = np.arange(128)[None, None, :]
    c["EE"] = np.where(j_ == 2 * kt_ + kk // 64, NEGM, 0.0).astype(np.float32).astype(bf)
    qp = np.arange(128)[:, None]
    rel = np.arange(254)[None, :] - 126
    cur = qp // 64
    elig = (rel <= cur).astype(np.float32)
    addw = np.where((rel == cur) | (rel == cur - 1), BIGS, np.where(rel > cur, -BIGS, 0.0)).astype(np.float32)
    c["eaW"] = np.ascontiguousarray(np.stack([elig, addw], axis=1))
    n = np.arange(NCT * 128)[:, None]
    j = np.arange(128)[None, :]
    ov = ((16 * n < 64 * j + 64) & (16 * n + 32 > 64 * j) & (n < ncmp)).astype(np.float32)
    c["ovl"] = np.ascontiguousarray(ov.reshape(NCT, 128, 128).transpose(1, 0, 2)).astype(bf)
    return c


def prep_ab(l, inp, b, g, S):
    w_in = inp["w_in"][l]
    d = {}
    d["posB"] = np.ascontiguousarray(np.broadcast_to(inp["positions"][b][None, :S], (128, S))).astype(np.int32)
    d["gA"] = _col8(inp["attn_norm"][l])
    r64 = np.arange(64)
    r128 = np.arange(128)
    plain = [256 * g + r128, 256 * g + 128 + r128,
             np.concatenate([512 + 64 * g + r64, 512 + 128 + 64 * g + r64]),
             np.concatenate([512 + 256 + 64 * g + r64] * 2),
             np.concatenate([512 + 512 + 64 * g + r64] * 2),
             1304 + 256 * g + r128, 1304 + 256 * g + 128 + r128,
             1816 + 256 * g + r128, 1816 + 256 * g + 128 + r128]
    swaps = [_swap64(plain[i]) for i in (0, 1, 3, 4, 5, 6, 7, 8)]
    cols = np.concatenate(plain + swaps)
    d["w_fm"] = np.ascontiguousarray(w_in[:, cols])
    tcols = np.concatenate([512 + 384 + 64 * g + r64, 512 + 640 + 64 * g + r64, 1280 + 12 * g + np.arange(12),
                            2328 + 256 * g + np.arange(256)])
    d["w_tm"] = np.ascontiguousarray(w_in[:, tcols])
    p = np.arange(128)
    lam_init = 0.8 - 0.6 * math.exp(-0.3 * l)
    colc = np.zeros((128, 4), np.float32)
    colc[:, 0] = (10000.0 ** (-(np.arange(32, dtype=np.float32)) / 32.0)).astype(np.float32)[p % 32]
    colc[:, 1] = np.where((p % 64) < 32, -1.0, 1.0)
    colc[:, 2] = lam_init
    colc[:, 3] = 1.0 - lam_init
    d["colc"] = colc
    w1k = inp["cmp_k_w1"][l].reshape(32, 64, 256).transpose(1, 0, 2)
    w1v = inp["cmp_v_w1"][l].reshape(32, 64, 256).transpose(1, 0, 2)
    d["w1kv"] = np.ascontiguousarray(np.concatenate([w1k, w1v], axis=0))
    d["posT"] = np.ascontiguousarray(np.concatenate([inp["cmp_pos_k"][l].T, inp["cmp_pos_v"][l].T], axis=0))
    w2k = inp["cmp_k_w2"][l].reshape(2, 128, 64).transpose(1, 0, 2)
    d["w2k"] = np.ascontiguousarray(np.concatenate([w2k, w2k], axis=2))
    d["w2v"] = np.ascontiguousarray(inp["cmp_v_w2"][l].reshape(2, 128, 64).transpose(1, 0, 2))
    lq = np.stack([inp["diff_lq1"][l], inp["diff_lk1"][l], inp["diff_lq2"][l], inp["diff_lk2"][l]], axis=0)
    d["lqk"] = np.ascontiguousarray(np.broadcast_to(lq[None], (128, 4, 64)))
    d["sublnB"] = np.ascontiguousarray(np.broadcast_to(inp["diff_subln"][l][None, :], (128, 128)))
    return d


_CACHE = {}


def _prog(name, fn, *args):
    key = (name,) + args
    if key not in _CACHE:
        _CACHE[key] = fn(*args)
    return _CACHE[key]


def kernel_unfused(**inputs):
    inp = {k: np.asarray(v) for k, v in inputs.items()}
    x = inp["x"].astype(np.float32, copy=False)
    B, S, _ = x.shape
    T = S // 2
    L = inp["w_in"].shape[0]
    cores = list(range(8))
    cst = ab_consts(S)
    xT = [np.ascontiguousarray(x[b].T) for b in range(B)]
    out = None
    for l in range(L):
        nc_ab = build_ab(S)
        maps = []
        for c in cores:
            b, g = c // 2, c % 2
            d = prep_ab(l, inp, b, g, S)
            d.update(cst)
            d["xT"] = xT[b]
            maps.append(d)
        res = run_bass_kernel_spmd(nc_ab, maps, core_ids=cores).results
        oaT = [np.concatenate([res[2 * b]["oT"][0:256], res[2 * b + 1]["oT"][0:256]], axis=0) for b in range(B)]
        obT = [np.concatenate([res[2 * b]["oT"][256:512], res[2 * b + 1]["oT"][256:512]], axis=0) for b in range(B)]
        del res, maps
        nc_c1 = build_c1(T)
        maps = []
        p1 = prep_c1(l, inp)
        for c in cores:
            b, g = c // 2, c % 2
            d = dict(p1)
            d["xT"] = np.ascontiguousarray(xT[b][:, g * T:(g + 1) * T])
            d["oaT"] = np.ascontiguousarray(oaT[b][:, g * T:(g + 1) * T])
            d["obT"] = np.ascontiguousarray(obT[b][:, g * T:(g + 1) * T])
            maps.append(d)
        res = run_bass_kernel_spmd(nc_c1, maps, core_ids=cores).results
        x1T = [res[c]["x1T"] for c in cores]
        del res, maps
        nc_c2 = build_c2(T)
        p2 = prep_c2(l, inp)
        maps = []
        for c in cores:
            d = dict(p2)
            d["x1T"] = x1T[c]
            maps.append(d)
        res = run_bass_kernel_spmd(nc_c2, maps, core_ids=cores).results
        xT = [np.concatenate([res[2 * b]["x2T"], res[2 * b + 1]["x2T"]], axis=1) for b in range(B)]
        if l == L - 1:
            out = np.stack([np.concatenate([res[2 * b]["x2nT"], res[2 * b + 1]["x2nT"]], axis=1).T for b in range(B)], axis=0)
        del res, maps
    return np.ascontiguousarray(out.astype(np.float32))


def build_fused(S, L=2):
    kb = KB()
    xin = kb.din("xT_in", [D, S])
    outT = kb.dout("outT", [D, S])
    xcur = xin
    for l in range(L):
        kb.sfx = "_l%d" % l
        oscr = kb.dscr("oscr", [2, 512, S], BF16)
        for g in range(2):
            ov = Tn(oscr.t[g], "oscr_g")
            ov.b = oscr.b
            build_ab(S, kb=kb, io={"xT": xcur, "oT": ov}, sfx="_l%dg%d" % (l, g))
        kb.sfx = "_l%d" % l
        x1 = kb.dscr("x1scr", [D, S], F32)

        def osrc_fn(which, c, t0, G, oscr=oscr):
            r0 = 256 * which + (c % 2) * 128
            return oscr.t[c // 2, r0:r0 + 128, t0:t0 + G]

        build_c1(S, kb=kb, io={"xT": xcur, "x1T": x1, "osrc_fn": osrc_fn, "osrc_tn": oscr}, sfx="_l%d" % l)
        if l < L - 1:
            kb.sfx = "_l%d" % l
            x2 = kb.dscr("x2scr", [D, S], F32)
            build_c2(S, kb=kb, io={"x1T": x1, "x2T": x2, "want_x2": True, "want_x2n": False}, sfx="_l%d" % l)
            xcur = x2
        else:
            build_c2(S, kb=kb, io={"x1T": x1, "x2nT": outT, "want_x2": False, "want_x2n": True}, sfx="_l%d" % l)
    return kb.finish()


def fused_inputs(inp, b, S, cst):
    L = inp["w_in"].shape[0]
    d = dict(cst)
    d["cm01"] = np.triu(np.ones((128, 128), np.float32))
    d["xT_in"] = np.ascontiguousarray(inp["x"][b].T)
    for l in range(L):
        for g in range(2):
            for k, v in prep_ab(l, inp, b, g, S).items():
                if k == "posB":
                    d["posB"] = v
                else:
                    d["%s_l%dg%d" % (k, l, g)] = v
        for k, v in prep_c1(l, inp).items():
            if k != "cm01":
                d["%s_l%d" % (k, l)] = v
        for k, v in prep_c2(l, inp).items():
            d["%s_l%d" % (k, l)] = v
    return d


def kernel_fused4(**inputs):
    inp = {k: np.asarray(v) for k, v in inputs.items()}
    B, S, _ = inp["x"].shape
    nc = build_fused(S)
    cst = ab_consts(S)
    per_b = [fused_inputs(inp, b, S, cst) for b in range(B)]
    maps = [per_b[c % B] for c in range(8)]
    res = run_bass_kernel_spmd(nc, maps, core_ids=list(range(8))).results
    out = np.stack([res[b]["outT"].T for b in range(B)], axis=0)
    return np.ascontiguousarray(out.astype(np.float32))


RG_PAIRS = [[0, 1], [2, 3], [4, 5], [6, 7]]


def _allgather(kb, src, dst, key):
    kb.pr.add("q_cc", lambda e: e.collective_compute("AllGather", ALU.bypass, replica_groups=RG_PAIRS,
                                                     ins=[src.opt()], outs=[dst.opt()]), [key[0]], [key[1]])


def build_fused8(S, L=2):
    kb = KB()
    T = S // 2
    xfull = kb.din("xT_in", [D, S])
    xhalf = kb.din("xT_half", [D, T])
    outT = kb.dout("outT", [D, T])
    hgath = None
    xown = xhalf
    for l in range(L):
        kb.sfx = "_l%d" % l
        opc = kb.dscr("opc", [4, 2, 128, T], BF16)
        ogath = kb.dscr("ogath", [4, 2, 256, T], BF16)
        io = {"o_piece": opc}
        if hgath is None:
            io["xT"] = xfull
        else:
            io["h_src"] = hgath
        build_ab(S, kb=kb, io=io, sfx="_l%d" % l)
        for c4 in range(4):
            for half in range(2):
                _allgather(kb, opc.t[c4, half], ogath.t[c4, half], ((opc, (c4, half)), (ogath, (c4, half))))
        kb.sfx = "_l%d" % l
        x1 = kb.dscr("x1scr", [D, T], F32)
        build_c1(T, kb=kb, io={"xT": xown, "x1T": x1, "o_gath": ogath}, sfx="_l%d" % l)
        if l < L - 1:
            kb.sfx = "_l%d" % l
            x2 = kb.dscr("x2scr", [D, T], F32)
            hpc = kb.dscr("hpc", [8, 128, T], BF16)
            hg = kb.dscr("hgath", [8, 256, T], BF16)
            build_c2(T, kb=kb, io={"x1T": x1, "x2T": x2, "want_x2": True, "want_x2n": False, "h_next": hpc}, sfx="_l%d" % l)
            for c in range(8):
                _allgather(kb, hpc.t[c], hg.t[c], ((hpc, c), (hg, c)))
            hgath = hg
            xown = x2
        else:
            build_c2(T, kb=kb, io={"x1T": x1, "x2nT": outT, "want_x2": False, "want_x2n": True}, sfx="_l%d" % l)
    return kb.finish()


def fused8_inputs(inp, b, g, S, cst):
    L = inp["w_in"].shape[0]
    T = S // 2
    d = dict(cst)
    d["cm01"] = np.triu(np.ones((128, 128), np.float32))
    xT = np.ascontiguousarray(inp["x"][b].T)
    d["xT_in"] = xT
    d["xT_half"] = np.ascontiguousarray(xT[:, g * T:(g + 1) * T])
    m = np.zeros((128, 2), np.float32)
    m[:, g] = 1.0
    for l in range(L):
        for k, v in prep_ab(l, inp, b, g, S).items():
            if k == "posB":
                d["posB"] = v
            elif not (l > 0 and k == "gA"):
                d["%s_l%d" % (k, l)] = v
        for k, v in prep_c1(l, inp).items():
            if k != "cm01":
                d["%s_l%d" % (k, l)] = v
        d["msel_l%d" % l] = m
        for k, v in prep_c2(l, inp).items():
            d["%s_l%d" % (k, l)] = v
        if l < L - 1:
            d["gN_l%d" % l] = _col8(inp["attn_norm"][l + 1])
    return d


def kernel(**inputs):
    inp = {k: np.asarray(v) for k, v in inputs.items()}
    B, S, _ = inp["x"].shape
    T = S // 2
    nc = build_fused8(S)
    cst = ab_consts(S)
    maps = [fused8_inputs(inp, c // 2, c % 2, S, cst) for c in range(8)]
    res = run_bass_kernel_spmd(nc, maps, core_ids=list(range(8))).results
    out = np.stack([np.concatenate([res[2 * b]["outT"].T, res[2 * b + 1]["outT"].T], axis=0) for b in range(B)], axis=0)
    return np.ascontiguousarray(out.astype(np.float32))
```

```python
import contextlib
import math
import numpy as np
import ml_dtypes
import concourse.bass as bass
import concourse.mybir as mybir
from concourse.bass_utils import run_bass_kernel_spmd

F32 = mybir.dt.float32
BF16 = mybir.dt.bfloat16
I32 = mybir.dt.int32
AF = mybir.ActivationFunctionType
ALU = mybir.AluOpType
AX = mybir.AxisListType

D = 1024
DFF = 2816
EPS = 1e-6
NEGM = 32768.0

COMPUTE = ("pe", "act", "dve", "pool")
DMAQ = {"q_sp": "sp", "q_pool": "pool", "q_act": "act", "q_cc": "pool"}
QINC = {"q_sp": 16, "q_pool": 16, "q_act": 16, "q_cc": 1}
NSEM_PER_Q = 10


class Buf:
    __slots__ = ("name", "w", "r", "excl")

    def __init__(self, name):
        self.name = name
        self.w = {}
        self.r = {}
        self.excl = False


class Tn:
    def __init__(self, t, name):
        self.t = t
        self.b = Buf(name)

    def __getitem__(self, idx):
        return self.t[idx]


def _norm(lst):
    out = []
    for x in lst:
        if isinstance(x, tuple):
            b, k = x
        else:
            b, k = x, None
        if isinstance(b, Tn):
            b = b.b
        out.append((b, k))
    return out


class Prog:
    def __init__(self, nc):
        self.nc = nc
        self.ops = []

    def add(self, stream, fn, reads=(), writes=()):
        i = len(self.ops)
        deps = {}

        def ck(d, k):
            if k is None:
                return list(d.keys())
            return [kk for kk in (k, None) if kk in d]

        reads = _norm(reads)
        writes = _norm(writes)
        writes = writes + [(b, k) for (b, k) in reads if b.excl and (b, k) not in writes]
        for (b, k) in reads:
            for kk in ck(b.w, k):
                deps[b.w[kk]] = True
        for (b, k) in writes:
            for kk in ck(b.w, k):
                deps.setdefault(b.w[kk], False)
            for kk in ck(b.r, k):
                for s, j in b.r[kk].items():
                    if isinstance(j, list):
                        for jj in j:
                            deps.setdefault(jj, False)
                    else:
                        deps.setdefault(j, False)
        for (b, k) in reads:
            d = b.r.setdefault(k, {})
            if stream in DMAQ:
                d.setdefault(stream, [])
                d[stream].append(i)
            else:
                d[stream] = i
        for (b, k) in writes:
            if k is None:
                b.w = {None: i}
                b.r = {}
            else:
                b.w[k] = i
                b.r.pop(k, None)
        deps.pop(i, None)
        self.ops.append(dict(stream=stream, fn=fn, deps=deps, sig=None))
        return i

    def emit(self, es):
        nc = self.nc
        ops = self.ops
        need = [[] for _ in ops]
        for c, o in enumerate(ops):
            cs = o["stream"]
            best = {}
            for p, raw in o["deps"].items():
                ps = ops[p]["stream"]
                if ps in DMAQ:
                    need[c].append(p)
                    continue
                if ps == cs:
                    if cs == "pe" or not raw:
                        continue
                if ps not in best or best[ps] < p:
                    best[ps] = p
            need[c].extend(best.values())
        signal = [False] * len(ops)
        for c in range(len(ops)):
            for p in need[c]:
                signal[p] = True
        sems = {}
        for s in COMPUTE:
            sems[s] = es.enter_context(nc.semaphore("s_" + s))
        qsems = {}
        for q in DMAQ:
            qsems[q] = [es.enter_context(nc.semaphore("s_%s_%d" % (q, j))) for j in range(NSEM_PER_Q)]
        cnt = {s: 0 for s in COMPUTE}
        qcnt = {q: 0 for q in DMAQ}
        qsemcnt = {q: [0] * NSEM_PER_Q for q in DMAQ}
        for i, o in enumerate(ops):
            s = o["stream"]
            if s in DMAQ:
                j = qcnt[s] % NSEM_PER_Q
                qcnt[s] += 1
                qsemcnt[s][j] += 1
                o["sig"] = (qsems[s][j], QINC[s] * qsemcnt[s][j], (s, j))
                o["prev"] = (qsems[s][j], QINC[s] * (qsemcnt[s][j] - 1), (s, j))
            elif signal[i]:
                cnt[s] += 1
                o["sig"] = (sems[s], cnt[s], s)
        per_eng = {e: [] for e in ("pe", "act", "dve", "pool", "sp")}
        for i, o in enumerate(ops):
            s = o["stream"]
            per_eng[DMAQ.get(s, s)].append(i)
        waited = {e: {} for e in per_eng}

        def run_engine(ename, eng):
            wd = waited[ename]
            for i in per_eng[ename]:
                o = ops[i]
                ws = []
                for p in need[i]:
                    ws.append(ops[p]["sig"])
                if o["stream"] in DMAQ:
                    if o["prev"][1] > 0:
                        ws.append(o["prev"])
                for sem, val, key in ws:
                    if wd.get(key, 0) >= val:
                        continue
                    wd[key] = val
                    eng.wait_ge(sem, val)
                ins = o["fn"](eng)
                if o["sig"] is not None:
                    sem, val, key = o["sig"]
                    ins.then_inc(sem, QINC.get(o["stream"], 1))
            for q, e in DMAQ.items():
                if e != ename:
                    continue
                for j in range(NSEM_PER_Q):
                    v = QINC[q] * qsemcnt[q][j]
                    if v > 0 and wd.get((q, j), 0) < v:
                        eng.wait_ge(qsems[q][j], v)

        with nc.Block() as block:
            @block.tensor
            def _(e):
                run_engine("pe", e)

            @block.scalar
            def _(e):
                run_engine("act", e)

            @block.vector
            def _(e):
                run_engine("dve", e)

            @block.gpsimd
            def _(e):
                run_engine("pool", e)

            @block.sync
            def _(e):
                run_engine("sp", e)


class KB:
    def __init__(self):
        self.nc = bass.Bass("TRN2", target_bir_lowering=False)
        self.es = contextlib.ExitStack()
        self.pr = Prog(self.nc)
        self.off = 16896
        self.maxoff = 0
        self.sfx = ""
        self.io = {}
        self.uid = 0
        self.alloc_log = []
        self.prev_list = []
        self.dram = {}
        self.psums = {}
        self.dummy = self.sb("dummy", [128, 16], F32)
        self.base = self.off

    SHARED = ("posB", "identb", "identN", "cmpbias", "cdiag", "EE", "eaW", "ovl", "cm01")

    def begin_phase(self, sfx):
        self.sfx = sfx
        self.off = self.base
        self.alloc_log = []

    def phase_barrier(self):
        if self.prev_list:
            self.barrier(self.prev_list, list(self.alloc_log))

    def end_phase(self):
        self.prev_list = list(self.alloc_log)

    def sb(self, name, shape, dt):
        nb = 4 if dt in (F32, I32) else 2
        size = nb
        for s_ in shape[1:]:
            size *= s_
        size = (size + 63) // 64 * 64
        off = self.off
        self.off += size
        self.maxoff = max(self.maxoff, self.off)
        assert self.off <= 229376 - 256, ("SBUF overflow", name, self.off)
        self.uid += 1
        name = "%s%s_%d" % (name, self.sfx, self.uid)
        t = Tn(self.nc.alloc_sbuf_tensor_at(name, list(shape), dt, offset=off), name)
        self.alloc_log.append(t)
        return t

    def barrier(self, old, new):
        d = self.dummy
        self.pr.add("dve", lambda e: e.memset(d[:, :], 0.0), list(old), list(new) + list(old))

    def psum(self, name, shape=(128, 512), dt=F32):
        if name in self.psums:
            return self.psums[name]
        t = Tn(self.es.enter_context(self.nc.psum_tensor(name, list(shape), dt)), name)
        t.b.excl = True
        self.psums[name] = t
        return t

    def din(self, name, shape, dt=F32):
        if name in self.io:
            return self.io[name]
        if name not in self.SHARED:
            name = name + self.sfx
        if name in self.dram:
            return self.dram[name]
        t = self.nc.dram_tensor(name, list(shape), dt, kind="ExternalInput")
        self.dram[name] = Tn(t.ap(), name)
        return self.dram[name]

    def dout(self, name, shape, dt=F32):
        if name in self.io:
            return self.io[name]
        t = self.nc.dram_tensor(name + self.sfx, list(shape), dt, kind="ExternalOutput")
        return Tn(t.ap(), name)

    def dscr(self, name, shape, dt):
        t = self.nc.dram_tensor(name + self.sfx, list(shape), dt, kind="Internal")
        return Tn(t.ap(), name)

    def dma(self, q, out, in_, r, w):
        self.pr.add(q, lambda e: e.dma_start(out=out, in_=in_), r, w)

    def mm(self, out, lhsT, rhs, start, stop, r, w):
        self.pr.add("pe", lambda e: e.matmul(out, lhsT, rhs, start=start, stop=stop, skip_group_check=True), r, w)

    def tr(self, out, in_, ident, r, w):
        self.pr.add("pe", lambda e: e.transpose(out, in_, ident), r, w)

    def act(self, out, in_, func, r, w, bias=None, scale=None):
        kw = {}
        if bias is not None:
            kw["bias"] = bias
        if scale is not None:
            kw["scale"] = scale
        self.pr.add("act", lambda e: e.activation(out=out, in_=in_, func=func, **kw), r, w)

    def ts(self, eng, out, in0, s1, s2, op0, op1, r, w, accum_out=None):
        if op1 is None:
            self.pr.add(eng, lambda e: e.tensor_scalar(out=out, in0=in0, scalar1=s1, scalar2=None, op0=op0), r, w)
        elif accum_out is not None:
            self.pr.add(eng, lambda e: e.tensor_scalar(out=out, in0=in0, scalar1=s1, scalar2=s2, op0=op0, op1=op1,
                                                      accum_out=accum_out), r, w)
        else:
            self.pr.add(eng, lambda e: e.tensor_scalar(out=out, in0=in0, scalar1=s1, scalar2=s2, op0=op0, op1=op1), r, w)

    def tt(self, eng, out, in0, in1, op, r, w):
        self.pr.add(eng, lambda e: e.tensor_tensor(out=out, in0=in0, in1=in1, op=op), r, w)

    def stt(self, out, in0, scalar, in1, op0, op1, r, w, accum_out=None):
        if accum_out is None:
            self.pr.add("dve", lambda e: e.scalar_tensor_tensor(out=out, in0=in0, scalar=scalar, in1=in1, op0=op0, op1=op1), r, w)
        else:
            self.pr.add("dve", lambda e: e.scalar_tensor_tensor(out=out, in0=in0, scalar=scalar, in1=in1, op0=op0, op1=op1,
                                                                accum_out=accum_out), r, w)

    def cp(self, eng, out, in_, r, w):
        if eng == "act":
            self.pr.add("act", lambda e: e.copy(out=out, in_=in_), r, w)
        else:
            self.pr.add(eng, lambda e: e.tensor_copy(out=out, in_=in_), r, w)

    def recip(self, out, in_, r, w):
        self.pr.add("dve", lambda e: e.reciprocal(out=out, in_=in_), r, w)

    def memset(self, eng, ap, val, w):
        self.pr.add(eng, lambda e: e.memset(ap, val), (), w)

    def finish(self):
        self.pr.emit(self.es)
        self.es.close()
        return self.nc


def _fin(kb, own):
    kb.end_phase()
    kb.io = {}
    if own:
        return kb.finish()
    return None


def load_w(kb, q, dst, src_ap, kchunks, r=(), extra_w=()):
    v = src_ap.t.rearrange("(c p) n -> p c n", p=128)
    for c in range(kchunks):
        kb.dma(q, dst[:, c, :], v[:, c, :], [src_ap] + list(r), [(dst, c)] + list(extra_w))


def rmsnorm_fm(kb, xg, hT, gcol, ones, epsT, ps_ss, tmpA, rstdB, G=512):
    kb.tt("pool", hT[:, :, :], xg[:, :, :], xg[:, :, :], ALU.mult, [xg], [hT])
    for c in range(8):
        kb.mm(ps_ss[:, :G], ones[:, :], hT[:, c, :], c == 0, c == 7, [ones, hT], [ps_ss])
    kb.act(tmpA[:, :G], ps_ss[:, :G], AF.Sqrt, [ps_ss, epsT], [tmpA], bias=epsT[:, 0:1], scale=1.0 / D)
    kb.recip(rstdB[:, :G], tmpA[:, :G], [tmpA], [rstdB])
    for c in range(8):
        kb.stt(hT[:, c, :], xg[:, c, :], gcol[:, c:c + 1], rstdB[:, :G], ALU.mult, ALU.mult, [xg, gcol, rstdB], [(hT, c)])


def build_c1(T, kb=None, io=None, sfx=""):
    own = kb is None
    if own:
        kb = KB()
    kb.io = io or {}
    kb.begin_phase(sfx)
    G = 512
    NG = T // G
    xT = kb.din("xT", [D, T])
    osrc_fn = kb.io.get("osrc_fn")
    osrc_tn = kb.io.get("osrc_tn")
    o_gath = kb.io.get("o_gath")
    msel_d = kb.din("msel", [128, 2]) if o_gath is not None else None
    if osrc_fn is None and o_gath is None:
        oaT = kb.din("oaT", [512, T], BF16)
        obT = kb.din("obT", [512, T], BF16)
        oaTv = oaT.t.rearrange("(c p) t -> p c t", p=128)
        obTv = obT.t.rearrange("(c p) t -> p c t", p=128)
    gA_d = kb.din("gA", [128, 8])
    wsgu_d = kb.din("w_sgu", [D, 1024])
    gB_d = kb.din("sgu_gB", [128, 512])
    swT_d = kb.din("sgu_wT", [128, 4, 128])
    sbB_d = kb.din("sgu_bB", [128, 4, 128])
    cm_d = kb.din("cm01", [128, 128])
    wbr_d = kb.din("w_br", [1536, D])
    wm_d = kb.din("w_merge", [D, 3072])
    bm_d = kb.din("bm", [128, 24])
    wo_d = kb.din("w_out", [D, D])
    x1T = kb.dout("x1T", [D, T])

    Wsgu = kb.sb("Wsgu", [128, 8, 1024], BF16)
    Wbr = kb.sb("Wbr", [128, 12, 1024], BF16)
    Wm = kb.sb("Wm", [128, 8, 3072], BF16)
    Wo = kb.sb("Wo", [128, 8, 1024], BF16)
    gA = kb.sb("gA_s", [128, 8], F32)
    gB = kb.sb("gB_s", [128, 512], F32)
    swT = kb.sb("swT_s", [128, 4, 128], F32)
    swTb = kb.sb("swTb", [128, 4, 128], BF16)
    sbB = kb.sb("sbB_s", [128, 4, 128], F32)
    cm = kb.sb("cm_s", [128, 128], F32)
    bm = kb.sb("bm_s", [128, 24], F32)
    ones = kb.sb("ones", [128, 128], BF16)
    epsT = kb.sb("epsT", [128, 1], F32)
    eps512 = kb.sb("eps512", [128, 1], F32)

    xg = kb.sb("xg", [128, 8, G], F32)
    hT = kb.sb("hT", [128, 8, G], BF16)
    tmpA = kb.sb("tmpA", [128, G], F32)
    rstdB = kb.sb("rstdB", [128, G], F32)
    uT = kb.sb("uT", [128, 4, G], BF16)
    vg = kb.sb("vg", [128, 512], F32)
    vjunk = kb.sb("vjunk", [128, 512], F32)
    vn = kb.sb("vn", [128, 512], BF16)
    ssv = kb.sb("ssv", [128, 4], F32)
    ocT = kb.sb("ocT", [128, 4, G], BF16)
    stmp = kb.sb("stmp", [128, G], F32)
    oa = kb.sb("oa", [128, 4, G], BF16)
    ob = kb.sb("ob", [128, 4, G], BF16)
    gate = [kb.sb("gate%d" % i, [128, G], F32) for i in range(2)]
    acc = kb.sb("acc", [128, G], F32)
    mixedT = kb.sb("mixedT", [128, 8, G], BF16)
    ost = [kb.sb("ost%d" % i, [128, 2, G], BF16) for i in range(2)]
    msel = kb.sb("msel_s", [128, 2], F32)
    sti = [0]
    PS = [kb.psum("ps%d" % i) for i in range(7)]
    kb.phase_barrier()
    if msel_d is not None:
        kb.dma("q_sp", msel.t[:], msel_d.t, [msel_d], [msel])

    kb.memset("dve", ones[:, :], 1.0, [ones])
    kb.memset("dve", epsT[:, :], EPS, [epsT])
    kb.memset("dve", eps512[:, :], EPS, [eps512])
    for (dst, src) in ((gA, gA_d), (gB, gB_d), (sbB, sbB_d), (cm, cm_d), (bm, bm_d), (swT, swT_d)):
        kb.dma("q_sp", dst.t[:], src.t, [src], [dst])
    for g4 in range(4):
        kb.tt("dve", swTb[:, g4, :], swT[:, g4, :], cm[:, :], ALU.mult, [swT, cm], [(swTb, g4)])
    load_w(kb, "q_pool", Wsgu, wsgu_d, 8)
    load_w(kb, "q_pool", Wm, wm_d, 8)
    load_w(kb, "q_pool", Wbr, wbr_d, 12)
    load_w(kb, "q_pool", Wo, wo_d, 8)

    xTv = xT.t.rearrange("(c p) t -> p c t", p=128)
    x1Tv = x1T.t.rearrange("(c p) t -> p c t", p=128)
    pi = [0]

    def nps():
        p = PS[pi[0] % 3]
        pi[0] += 1
        return p

    for gi in range(NG):
        t0 = gi * G
        for c in range(8):
            kb.dma("q_sp", xg[:, c, :], xTv[:, c, t0:t0 + G], [xT], [(xg, c)])
        for c in range(4):
            if osrc_fn is None and o_gath is None:
                kb.dma("q_sp", oa[:, c, :], oaTv[:, c, t0:t0 + G], [oaT], [(oa, c)])
                kb.dma("q_sp", ob[:, c, :], obTv[:, c, t0:t0 + G], [obT], [(ob, c)])
            elif o_gath is None:
                kb.dma("q_sp", oa[:, c, :], osrc_fn(0, c, t0, G), [osrc_tn], [(oa, c)])
                kb.dma("q_sp", ob[:, c, :], osrc_fn(1, c, t0, G), [osrc_tn], [(ob, c)])
            else:
                for which, dst in ((0, oa), (1, ob)):
                    c4 = 2 * which + c % 2
                    r = c // 2
                    st = ost[sti[0] % 2]
                    sti[0] += 1
                    for half in range(2):
                        kb.dma("q_sp", st[:, half, :], o_gath.t[c4, half, r * 128:(r + 1) * 128, t0:t0 + G], [o_gath], [(st, half)])
                    kb.ts("dve", dst[:, c, :], st[:, 0, :], msel[:, 0:1], None, ALU.mult, None, [st, msel], [(dst, c)])
                    kb.stt(dst[:, c, :], st[:, 1, :], msel[:, 1:2], dst[:, c, :], ALU.mult, ALU.add, [st, msel, (dst, c)], [(dst, c)])
        rmsnorm_fm(kb, xg, hT, gA, ones, epsT, nps(), tmpA, rstdB, G)
        for uc in range(4):
            p = nps()
            for k in range(8):
                kb.mm(p[:, :G], Wsgu[:, k, uc * 128:(uc + 1) * 128], hT[:, k, :], k == 0, k == 7, [(Wsgu, k), hT], [p])
            kb.act(uT[:, uc, :], p[:, :G], AF.Gelu_apprx_tanh, [p], [(uT, uc)])
        ps_s = PS[3:7]
        for tt in range(4):
            p = nps()
            for k in range(8):
                kb.mm(p[:, :512], hT[:, k, tt * 128:(tt + 1) * 128], Wsgu[:, k, 512:1024], k == 0, k == 7, [hT, (Wsgu, k)], [p])
            kb.act(vg[:, :], p[:, :512], AF.Gelu_apprx_tanh, [p], [vg])
            kb.stt(vjunk[:, :], vg[:, :], 1.0, vg[:, :], ALU.mult, ALU.mult, [vg], [vjunk, (ssv, tt)], accum_out=ssv[:, tt:tt + 1])
            kb.act(ssv[:, tt:tt + 1], ssv[:, tt:tt + 1], AF.Sqrt, [(ssv, tt), eps512], [(ssv, tt)], bias=eps512[:, 0:1], scale=1.0 / 512)
            kb.recip(ssv[:, tt:tt + 1], ssv[:, tt:tt + 1], [(ssv, tt)], [(ssv, tt)])
            kb.stt(vn[:, :], vg[:, :], ssv[:, tt:tt + 1], gB[:, :], ALU.mult, ALU.mult, [vg, (ssv, tt), gB], [vn])
            for g4 in range(4):
                kb.mm(ps_s[g4][:, tt * 128:(tt + 1) * 128], vn[:, g4 * 128:(g4 + 1) * 128], swTb[:, g4, :], True, True,
                      [vn, swTb], [(ps_s[g4], tt)])
        for g4 in range(4):
            for tt in range(4):
                kb.tt("dve", stmp[:, tt * 128:(tt + 1) * 128], ps_s[g4][:, tt * 128:(tt + 1) * 128], sbB[:, g4, :], ALU.add,
                      [ps_s[g4], sbB], [(stmp, tt)])
            kb.tt("dve", ocT[:, g4, :], stmp[:, :], uT[:, g4, :], ALU.mult, [stmp, (uT, g4)], [(ocT, g4)])
        osrc = (oa, ob, ocT)
        for oc in range(8):
            for br in range(3):
                pg = nps()
                for k in range(8):
                    kb.mm(pg[:, :G], Wm[:, k, br * 1024 + oc * 128: br * 1024 + (oc + 1) * 128], hT[:, k, :], k == 0, k == 7,
                          [(Wm, k), hT], [pg])
                gt = gate[br % 2]
                kb.act(gt[:, :], pg[:, :G], AF.Sigmoid, [pg, bm], [gt], bias=bm[:, br * 8 + oc: br * 8 + oc + 1])
                pb = nps()
                for k in range(4):
                    kb.mm(pb[:, :G], Wbr[:, br * 4 + k, oc * 128:(oc + 1) * 128], osrc[br][:, k, :], k == 0, k == 3,
                          [(Wbr, br * 4 + k), (osrc[br], k)], [pb])
                if br == 0:
                    kb.tt("dve", acc[:, :], gt[:, :], pb[:, :G], ALU.mult, [gt, pb], [acc])
                else:
                    kb.tt("dve", gt[:, :], gt[:, :], pb[:, :G], ALU.mult, [gt, pb], [gt])
                    if br == 1:
                        kb.tt("dve", acc[:, :], acc[:, :], gt[:, :], ALU.add, [acc, gt], [acc])
                    else:
                        kb.tt("dve", mixedT[:, oc, :], acc[:, :], gt[:, :], ALU.add, [acc, gt], [(mixedT, oc)])
        for oc in range(8):
            p = nps()
            for k in range(8):
                kb.mm(p[:, :G], Wo[:, k, oc * 128:(oc + 1) * 128], mixedT[:, k, :], k == 0, k == 7, [(Wo, k), (mixedT, k)], [p])
            kb.tt("dve", xg[:, oc, :], xg[:, oc, :], p[:, :G], ALU.add, [(xg, oc), p], [(xg, oc)])
            kb.dma("q_sp", x1Tv[:, oc, t0:t0 + G], xg[:, oc, :], [(xg, oc)], [x1T])
    return _fin(kb, own)


def build_c2(T, kb=None, io=None, sfx=""):
    own = kb is None
    if own:
        kb = KB()
    kb.io = io or {}
    kb.begin_phase(sfx)
    G = 512
    NG = T // G
    NF = DFF // 128
    x1T = kb.din("x1T", [D, T])
    gF_d = kb.din("gF", [128, 8])
    gZ_d = kb.din("gZ", [128, 8])
    w1_d = kb.din("w1", [D, DFF])
    w3_d = kb.din("w3", [D, DFF])
    w2_d = kb.din("w2", [DFF, D])
    x2T = kb.dout("x2T", [D, T]) if kb.io.get("want_x2", True) else None
    x2nT = kb.dout("x2nT", [D, T]) if kb.io.get("want_x2n", True) else None

    W1 = kb.sb("W1", [128, 8, DFF], BF16)
    W3 = kb.sb("W3", [128, 8, DFF], BF16)
    W2 = kb.sb("W2", [128, NF, D], BF16)
    gF = kb.sb("gF_s", [128, 8], F32)
    gZ = kb.sb("gZ_s", [128, 8], F32)
    ones = kb.sb("ones", [128, 128], BF16)
    epsT = kb.sb("epsT", [128, 1], F32)
    xg = kb.sb("xg", [128, 8, G], F32)
    hT = kb.sb("hT", [128, 8, G], BF16)
    tmpA = kb.sb("tmpA", [128, G], F32)
    rstdB = kb.sb("rstdB", [128, G], F32)
    aT = kb.sb("aT", [128, NF, G], BF16)
    sl = [kb.sb("sl%d" % i, [128, G], F32) for i in range(2)]
    gN = kb.sb("gN_s", [128, 8], F32)
    PS = [kb.psum("ps%d" % i) for i in range(7)]
    kb.phase_barrier()
    want_x2 = kb.io.get("want_x2", True)
    want_x2n = kb.io.get("want_x2n", True)
    h_next = kb.io.get("h_next")
    if h_next is not None:
        gN_d = kb.din("gN", [128, 8])
        kb.dma("q_sp", gN.t[:], gN_d.t, [gN_d], [gN])

    kb.memset("dve", ones[:, :], 1.0, [ones])
    kb.memset("dve", epsT[:, :], EPS, [epsT])
    kb.dma("q_sp", gF.t[:], gF_d.t, [gF_d], [gF])
    kb.dma("q_sp", gZ.t[:], gZ_d.t, [gZ_d], [gZ])
    load_w(kb, "q_pool", W1, w1_d, 8)
    load_w(kb, "q_pool", W3, w3_d, 8)
    load_w(kb, "q_pool", W2, w2_d, NF)
    x1Tv = x1T.t.rearrange("(c p) t -> p c t", p=128)
    x2Tv = x2T.t.rearrange("(c p) t -> p c t", p=128) if x2T is not None else None
    x2nTv = x2nT.t.rearrange("(c p) t -> p c t", p=128) if x2nT is not None else None
    pi = [0]

    def nps():
        p = PS[pi[0] % 7]
        pi[0] += 1
        return p

    for gi in range(NG):
        t0 = gi * G
        for c in range(8):
            kb.dma("q_sp", xg[:, c, :], x1Tv[:, c, t0:t0 + G], [x1T], [(xg, c)])
        rmsnorm_fm(kb, xg, hT, gF, ones, epsT, nps(), tmpA, rstdB, G)
        for fc in range(NF):
            p1 = nps()
            for k in range(8):
                kb.mm(p1[:, :G], W1[:, k, fc * 128:(fc + 1) * 128], hT[:, k, :], k == 0, k == 7, [(W1, k), hT], [p1])
            p3 = nps()
            for k in range(8):
                kb.mm(p3[:, :G], W3[:, k, fc * 128:(fc + 1) * 128], hT[:, k, :], k == 0, k == 7, [(W3, k), hT], [p3])
            s = sl[fc % 2]
            kb.act(s[:, :], p1[:, :G], AF.Silu, [p1], [s])
            kb.tt("dve", aT[:, fc, :], s[:, :], p3[:, :G], ALU.mult, [s, p3], [(aT, fc)])
        for oc in range(8):
            p = nps()
            for k in range(NF):
                kb.mm(p[:, :G], W2[:, k, oc * 128:(oc + 1) * 128], aT[:, k, :], k == 0, k == NF - 1, [(W2, k), (aT, k)], [p])
            kb.tt("dve", xg[:, oc, :], xg[:, oc, :], p[:, :G], ALU.add, [(xg, oc), p], [(xg, oc)])
            if want_x2:
                kb.dma("q_sp", x2Tv[:, oc, t0:t0 + G], xg[:, oc, :], [(xg, oc)], [x2T])
        if h_next is not None:
            rmsnorm_fm(kb, xg, hT, gN, ones, epsT, nps(), tmpA, rstdB, G)
            for c in range(8):
                kb.dma("q_sp", h_next.t[c, :, t0:t0 + G], hT[:, c, :], [(hT, c)], [(h_next, c)])
        if want_x2n:
            rmsnorm_fm_f32(kb, xg, hT, gZ, ones, epsT, nps(), tmpA, rstdB, G)
            for oc in range(8):
                kb.dma("q_sp", x2nTv[:, oc, t0:t0 + G], xg[:, oc, :], [(xg, oc)], [x2nT])
    return _fin(kb, own)


def rmsnorm_fm_f32(kb, xg, hT, gcol, ones, epsT, ps_ss, tmpA, rstdB, G=512):
    kb.tt("pool", hT[:, :, :], xg[:, :, :], xg[:, :, :], ALU.mult, [xg], [hT])
    for c in range(8):
        kb.mm(ps_ss[:, :G], ones[:, :], hT[:, c, :], c == 0, c == 7, [ones, hT], [ps_ss])
    kb.act(tmpA[:, :G], ps_ss[:, :G], AF.Sqrt, [ps_ss, epsT], [tmpA], bias=epsT[:, 0:1], scale=1.0 / D)
    kb.recip(rstdB[:, :G], tmpA[:, :G], [tmpA], [rstdB])
    for c in range(8):
        kb.stt(xg[:, c, :], xg[:, c, :], gcol[:, c:c + 1], rstdB[:, :G], ALU.mult, ALU.mult, [(xg, c), gcol, rstdB], [(xg, c)])


def _col8(v):
    return np.ascontiguousarray(v.reshape(8, 128).T)


def prep_c1(l, inp):
    w_in = inp["w_in"][l]
    d = {}
    d["gA"] = _col8(inp["attn_norm"][l])
    d["w_sgu"] = np.ascontiguousarray(w_in[:, 2840:3864])
    d["sgu_gB"] = np.ascontiguousarray(np.broadcast_to(inp["sgu_norm"][l][None, :], (128, 512)))
    d["sgu_wT"] = np.ascontiguousarray(inp["sgu_w"][l].transpose(2, 0, 1))
    d["sgu_bB"] = np.ascontiguousarray(np.broadcast_to(inp["sgu_b"][l][None], (128, 4, 128)))
    d["cm01"] = np.triu(np.ones((128, 128), np.float32))
    d["w_br"] = np.ascontiguousarray(np.concatenate([inp["w_branch_a"][l], inp["w_branch_b"][l], inp["w_branch_c"][l]], 0))
    d["w_merge"] = np.ascontiguousarray(inp["w_merge"][l])
    d["bm"] = np.ascontiguousarray(inp["b_merge"][l].reshape(24, 128).T)
    d["w_out"] = np.ascontiguousarray(inp["w_out"][l])
    return d


def prep_c2(l, inp):
    d = {}
    d["gF"] = _col8(inp["ffn_norm"][l])
    d["gZ"] = _col8(inp["final_norm"])
    d["w1"] = np.ascontiguousarray(inp["w_ffn1"][l])
    d["w3"] = np.ascontiguousarray(inp["w_ffn3"][l])
    d["w2"] = np.ascontiguousarray(inp["w_ffn2"][l])
    return d


NFM = 17
FM_ROPE = {0: 9, 1: 10, 3: 11, 4: 12, 5: 13, 6: 14, 7: 15, 8: 16}
TWO_PI = 2.0 * math.pi
C1 = 6.28125
C2 = TWO_PI - C1
BIGS = 1.0e9


def build_ab(S, stop=None, kb=None, io=None, sfx=""):
    own = kb is None
    if own:
        kb = KB()
    kb.io = io or {}
    kb.begin_phase(sfx)
    PG = 256
    NPG = S // PG
    QG = 512
    NQ = S // QG
    NT = S // 128
    NCP = S // 16
    ncmp = NCP - 1
    NCT = (NCP + 127) // 128
    NCW = NCT * 128

    h_src = kb.io.get("h_src")
    o_piece = kb.io.get("o_piece")
    xT = kb.din("xT", [D, S]) if h_src is None else None
    posB_d = kb.din("posB", [128, S], I32)
    gA_d = kb.din("gA", [128, 8])
    wfm_d = kb.din("w_fm", [D, NFM * 128])
    wtm_d = kb.din("w_tm", [D, 396])
    identb_d = kb.din("identb", [128, 128], BF16)
    identN_d = kb.din("identN", [128, 128], BF16)
    colc_d = kb.din("colc", [128, 4])
    cmpbias_d = kb.din("cmpbias", [128, 5, 512], BF16)
    cdiag_d = kb.din("cdiag", [128, 2, 128], BF16)
    EE_d = kb.din("EE", [128, NT, 128], BF16)
    eaW_d = kb.din("eaW", [128, 2, 254])
    ovl_d = kb.din("ovl", [128, NCT, 128], BF16)
    w1kv_d = kb.din("w1kv", [128, 32, 256])
    posT_d = kb.din("posT", [128, 32])
    w2k_d = kb.din("w2k", [128, 2, 128])
    w2v_d = kb.din("w2v", [128, 2, 64])
    lqk_d = kb.din("lqk", [128, 4, 64])
    sublnB_d = kb.din("sublnB", [128, 128])
    oT = kb.dout("oT", [512, S], BF16) if o_piece is None else None
    qscr = kb.dscr("qscr", [7, 128, S], BF16)

    ident = kb.sb("ident", [128, 128], BF16)
    identN = kb.sb("identN", [128, 128], BF16)
    ones = kb.sb("ones", [128, 128], BF16)
    gA = kb.sb("gA_s", [128, 8], F32)
    epsT = kb.sb("epsT", [128, 1], F32)
    eps128 = kb.sb("eps128", [128, 1], F32)
    colc = kb.sb("colc_s", [128, 4], F32)
    cmpbias = kb.sb("cmpbias_s", [128, 5, 512], BF16)
    cdiag = kb.sb("cdiag_s", [128, 2, 128], BF16)
    eaW = kb.sb("eaW_s", [128, 2, 254], F32)
    sublnB = kb.sb("sublnB_s", [128, 128], F32)
    lqk = kb.sb("lqk_s", [128, 4, 64], F32)
    lsc = kb.sb("lsc", [128, 8], F32)
    kslc = kb.sb("kslc", [128, S], BF16)
    kb0 = kb.sb("kb0", [128, S], BF16)
    kb1 = kb.sb("kb1", [128, S], BF16)
    Vnsa = kb.sb("Vnsa", [128, NT, 130], BF16)
    Vd = kb.sb("Vd", [128, NT, 258], BF16)
    gates = kb.sb("gates", [128, NT, 12], F32)
    kcT = kb.sb("kcT", [128, NCW], BF16)
    Vc = kb.sb("Vc", [128, NCT, 193], BF16)
    kvT1 = kb.sb("kvT1", [128, S], BF16)
    EE = Tn(kb.nc.alloc_sbuf_tensor_at("EE_s" + kb.sfx, [128, NT, 128], BF16, offset=kb.off - 2 * S), "EE_s")
    PS = [kb.psum("ps%d" % i) for i in range(7)]
    PSB = kb.psum("psb", [128, 1024], BF16)
    mark = kb.off

    Wfm = kb.sb("Wfm", [128, 8, NFM * 128], BF16)
    Wtm = kb.sb("Wtm", [128, 8, 396], BF16)
    xg = kb.sb("xg", [128, 8, PG], F32)
    hT = kb.sb("hT", [128, 8, PG], BF16)
    tmpA = kb.sb("tmpA", [128, PG], F32)
    rstdB = kb.sb("rstdB", [128, PG], F32)
    posi = kb.sb("posi", [128, PG], I32)
    ang = kb.sb("ang", [128, PG], F32)
    ra = kb.sb("ra", [128, PG], F32)
    rk = kb.sb("rk", [128, PG], F32)
    rki = kb.sb("rki", [128, PG], I32)
    rfix = kb.sb("rfix", [128, PG], F32)
    cosT = kb.sb("cosT", [128, PG], F32)
    sinT = kb.sb("sinT", [128, PG], F32)
    t1 = kb.sb("t1", [128, PG], F32)
    t2 = kb.sb("t2", [128, PG], F32)
    qst = [kb.sb("qstP%d" % i, [128, 7, PG], BF16) for i in range(2)]
    P_list = [Wfm, Wtm, xg, hT, tmpA, rstdB, posi, ang, ra, rk, rki, rfix, cosT, sinT, t1, t2] + qst
    kb.alloc_log.append(EE)
    kb.phase_barrier()

    kb.memset("dve", ones[:, :], 1.0, [ones])
    kb.memset("dve", epsT[:, :], EPS, [epsT])
    kb.memset("dve", eps128[:, :], EPS, [eps128])
    kb.memset("pool", Vnsa[:, :, :], 1.0, [Vnsa])
    kb.memset("pool", Vd[:, :, :], 1.0, [Vd])
    for (dst, src) in ((ident, identb_d), (identN, identN_d), (gA, gA_d), (colc, colc_d), (cmpbias, cmpbias_d),
                       (cdiag, cdiag_d), (eaW, eaW_d), (sublnB, sublnB_d), (lqk, lqk_d)):
        kb.dma("q_sp", dst.t[:], src.t, [src], [dst])
    load_w(kb, "q_pool", Wfm, wfm_d, 8)
    load_w(kb, "q_pool", Wtm, wtm_d, 8)
    invf = colc[:, 0:1]
    sgn = colc[:, 1:2]
    kb.tt("dve", lqk[:, 0, :], lqk[:, 0, :], lqk[:, 1, :], ALU.mult, [lqk], [lqk])
    kb.tt("dve", lqk[:, 2, :], lqk[:, 2, :], lqk[:, 3, :], ALU.mult, [lqk], [lqk])
    kb.pr.add("dve", lambda e: e.reduce_sum(out=lsc[:, 0:1], in_=lqk[:, 0, :], axis=AX.X), [lqk], [lsc])
    kb.pr.add("dve", lambda e: e.reduce_sum(out=lsc[:, 1:2], in_=lqk[:, 2, :], axis=AX.X), [lqk], [lsc])
    kb.act(lsc[:, 2:4], lsc[:, 0:2], AF.Exp, [lsc], [lsc])
    kb.tt("dve", lsc[:, 4:5], lsc[:, 3:4], lsc[:, 2:3], ALU.subtract, [lsc], [lsc])
    kb.tt("dve", lsc[:, 4:5], lsc[:, 4:5], colc[:, 2:3], ALU.subtract, [lsc, colc], [lsc])
    neglam = lsc[:, 4:5]
    kb.ts("dve", sublnB[:, :], sublnB[:, :], colc[:, 3:4], None, ALU.mult, None, [sublnB, colc], [sublnB])

    if stop == 'C':
        return _fin(kb, own)
    xTv = xT.t.rearrange("(c p) t -> p c t", p=128) if xT is not None else None
    pi = [0]

    def nps():
        p = PS[pi[0] % 7]
        pi[0] += 1
        return p

    for gi in range(NPG):
        t0 = gi * PG
        if h_src is None:
            for c in range(8):
                kb.dma("q_sp", xg[:, c, :], xTv[:, c, t0:t0 + PG], [xT], [(xg, c)])
        kb.dma("q_sp", posi[:, :], posB_d[:, t0:t0 + PG], [posB_d], [posi])
        if h_src is None:
            rmsnorm_fm(kb, xg, hT, gA, ones, epsT, nps(), tmpA, rstdB, PG)
        else:
            Th = S // 2
            rr, col = t0 // Th, t0 % Th
            for c in range(8):
                kb.dma("q_sp", hT[:, c, :], h_src.t[c, rr * 128:(rr + 1) * 128, col:col + PG], [h_src], [(hT, c)])
        if stop == 'P1':
            return _fin(kb, own)
        kb.cp("dve", ang[:, :], posi[:, :], [posi], [ang])
        kb.ts("dve", ang[:, :], ang[:, :], invf, None, ALU.mult, None, [ang, colc], [ang])
        for which in range(2):
            dst = sinT if which == 0 else cosT
            if which == 0:
                src = ang
            else:
                kb.ts("dve", ra[:, :], ang[:, :], math.pi / 2, None, ALU.add, None, [ang], [ra])
                src = ra
            kb.ts("dve", rk[:, :], src[:, :], 1.0 / TWO_PI, None, ALU.mult, None, [src], [rk])
            kb.cp("dve", rki[:, :], rk[:, :], [rk], [rki])
            kb.cp("dve", rk[:, :], rki[:, :], [rki], [rk])
            kb.pr.add("dve", lambda e, src=src: e.scalar_tensor_tensor(out=rfix[:, :], in0=rk[:, :], scalar=-C1, in1=src[:, :],
                                                                       op0=ALU.mult, op1=ALU.add), [rk, src], [rfix])
            kb.pr.add("dve", lambda e: e.scalar_tensor_tensor(out=rfix[:, :], in0=rk[:, :], scalar=-C2, in1=rfix[:, :],
                                                              op0=ALU.mult, op1=ALU.add), [rk, rfix], [rfix])
            kb.ts("dve", rk[:, :], rfix[:, :], math.pi, -TWO_PI, ALU.is_gt, ALU.mult, [rfix], [rk])
            kb.tt("dve", rfix[:, :], rfix[:, :], rk[:, :], ALU.add, [rfix, rk], [rfix])
            kb.ts("dve", rk[:, :], rfix[:, :], -math.pi, TWO_PI, ALU.is_lt, ALU.mult, [rfix], [rk])
            kb.tt("dve", rfix[:, :], rfix[:, :], rk[:, :], ALU.add, [rfix, rk], [rfix])
            kb.ts("dve", rfix[:, :], rfix[:, :], math.pi, -math.pi, ALU.min, ALU.max, [rfix], [rfix])
            if which == 0:
                kb.act(dst[:, :], rfix[:, :], AF.Sin, [rfix, colc], [dst], scale=sgn)
            else:
                kb.act(dst[:, :], rfix[:, :], AF.Sin, [rfix], [dst])
        if stop == 'P2':
            return _fin(kb, own)
        qs = qst[gi % 2]
        for ch in range(9):
            pp = nps()
            for k in range(8):
                kb.mm(pp[:, :PG], Wfm[:, k, ch * 128:(ch + 1) * 128], hT[:, k, :], k == 0, k == 7, [(Wfm, k), hT], [pp])
            if ch in (0, 1):
                kb.cp("act", qs[:, ch, :], pp[:, :PG], [pp], [(qs, ch)])
            if ch == 2:
                kb.cp("act", kvT1[:, t0:t0 + PG], pp[:, :PG], [pp], [(kvT1, gi)])
                continue
            sw = FM_ROPE[ch]
            psw = nps()
            for k in range(8):
                kb.mm(psw[:, :PG], Wfm[:, k, sw * 128:(sw + 1) * 128], hT[:, k, :], k == 0, k == 7, [(Wfm, k), hT], [psw])
            kb.tt("dve", t1[:, :], pp[:, :PG], cosT[:, :], ALU.mult, [pp, cosT], [t1])
            kb.tt("dve", t2[:, :], psw[:, :PG], sinT[:, :], ALU.mult, [psw, sinT], [t2])
            if ch in (0, 1):
                dst, dk, dt_ = qs[:, 2 + ch, :], (qs, 2 + ch), qs
            elif ch == 3:
                dst, dk, dt_ = kslc[:, t0:t0 + PG], (kslc, gi), kslc
            elif ch == 4:
                dst, dk, dt_ = qs[:, 6, :], (qs, 6), qs
            elif ch in (5, 6):
                dst, dk, dt_ = qs[:, ch - 1, :], (qs, ch - 1), qs
            elif ch == 7:
                dst, dk, dt_ = kb0[:, t0:t0 + PG], (kb0, gi), kb0
            else:
                dst, dk, dt_ = kb1[:, t0:t0 + PG], (kb1, gi), kb1
            kb.tt("pool", dst, t1[:, :], t2[:, :], ALU.add, [t1, t2], [dk])
        if stop == 'P3':
            return _fin(kb, own)
        for j in range(7):
            kb.dma("q_sp", qscr[j, :, t0:t0 + PG], qs[:, j, :], [(qs, j)], [qscr])
        if stop == 'P4':
            return _fin(kb, own)
        for tt in range(PG // 128):
            T_ = gi * (PG // 128) + tt
            p = nps()
            for k in range(8):
                kb.mm(p[:, :396], hT[:, k, tt * 128:(tt + 1) * 128], Wtm[:, k, :], k == 0, k == 7, [hT, (Wtm, k)], [p])
            kb.cp("act", Vnsa[:, T_, 0:64], p[:, 0:64], [p], [(Vnsa, T_)])
            kb.cp("act", Vnsa[:, T_, 65:129], p[:, 64:128], [p], [(Vnsa, T_)])
            kb.act(gates[:, T_, :], p[:, 128:140], AF.Sigmoid, [p], [(gates, T_)])
            kb.cp("dve", Vd[:, T_, 0:128], p[:, 140:268], [p], [(Vd, T_)])
            kb.cp("dve", Vd[:, T_, 129:257], p[:, 268:396], [p], [(Vd, T_)])
        if stop == 'P5' or (stop == 'P6' and gi == 1):
            return _fin(kb, own)

    if stop == 'P':
        return _fin(kb, own)
    kb.off = mark
    w1kv = kb.sb("w1kv", [128, 32, 256], BF16)
    posT = kb.sb("posT", [128, 32], BF16)
    w2k = kb.sb("w2k", [128, 2, 128], BF16)
    w2v = kb.sb("w2v", [128, 2, 64], BF16)
    hidT = kb.sb("hidT", [128, 2, 2, NCW], BF16)
    posb = kb.sb("posb", [128, 4], F32)
    X_list = [w1kv, posT, w2k, w2v, hidT, posb]
    kb.barrier(P_list, X_list)
    for l4 in range(4):
        kb.dma("q_pool", w1kv[:, l4 * 8:(l4 + 1) * 8, :], w1kv_d[:, l4 * 8:(l4 + 1) * 8, :], [w1kv_d], [(w1kv, l4)])
    kb.dma("q_pool", posT[:, :], posT_d.t, [posT_d], [posT])
    kb.dma("q_pool", w2k[:, :, :], w2k_d.t, [w2k_d], [w2k])
    kb.dma("q_pool", w2v[:, :, :], w2v_d.t, [w2v_d], [w2v])
    kb.memset("pool", hidT[:, :, :, :], 0.0, [hidT])
    kvv = kvT1.t.reshape([128, NCP, 16])
    for which in range(2):
        r0 = 64 * which
        for half in range(2):
            ph = nps()
            for l in range(32):
                kb.mm(ph[:, :ncmp], w1kv[r0:r0 + 64, l, half * 128:(half + 1) * 128],
                      kvv[r0:r0 + 64, (l // 16):(l // 16) + ncmp, l % 16], l == 0, l == 31, [w1kv, kvT1], [ph])
            pb = nps()
            for l in range(32):
                kb.mm(pb[:, 0:1], w1kv[r0:r0 + 64, l, half * 128:(half + 1) * 128], posT[r0:r0 + 64, l:l + 1], l == 0, l == 31,
                      [w1kv, posT], [pb])
            idx = which * 2 + half
            kb.cp("dve", posb[:, idx:idx + 1], pb[:, 0:1], [pb], [(posb, idx)])
            kb.act(hidT[:, which, half, :ncmp], ph[:, :ncmp], AF.Gelu_apprx_tanh, [ph, (posb, idx)], [(hidT, idx)],
                   bias=posb[:, idx:idx + 1])
    pk = nps()
    for half in range(2):
        kb.mm(pk[:, :NCW], w2k[:, half, :], hidT[:, 0, half, :], half == 0, half == 1, [w2k, hidT], [pk])
    kb.cp("act", kcT[:, :], pk[:, :NCW], [pk], [kcT])
    kb.memset("pool", Vc[:, :, :], 1.0, [Vc])
    for nt in range(NCT):
        pv = nps()
        for half in range(2):
            kb.mm(pv[:, 0:64], hidT[:, 1, half, nt * 128:(nt + 1) * 128], w2v[:, half, :], half == 0, half == 1, [hidT, w2v], [pv])
        kb.cp("dve", Vc[:, nt, 0:64], pv[:, 0:64], [pv], [Vc])
    kb.dma("q_sp", Vc[:, :, 65:193], ovl_d.t, [ovl_d], [Vc])

    if stop == 'X':
        return _fin(kb, own)
    kb.off = mark
    qstA = [kb.sb("qstA%d" % i, [128, 6, QG], BF16) for i in range(2)]
    qzA = [kb.sb("qzA%d" % i, [128, 12, QG], BF16) for i in range(1)]
    kwst = [kb.sb("kwst%d" % i, [128, 1024], BF16) for i in range(2)]
    PT = [kb.sb("PT%d" % i, [128, 512], BF16) for i in range(4)]
    negT = kb.sb("negT", [128, 512], BF16)
    Oev = [kb.sb("Oev%d" % i, [128, 4, 193], F32) for i in range(2)]
    rc = [kb.sb("rc%d" % i, [128, 4], F32) for i in range(2)]
    coef = [kb.sb("coef%d" % i, [128, 4], F32) for i in range(2)]
    ocmp = kb.sb("ocmp", [128, 4, 4, 64], F32)
    oacc = kb.sb("oacc", [128, 4, 256], F32)
    obt = kb.sb("obt", [128, 4, 256], F32)
    imp = kb.sb("imp", [128, 4, 128], F32)
    score = kb.sb("score", [128, 4, 128], F32)
    sc2 = kb.sb("sc2", [128, 4, 128], F32)
    m8 = kb.sb("m8", [128, 4, 8], F32)
    neg01 = kb.sb("neg01", [128, 4, 128], BF16)
    od0 = kb.sb("od0", [128, 4, 128], F32)
    od1 = kb.sb("od1", [128, 4, 128], F32)
    djunk = kb.sb("djunk", [128, 128], F32)
    dss = kb.sb("dss", [128, 4], F32)
    o16 = kb.sb("o16", [128, 4, 512], BF16)
    oTs = kb.sb("oTs", [128, 4, 512], BF16)
    A_list = qzA + qstA + kwst + PT + [negT, ocmp, oacc, obt, imp, score, sc2, m8, neg01, od0, od1, djunk, dss, o16, oTs] + Oev + rc + coef
    kb.barrier(X_list + P_list + [kvT1], A_list + [EE])
    kb.dma("q_sp", EE.t[:], EE_d.t, [EE_d], [EE])
    for qzb in qzA:
        kb.memset("pool", qzb[:, :, :], 0.0, [qzb])
    LB = [PS[0], PS[1], PS[6]]
    DEPTH = 2
    ACC = [(PS[2], PS[3]), (PS[4], PS[5])]
    PSM = PS[6]
    li = [0]
    ai = [0]
    pti = [0]
    ei = [0]
    oTv = oT.t.rearrange("(c p) t -> p c t", p=128) if oT is not None else None

    def nL():
        li[0] += 1
        return LB[li[0] % 3]

    def nA():
        ai[0] += 1
        return ACC[ai[0] % 2]

    def nPT():
        pti[0] += 1
        return PT[pti[0] % 4]

    def nE():
        ei[0] += 1
        return Oev[ei[0] % 2], rc[ei[0] % 2], coef[ei[0] % 2]

    def evac(acc, w, nbank_q):
        ev, r_, cf = nE()
        nb = 4 // nbank_q
        for bnk in range(nb):
            kb.cp("dve", ev[:, bnk * nbank_q:(bnk + 1) * nbank_q, 0:w],
                  acc[bnk][:, 0:nbank_q * w].rearrange("p (q w) -> p q w", w=w), [acc[bnk]], [(ev, bnk)])
        sumcol = 64 if w in (65, 193) else 128
        kb.ts("dve", r_[:, :], ev[:, :, sumcol], 1e-30, None, ALU.max, None, [ev], [r_])
        kb.recip(r_[:, :], r_[:, :], [r_], [r_])
        return ev, r_, cf

    for Q in range(NQ):
        q0 = Q * QG
        qs = qstA[Q % 2]
        kw = kwst[Q % 2]
        for j in range(6):
            kb.dma("q_sp", qs[:, j, :], qscr[j, :, q0:q0 + QG], [qscr], [(qs, j)])
        qz = qzA[0]
        for j in range(6):
            for hf in range(2):
                kb.cp("pool", qz[64 * hf:64 * hf + 64, 2 * j + hf, :], qs[64 * hf:64 * hf + 64, j, :], [(qs, j)], [(qz, 2 * j + hf)])
        klo = max(0, q0 - 512)
        kb.dma("q_sp", kw[:, (klo - (q0 - 512)):1024], qscr[6, :, klo:q0 + 512], [qscr], [kw])
        for h in range(4):
            r0 = 64 * (h % 2)
            qa = qz[:, 2 * (h // 2) + h % 2, :]
            nts = [nt for nt in range(NCT) if Q - 4 * nt >= 0]
            acc = nA()
            def c_qk(ix, nt):
                Dd = Q - 4 * nt
                L = nL()
                kb.mm(L[:, :512], kcT[:, nt * 128:(nt + 1) * 128], qa, True, Dd > 4, [kcT, qz], [L])
                if Dd <= 4:
                    kb.mm(L[:, :512], identN[:, :], cmpbias[:, Dd, :], False, True, [identN, cmpbias], [L])
                pt = nPT()
                kb.act(pt[:, :], L[:, :512], AF.Exp, [L], [pt], scale=0.125)
                return pt

            def c_pv(ix, nt, pt):
                for qt in range(4):
                    kb.mm(acc[qt // 2][:, (qt % 2) * 193:(qt % 2) * 193 + 193], pt[:, qt * 128:(qt + 1) * 128], Vc[:, nt, :],
                          ix == 0 and qt % 2 == 0, ix == len(nts) - 1, [pt, Vc], [acc[qt // 2]])

            pend = []
            for ix, nt in enumerate(nts):
                pt = c_qk(ix, nt)
                pend.append((ix, nt, pt))
                if len(pend) > DEPTH:
                    c_pv(*pend.pop(0))
            while pend:
                c_pv(*pend.pop(0))
            ev, r_, cf = evac(acc, 193, 2)
            for qt in range(4):
                kb.ts("dve", ocmp[:, h, qt, :], ev[:, qt, 0:64], r_[:, qt:qt + 1], None, ALU.mult, None, [ev, r_], [(ocmp, h)])
                if h == 0:
                    kb.ts("dve", imp[:, qt, :], ev[:, qt, 65:193], r_[:, qt:qt + 1], None, ALU.mult, None, [ev, r_], [imp])
                else:
                    kb.stt(imp[:, qt, :], ev[:, qt, 65:193], r_[:, qt:qt + 1], imp[:, qt, :], ALU.mult, ALU.add, [ev, r_, imp], [imp])
        for qt in range(4):
            qta = 4 * Q + qt
            off = 126 - 2 * qta
            kb.tt("dve", score[:, qt, :], imp[:, qt, :], eaW[:, 0, off:off + 128], ALU.mult, [imp, eaW], [score])
            kb.tt("dve", score[:, qt, :], score[:, qt, :], eaW[:, 1, off:off + 128], ALU.add, [score, eaW], [score])
            kb.memset("dve", score[:, qt, 0:1], BIGS, [score])
            kb.pr.add("dve", lambda e, qt=qt: e.max(out=m8[:, qt, :], in_=score[:, qt, :]), [score], [m8])
            kb.pr.add("dve", lambda e, qt=qt: e.match_replace(out=sc2[:, qt, :], in_to_replace=m8[:, qt, :],
                                                              in_values=score[:, qt, :], imm_value=-3.0e38), [score, m8], [sc2])
            kb.pr.add("dve", lambda e, qt=qt: e.max(out=m8[:, qt, :], in_=sc2[:, qt, :]), [sc2], [m8])
            kb.ts("dve", neg01[:, qt, :], score[:, qt, :], m8[:, qt, 7:8], 1.0, ALU.is_ge, ALU.subtract, [score, m8], [neg01])
            kb.tr(PSB[:, qt * 128:(qt + 1) * 128], neg01[:, qt, :], ident[:, :], [neg01, ident], [PSB])
        kb.cp("dve", negT[:, :], PSB[:, 0:512], [PSB], [negT])
        for h in range(4):
            r0 = 64 * (h % 2)
            qr = qz[:, 2 * (2 + h // 2) + h % 2, :]
            acc = nA()
            firstb = True
            ilist = [i for i in range(8) if 4 * Q - 4 + i >= 0]
            def w_qk(i):
                qts = [qt for qt in range(4) if 0 <= 4 - i + qt <= 4]
                c0, c1 = qts[0] * 128, (qts[-1] + 1) * 128
                L = nL()
                kb.mm(L[:, c0:c1], kw[:, i * 128:(i + 1) * 128], qr[:, c0:c1], True, False, [kw, qz], [L])
                for qt in qts:
                    dd = 4 - i + qt
                    if dd == 0:
                        kb.mm(L[:, qt * 128:(qt + 1) * 128], identN[:, :], cdiag[:, 0, :], False, True, [identN, cdiag], [L])
                    elif dd == 4:
                        kb.mm(L[:, qt * 128:(qt + 1) * 128], identN[:, :], cdiag[:, 1, :], False, True, [identN, cdiag], [L])
                pt = nPT()
                kb.act(pt[:, c0:c1], L[:, c0:c1], AF.Exp, [L], [pt], scale=0.125)
                return qts, pt

            fb = [True]

            def w_pv(i, qts, pt):
                kt = 4 * Q - 4 + i
                for qt in qts:
                    kb.mm(acc[0][:, qt * 65:qt * 65 + 65], pt[:, qt * 128:(qt + 1) * 128], Vnsa[:, kt, 65:130],
                          fb[0], i == qt + 4, [pt, Vnsa], [acc[0]])
                    fb[0] = False

            pend = []
            for i in ilist:
                qts, pt = w_qk(i)
                pend.append((i, qts, pt))
                if len(pend) > DEPTH:
                    w_pv(*pend.pop(0))
            while pend:
                w_pv(*pend.pop(0))
            ev, r_, cf = evac(acc, 65, 4)
            for qt in range(4):
                qta = 4 * Q + qt
                kb.tt("dve", cf[:, qt:qt + 1], r_[:, qt:qt + 1], gates[:, qta, h * 3 + 2:h * 3 + 3], ALU.mult, [r_, gates], [cf])
                kb.ts("dve", oacc[:, qt, h * 64:(h + 1) * 64], ocmp[:, h, qt, :], gates[:, qta, h * 3:h * 3 + 1], None, ALU.mult, None,
                      [(ocmp, h), gates], [(oacc, h)])
                kb.stt(oacc[:, qt, h * 64:(h + 1) * 64], ev[:, qt, 0:64], cf[:, qt:qt + 1], oacc[:, qt, h * 64:(h + 1) * 64],
                       ALU.mult, ALU.add, [ev, cf, (oacc, h)], [(oacc, h)])

        def causal_unit(qap, kT, vfn, w, nbq, use_sel, qbuf):
            acc = nA()
            nkt = 4 * Q + 4
            ktn, vtn = kT_tn[0], vT_tn[0]

            def u_qk(kt):
                i = kt - 4 * Q
                c0 = max(i, 0) * 128
                L = nL()
                kb.mm(L[:, c0:512], kT[:, kt * 128:(kt + 1) * 128], qap[:, c0:512], True, False, [ktn, qbuf], [L])
                if use_sel:
                    kb.mm(L[:, c0:512], EE[:, kt, :], negT[:, c0:512], False, False, [EE, negT], [L])
                if i >= 0:
                    kb.mm(L[:, c0:c0 + 128], identN[:, :], cdiag[:, 0, :], False, True, [identN, cdiag], [L])
                pt = nPT()
                kb.act(pt[:, c0:512], L[:, c0:512], AF.Exp, [L], [pt], scale=0.125)
                return pt

            def u_pv(kt, pt):
                i = kt - 4 * Q
                for qt in range(max(i, 0), 4):
                    bnk, sl = qt // nbq, (qt % nbq) * w
                    kb.mm(acc[bnk][:, sl:sl + w], pt[:, qt * 128:(qt + 1) * 128], vfn(kt),
                          kt == 0 and qt % nbq == 0, kt == 4 * Q + qt, [pt, vtn], [acc[bnk]])

            pend = []
            for kt in range(nkt):
                pt = u_qk(kt)
                pend.append((kt, pt))
                if len(pend) > DEPTH:
                    u_pv(*pend.pop(0))
            while pend:
                u_pv(*pend.pop(0))
            return evac(acc, w, nbq)

        kT_tn = [None]
        vT_tn = [None]
        for hh in range(2):
            for m in range(2):
                mi = hh * 2 + m
                kT_tn[0] = kb0 if mi < 2 else kb1
                vT_tn[0] = Vd
                r0 = 64 * (mi % 2)
                qap = qz[:, 2 * (4 + mi // 2) + mi % 2, :]
                ev, r_, cf = causal_unit(qap, kT_tn[0][:, :], lambda kt, hh=hh: Vd[:, kt, hh * 129:hh * 129 + 129], 129, 2, False, qz)
                if m == 0:
                    for qt in range(4):
                        kb.ts("dve", od0[:, qt, :], ev[:, qt, 0:128], r_[:, qt:qt + 1], None, ALU.mult, None, [ev, r_], [od0])
                else:
                    for qt in range(4):
                        kb.ts("dve", od1[:, qt, :], ev[:, qt, 0:128], r_[:, qt:qt + 1], neglam, ALU.mult, ALU.mult, [ev, r_, lsc], [od1])
                        kb.tt("dve", od0[:, qt, :], od0[:, qt, :], od1[:, qt, :], ALU.add, [od0, od1], [od0])
                        kb.stt(djunk[:, :], od0[:, qt, :], 1.0, od0[:, qt, :], ALU.mult, ALU.mult, [od0], [djunk, dss],
                               accum_out=dss[:, qt:qt + 1])
                    kb.act(dss[:, :], dss[:, :], AF.Sqrt, [dss, eps128], [dss], bias=eps128[:, 0:1], scale=1.0 / 128)
                    kb.recip(dss[:, :], dss[:, :], [dss], [dss])
                    for qt in range(4):
                        kb.stt(obt[:, qt, hh * 128:(hh + 1) * 128], od0[:, qt, :], dss[:, qt:qt + 1], sublnB[:, :], ALU.mult, ALU.mult,
                               [od0, dss, sublnB], [(obt, hh)])
        for h in range(4):
            r0 = 64 * (h % 2)
            kT_tn[0] = kslc
            vT_tn[0] = Vnsa
            qap = qz[:, 2 * (2 + h // 2) + h % 2, :]
            ev, r_, cf = causal_unit(qap, kslc[:, :], lambda kt: Vnsa[:, kt, 0:65], 65, 4, True, qz)
            for qt in range(4):
                qta = 4 * Q + qt
                kb.tt("dve", cf[:, qt:qt + 1], r_[:, qt:qt + 1], gates[:, qta, h * 3 + 1:h * 3 + 2], ALU.mult, [r_, gates], [cf])
                kb.stt(oacc[:, qt, h * 64:(h + 1) * 64], ev[:, qt, 0:64], cf[:, qt:qt + 1], oacc[:, qt, h * 64:(h + 1) * 64],
                       ALU.mult, ALU.add, [ev, cf, (oacc, h)], [(oacc, h)])
        kb.cp("act", o16[:, :, 0:256], oacc[:, :, :], [oacc], [o16])
        kb.cp("act", o16[:, :, 256:512], obt[:, :, :], [obt], [o16])
        for c4 in range(4):
            for qt in range(4):
                kb.tr(PSB[:, 512 + qt * 128:512 + (qt + 1) * 128], o16[:, qt, c4 * 128:(c4 + 1) * 128], ident[:, :], [o16, ident], [(PSB, "o")])
            kb.cp("dve", oTs[:, c4, :], PSB[:, 512:1024], [(PSB, "o")], [(oTs, c4)])
            if o_piece is None:
                kb.dma("q_sp", oTv[:, c4, q0:q0 + QG], oTs[:, c4, :], [(oTs, c4)], [oT])
            else:
                hq = NQ // 2
                kb.dma("q_sp", o_piece.t[c4, Q // hq, :, (Q % hq) * QG:(Q % hq + 1) * QG], oTs[:, c4, :], [(oTs, c4)],
                       [(o_piece, (c4, Q // hq))])
        if stop is not None and stop.startswith('A') and int(stop[1:]) == Q:
            return _fin(kb, own)
    return _fin(kb, own)


def _swap64(cols):
    cols = np.asarray(cols).reshape(-1, 64)
    return np.concatenate([cols[:, 32:], cols[:, :32]], axis=1).reshape(-1)


def ab_consts(S):
    NT = S // 128
    NCP = S // 16
    ncmp = NCP - 1
    NCT = (NCP + 127) // 128
    bf = ml_dtypes.bfloat16
    c = {}
    c["identb"] = np.eye(128, dtype=np.float32).astype(bf)
    c["identN"] = (np.eye(128, dtype=np.float32) * NEGM).astype(bf)
    n_ = np.arange(128)[:, None, None]
    D_ = np.arange(5)[None, :, None]
    q_ = np.arange(512)[None, None, :]
    c["cmpbias"] = np.where(16 * n_ + 31 - q_ <= 512 * D_, 0.0, -1.0).astype(np.float32).astype(bf)
    k_ = np.arange(128)[:, None]
    qq = np.arange(128)[None, :]
    cd = np.stack([np.where(k_ <= qq, 0.0, -1.0), np.where(k_ > qq, 0.0, -1.0)], axis=1)
    c["cdiag"] = cd.astype(np.float32).astype(bf)
    j_ = np.arange(128)[:, None, None]
    kt_ = np.arange(NT)[None, :, None]
    kk = np.arange(128)[None, None, :]
    c["EE"] = np.where(j_ == 2 * kt_ + kk // 64, NEGM, 0.0).astype(np.float32).astype(bf)
    qp = np.arange(128)[:, None]
    rel = np.arange(254)[None, :] - 126
    cur = qp // 64
    elig = (rel <= cur).astype(np.float32)
    addw = np.where((rel == cur) | (rel == cur - 1), BIGS, np.where(rel > cur, -BIGS, 0.0)).astype(np.float32)
    c["eaW"] = np.ascontiguousarray(np.stack([elig, addw], axis=1))
    n = np.arange(NCT * 128)[:, None]
    j = np.arange(128)[None, :]
    ov = ((16 * n < 64 * j + 64) & (16 * n + 32 > 64 * j) & (n < ncmp)).astype(np.float32)
    c["ovl"] = np.ascontiguousarray(ov.reshape(NCT, 128, 128).transpose(1, 0, 2)).astype(bf)
    return c


def prep_ab(l, inp, b, g, S):
    w_in = inp["w_in"][l]
    d = {}
    d["posB"] = np.ascontiguousarray(np.broadcast_to(inp["positions"][b][None, :S], (128, S))).astype(np.int32)
    d["gA"] = _col8(inp["attn_norm"][l])
    r64 = np.arange(64)
    r128 = np.arange(128)
    plain = [256 * g + r128, 256 * g + 128 + r128,
             np.concatenate([512 + 64 * g + r64, 512 + 128 + 64 * g + r64]),
             np.concatenate([512 + 256 + 64 * g + r64] * 2),
             np.concatenate([512 + 512 + 64 * g + r64] * 2),
             1304 + 256 * g + r128, 1304 + 256 * g + 128 + r128,
             1816 + 256 * g + r128, 1816 + 256 * g + 128 + r128]
    swaps = [_swap64(plain[i]) for i in (0, 1, 3, 4, 5, 6, 7, 8)]
    cols = np.concatenate(plain + swaps)
    d["w_fm"] = np.ascontiguousarray(w_in[:, cols])
    tcols = np.concatenate([512 + 384 + 64 * g + r64, 512 + 640 + 64 * g + r64, 1280 + 12 * g + np.arange(12),
                            2328 + 256 * g + np.arange(256)])
    d["w_tm"] = np.ascontiguousarray(w_in[:, tcols])
    p = np.arange(128)
    lam_init = 0.8 - 0.6 * math.exp(-0.3 * l)
    colc = np.zeros((128, 4), np.float32)
    colc[:, 0] = (10000.0 ** (-(np.arange(32, dtype=np.float32)) / 32.0)).astype(np.float32)[p % 32]
    colc[:, 1] = np.where((p % 64) < 32, -1.0, 1.0)
    colc[:, 2] = lam_init
    colc[:, 3] = 1.0 - lam_init
    d["colc"] = colc
    w1k = inp["cmp_k_w1"][l].reshape(32, 64, 256).transpose(1, 0, 2)
    w1v = inp["cmp_v_w1"][l].reshape(32, 64, 256).transpose(1, 0, 2)
    d["w1kv"] = np.ascontiguousarray(np.concatenate([w1k, w1v], axis=0))
    d["posT"] = np.ascontiguousarray(np.concatenate([inp["cmp_pos_k"][l].T, inp["cmp_pos_v"][l].T], axis=0))
    w2k = inp["cmp_k_w2"][l].reshape(2, 128, 64).transpose(1, 0, 2)
    d["w2k"] = np.ascontiguousarray(np.concatenate([w2k, w2k], axis=2))
    d["w2v"] = np.ascontiguousarray(inp["cmp_v_w2"][l].reshape(2, 128, 64).transpose(1, 0, 2))
    lq = np.stack([inp["diff_lq1"][l], inp["diff_lk1"][l], inp["diff_lq2"][l], inp["diff_lk2"][l]], axis=0)
    d["lqk"] = np.ascontiguousarray(np.broadcast_to(lq[None], (128, 4, 64)))
    d["sublnB"] = np.ascontiguousarray(np.broadcast_to(inp["diff_subln"][l][None, :], (128, 128)))
    return d


_CACHE = {}


def _prog(name, fn, *args):
    key = (name,) + args
    if key not in _CACHE:
        _CACHE[key] = fn(*args)
    return _CACHE[key]


def kernel_unfused(**inputs):
    inp = {k: np.asarray(v) for k, v in inputs.items()}
    x = inp["x"].astype(np.float32, copy=False)
    B, S, _ = x.shape
    T = S // 2
    L = inp["w_in"].shape[0]
    cores = list(range(8))
    cst = ab_consts(S)
    xT = [np.ascontiguousarray(x[b].T) for b in range(B)]
    out = None
    for l in range(L):
        nc_ab = build_ab(S)
        maps = []
        for c in cores:
            b, g = c // 2, c % 2
            d = prep_ab(l, inp, b, g, S)
            d.update(cst)
            d["xT"] = xT[b]
            maps.append(d)
        res = run_bass_kernel_spmd(nc_ab, maps, core_ids=cores).results
        oaT = [np.concatenate([res[2 * b]["oT"][0:256], res[2 * b + 1]["oT"][0:256]], axis=0) for b in range(B)]
        obT = [np.concatenate([res[2 * b]["oT"][256:512], res[2 * b + 1]["oT"][256:512]], axis=0) for b in range(B)]
        del res, maps
        nc_c1 = build_c1(T)
        maps = []
        p1 = prep_c1(l, inp)
        for c in cores:
            b, g = c // 2, c % 2
            d = dict(p1)
            d["xT"] = np.ascontiguousarray(xT[b][:, g * T:(g + 1) * T])
            d["oaT"] = np.ascontiguousarray(oaT[b][:, g * T:(g + 1) * T])
            d["obT"] = np.ascontiguousarray(obT[b][:, g * T:(g + 1) * T])
            maps.append(d)
        res = run_bass_kernel_spmd(nc_c1, maps, core_ids=cores).results
        x1T = [res[c]["x1T"] for c in cores]
        del res, maps
        nc_c2 = build_c2(T)
        p2 = prep_c2(l, inp)
        maps = []
        for c in cores:
            d = dict(p2)
            d["x1T"] = x1T[c]
            maps.append(d)
        res = run_bass_kernel_spmd(nc_c2, maps, core_ids=cores).results
        xT = [np.concatenate([res[2 * b]["x2T"], res[2 * b + 1]["x2T"]], axis=1) for b in range(B)]
        if l == L - 1:
            out = np.stack([np.concatenate([res[2 * b]["x2nT"], res[2 * b + 1]["x2nT"]], axis=1).T for b in range(B)], axis=0)
        del res, maps
    return np.ascontiguousarray(out.astype(np.float32))


def build_fused(S, L=2):
    kb = KB()
    xin = kb.din("xT_in", [D, S])
    outT = kb.dout("outT", [D, S])
    xcur = xin
    for l in range(L):
        kb.sfx = "_l%d" % l
        oscr = kb.dscr("oscr", [2, 512, S], BF16)
        for g in range(2):
            ov = Tn(oscr.t[g], "oscr_g")
            ov.b = oscr.b
            build_ab(S, kb=kb, io={"xT": xcur, "oT": ov}, sfx="_l%dg%d" % (l, g))
        kb.sfx = "_l%d" % l
        x1 = kb.dscr("x1scr", [D, S], F32)

        def osrc_fn(which, c, t0, G, oscr=oscr):
            r0 = 256 * which + (c % 2) * 128
            return oscr.t[c // 2, r0:r0 + 128, t0:t0 + G]

        build_c1(S, kb=kb, io={"xT": xcur, "x1T": x1, "osrc_fn": osrc_fn, "osrc_tn": oscr}, sfx="_l%d" % l)
        if l < L - 1:
            kb.sfx = "_l%d" % l
            x2 = kb.dscr("x2scr", [D, S], F32)
            build_c2(S, kb=kb, io={"x1T": x1, "x2T": x2, "want_x2": True, "want_x2n": False}, sfx="_l%d" % l)
            xcur = x2
        else:
            build_c2(S, kb=kb, io={"x1T": x1, "x2nT": outT, "want_x2": False, "want_x2n": True}, sfx="_l%d" % l)
    return kb.finish()


def fused_inputs(inp, b, S, cst):
    L = inp["w_in"].shape[0]
    d = dict(cst)
    d["cm01"] = np.triu(np.ones((128, 128), np.float32))
    d["xT_in"] = np.ascontiguousarray(inp["x"][b].T)
    for l in range(L):
        for g in range(2):
            for k, v in prep_ab(l, inp, b, g, S).items():
                if k == "posB":
                    d["posB"] = v
                else:
                    d["%s_l%dg%d" % (k, l, g)] = v
        for k, v in prep_c1(l, inp).items():
            if k != "cm01":
                d["%s_l%d" % (k, l)] = v
        for k, v in prep_c2(l, inp).items():
            d["%s_l%d" % (k, l)] = v
    return d


def kernel_fused4(**inputs):
    inp = {k: np.asarray(v) for k, v in inputs.items()}
    B, S, _ = inp["x"].shape
    nc = build_fused(S)
    cst = ab_consts(S)
    per_b = [fused_inputs(inp, b, S, cst) for b in range(B)]
    maps = [per_b[c % B] for c in range(8)]
    res = run_bass_kernel_spmd(nc, maps, core_ids=list(range(8))).results
    out = np.stack([res[b]["outT"].T for b in range(B)], axis=0)
    return np.ascontiguousarray(out.astype(np.float32))


RG_PAIRS = [[0, 1], [2, 3], [4, 5], [6, 7]]


def _allgather(kb, src, dst, key):
    kb.pr.add("q_cc", lambda e: e.collective_compute("AllGather", ALU.bypass, replica_groups=RG_PAIRS,
                                                     ins=[src.opt()], outs=[dst.opt()]), [key[0]], [key[1]])


def build_fused8(S, L=2):
    kb = KB()
    T = S // 2
    xfull = kb.din("xT_in", [D, S])
    xhalf = kb.din("xT_half", [D, T])
    outT = kb.dout("outT", [D, T])
    hgath = None
    xown = xhalf
    for l in range(L):
        kb.sfx = "_l%d" % l
        opc = kb.dscr("opc", [4, 2, 128, T], BF16)
        ogath = kb.dscr("ogath", [4, 2, 256, T], BF16)
        io = {"o_piece": opc}
        if hgath is None:
            io["xT"] = xfull
        else:
            io["h_src"] = hgath
        build_ab(S, kb=kb, io=io, sfx="_l%d" % l)
        for c4 in range(4):
            for half in range(2):
                _allgather(kb, opc.t[c4, half], ogath.t[c4, half], ((opc, (c4, half)), (ogath, (c4, half))))
        kb.sfx = "_l%d" % l
        x1 = kb.dscr("x1scr", [D, T], F32)
        build_c1(T, kb=kb, io={"xT": xown, "x1T": x1, "o_gath": ogath}, sfx="_l%d" % l)
        if l < L - 1:
            kb.sfx = "_l%d" % l
            x2 = kb.dscr("x2scr", [D, T], F32)
            hpc = kb.dscr("hpc", [8, 128, T], BF16)
            hg = kb.dscr("hgath", [8, 256, T], BF16)
            build_c2(T, kb=kb, io={"x1T": x1, "x2T": x2, "want_x2": True, "want_x2n": False, "h_next": hpc}, sfx="_l%d" % l)
            for c in range(8):
                _allgather(kb, hpc.t[c], hg.t[c], ((hpc, c), (hg, c)))
            hgath = hg
            xown = x2
        else:
            build_c2(T, kb=kb, io={"x1T": x1, "x2nT": outT, "want_x2": False, "want_x2n": True}, sfx="_l%d" % l)
    return kb.finish()


def fused8_inputs(inp, b, g, S, cst):
    L = inp["w_in"].shape[0]
    T = S // 2
    d = dict(cst)
    d["cm01"] = np.triu(np.ones((128, 128), np.float32))
    xT = np.ascontiguousarray(inp["x"][b].T)
    d["xT_in"] = xT
    d["xT_half"] = np.ascontiguousarray(xT[:, g * T:(g + 1) * T])
    m = np.zeros((128, 2), np.float32)
    m[:, g] = 1.0
    for l in range(L):
        for k, v in prep_ab(l, inp, b, g, S).items():
            if k == "posB":
                d["posB"] = v
            elif not (l > 0 and k == "gA"):
                d["%s_l%d" % (k, l)] = v
        for k, v in prep_c1(l, inp).items():
            if k != "cm01":
                d["%s_l%d" % (k, l)] = v
        d["msel_l%d" % l] = m
        for k, v in prep_c2(l, inp).items():
            d["%s_l%d" % (k, l)] = v
        if l < L - 1:
            d["gN_l%d" % l] = _col8(inp["attn_norm"][l + 1])
    return d


def kernel(**inputs):
    inp = {k: np.asarray(v) for k, v in inputs.items()}
    B, S, _ = inp["x"].shape
    T = S // 2
    nc = build_fused8(S)
    cst = ab_consts(S)
    maps = [fused8_inputs(inp, c // 2, c % 2, S, cst) for c in range(8)]
    res = run_bass_kernel_spmd(nc, maps, core_ids=list(range(8))).results
    out = np.stack([np.concatenate([res[2 * b]["outT"].T, res[2 * b + 1]["outT"].T], axis=0) for b in range(B)], axis=0)
    return np.ascontiguousarray(out.astype(np.float32))
```

```python
import contextlib
import math
import numpy as np
import ml_dtypes
import concourse.bass as bass
import concourse.mybir as mybir
from concourse.bass_utils import run_bass_kernel_spmd

F32 = mybir.dt.float32
BF16 = mybir.dt.bfloat16
I32 = mybir.dt.int32
AF = mybir.ActivationFunctionType
ALU = mybir.AluOpType
AX = mybir.AxisListType

D = 1024
DFF = 2816
EPS = 1e-6
NEGM = 32768.0

COMPUTE = ("pe", "act", "dve", "pool")
DMAQ = {"q_sp": "sp", "q_pool": "pool", "q_act": "act", "q_cc": "pool"}
QINC = {"q_sp": 16, "q_pool": 16, "q_act": 16, "q_cc": 1}
NSEM_PER_Q = 10


class Buf:
    __slots__ = ("name", "w", "r", "excl")

    def __init__(self, name):
        self.name = name
        self.w = {}
        self.r = {}
        self.excl = False


class Tn:
    def __init__(self, t, name):
        self.t = t
        self.b = Buf(name)

    def __getitem__(self, idx):
        return self.t[idx]


def _norm(lst):
    out = []
    for x in lst:
        if isinstance(x, tuple):
            b, k = x
        else:
            b, k = x, None
        if isinstance(b, Tn):
            b = b.b
        out.append((b, k))
    return out


class Prog:
    def __init__(self, nc):
        self.nc = nc
        self.ops = []

    def add(self, stream, fn, reads=(), writes=()):
        i = len(self.ops)
        deps = {}

        def ck(d, k):
            if k is None:
                return list(d.keys())
            return [kk for kk in (k, None) if kk in d]

        reads = _norm(reads)
        writes = _norm(writes)
        writes = writes + [(b, k) for (b, k) in reads if b.excl and (b, k) not in writes]
        for (b, k) in reads:
            for kk in ck(b.w, k):
                deps[b.w[kk]] = True
        for (b, k) in writes:
            for kk in ck(b.w, k):
                deps.setdefault(b.w[kk], False)
            for kk in ck(b.r, k):
                for s, j in b.r[kk].items():
                    if isinstance(j, list):
                        for jj in j:
                            deps.setdefault(jj, False)
                    else:
                        deps.setdefault(j, False)
        for (b, k) in reads:
            d = b.r.setdefault(k, {})
            if stream in DMAQ:
                d.setdefault(stream, [])
                d[stream].append(i)
            else:
                d[stream] = i
        for (b, k) in writes:
            if k is None:
                b.w = {None: i}
                b.r = {}
            else:
                b.w[k] = i
                b.r.pop(k, None)
        deps.pop(i, None)
        self.ops.append(dict(stream=stream, fn=fn, deps=deps, sig=None))
        return i

    def emit(self, es):
        nc = self.nc
        ops = self.ops
        need = [[] for _ in ops]
        for c, o in enumerate(ops):
            cs = o["stream"]
            best = {}
            for p, raw in o["deps"].items():
                ps = ops[p]["stream"]
                if ps in DMAQ:
                    need[c].append(p)
                    continue
                if ps == cs:
                    if cs == "pe" or not raw:
                        continue
                if ps not in best or best[ps] < p:
                    best[ps] = p
            need[c].extend(best.values())
        signal = [False] * len(ops)
        for c in range(len(ops)):
            for p in need[c]:
                signal[p] = True
        sems = {}
        for s in COMPUTE:
            sems[s] = es.enter_context(nc.semaphore("s_" + s))
        qsems = {}
        for q in DMAQ:
            qsems[q] = [es.enter_context(nc.semaphore("s_%s_%d" % (q, j))) for j in range(NSEM_PER_Q)]
        cnt = {s: 0 for s in COMPUTE}
        qcnt = {q: 0 for q in DMAQ}
        qsemcnt = {q: [0] * NSEM_PER_Q for q in DMAQ}
        for i, o in enumerate(ops):
            s = o["stream"]
            if s in DMAQ:
                j = qcnt[s] % NSEM_PER_Q
                qcnt[s] += 1
                qsemcnt[s][j] += 1
                o["sig"] = (qsems[s][j], QINC[s] * qsemcnt[s][j], (s, j))
                o["prev"] = (qsems[s][j], QINC[s] * (qsemcnt[s][j] - 1), (s, j))
            elif signal[i]:
                cnt[s] += 1
                o["sig"] = (sems[s], cnt[s], s)
        per_eng = {e: [] for e in ("pe", "act", "dve", "pool", "sp")}
        for i, o in enumerate(ops):
            s = o["stream"]
            per_eng[DMAQ.get(s, s)].append(i)
        waited = {e: {} for e in per_eng}

        def run_engine(ename, eng):
            wd = waited[ename]
            for i in per_eng[ename]:
                o = ops[i]
                ws = []
                for p in need[i]:
                    ws.append(ops[p]["sig"])
                if o["stream"] in DMAQ:
                    if o["prev"][1] > 0:
                        ws.append(o["prev"])
                for sem, val, key in ws:
                    if wd.get(key, 0) >= val:
                        continue
                    wd[key] = val
                    eng.wait_ge(sem, val)
                ins = o["fn"](eng)
                if o["sig"] is not None:
                    sem, val, key = o["sig"]
                    ins.then_inc(sem, QINC.get(o["stream"], 1))
            for q, e in DMAQ.items():
                if e != ename:
                    continue
                for j in range(NSEM_PER_Q):
                    v = QINC[q] * qsemcnt[q][j]
                    if v > 0 and wd.get((q, j), 0) < v:
                        eng.wait_ge(qsems[q][j], v)

        with nc.Block() as block:
            @block.tensor
            def _(e):
                run_engine("pe", e)

            @block.scalar
            def _(e):
                run_engine("act", e)

            @block.vector
            def _(e):
                run_engine("dve", e)

            @block.gpsimd
            def _(e):
                run_engine("pool", e)

            @block.sync
            def _(e):
                run_engine("sp", e)


class KB:
    def __init__(self):
        self.nc = bass.Bass("TRN2", target_bir_lowering=False)
        self.es = contextlib.ExitStack()
        self.pr = Prog(self.nc)
        self.off = 16896
        self.maxoff = 0
        self.sfx = ""
        self.io = {}
        self.uid = 0
        self.alloc_log = []
        self.prev_list = []
        self.dram = {}
        self.psums = {}
        self.dummy = self.sb("dummy", [128, 16], F32)
        self.base = self.off

    SHARED = ("posB", "identb", "identN", "cmpbias", "cdiag", "EE", "eaW", "ovl", "cm01")

    def begin_phase(self, sfx):
        self.sfx = sfx
        self.off = self.base
        self.alloc_log = []

    def phase_barrier(self):
        if self.prev_list:
            self.barrier(self.prev_list, list(self.alloc_log))

    def end_phase(self):
        self.prev_list = list(self.alloc_log)

    def sb(self, name, shape, dt):
        nb = 4 if dt in (F32, I32) else 2
        size = nb
        for s_ in shape[1:]:
            size *= s_
        size = (size + 63) // 64 * 64
        off = self.off
        self.off += size
        self.maxoff = max(self.maxoff, self.off)
        assert self.off <= 229376 - 256, ("SBUF overflow", name, self.off)
        self.uid += 1
        name = "%s%s_%d" % (name, self.sfx, self.uid)
        t = Tn(self.nc.alloc_sbuf_tensor_at(name, list(shape), dt, offset=off), name)
        self.alloc_log.append(t)
        return t

    def barrier(self, old, new):
        d = self.dummy
        self.pr.add("dve", lambda e: e.memset(d[:, :], 0.0), list(old), list(new) + list(old))

    def psum(self, name, shape=(128, 512), dt=F32):
        if name in self.psums:
            return self.psums[name]
        t = Tn(self.es.enter_context(self.nc.psum_tensor(name, list(shape), dt)), name)
        t.b.excl = True
        self.psums[name] = t
        return t

    def din(self, name, shape, dt=F32):
        if name in self.io:
            return self.io[name]
        if name not in self.SHARED:
            name = name + self.sfx
        if name in self.dram:
            return self.dram[name]
        t = self.nc.dram_tensor(name, list(shape), dt, kind="ExternalInput")
        self.dram[name] = Tn(t.ap(), name)
        return self.dram[name]

    def dout(self, name, shape, dt=F32):
        if name in self.io:
            return self.io[name]
        t = self.nc.dram_tensor(name + self.sfx, list(shape), dt, kind="ExternalOutput")
        return Tn(t.ap(), name)

    def dscr(self, name, shape, dt):
        t = self.nc.dram_tensor(name + self.sfx, list(shape), dt, kind="Internal")
        return Tn(t.ap(), name)

    def dma(self, q, out, in_, r, w):
        self.pr.add(q, lambda e: e.dma_start(out=out, in_=in_), r, w)

    def mm(self, out, lhsT, rhs, start, stop, r, w):
        self.pr.add("pe", lambda e: e.matmul(out, lhsT, rhs, start=start, stop=stop, skip_group_check=True), r, w)

    def tr(self, out, in_, ident, r, w):
        self.pr.add("pe", lambda e: e.transpose(out, in_, ident), r, w)

    def act(self, out, in_, func, r, w, bias=None, scale=None):
        kw = {}
        if bias is not None:
            kw["bias"] = bias
        if scale is not None:
            kw["scale"] = scale
        self.pr.add("act", lambda e: e.activation(out=out, in_=in_, func=func, **kw), r, w)

    def ts(self, eng, out, in0, s1, s2, op0, op1, r, w, accum_out=None):
        if op1 is None:
            self.pr.add(eng, lambda e: e.tensor_scalar(out=out, in0=in0, scalar1=s1, scalar2=None, op0=op0), r, w)
        elif accum_out is not None:
            self.pr.add(eng, lambda e: e.tensor_scalar(out=out, in0=in0, scalar1=s1, scalar2=s2, op0=op0, op1=op1,
                                                      accum_out=accum_out), r, w)
        else:
            self.pr.add(eng, lambda e: e.tensor_scalar(out=out, in0=in0, scalar1=s1, scalar2=s2, op0=op0, op1=op1), r, w)

    def tt(self, eng, out, in0, in1, op, r, w):
        self.pr.add(eng, lambda e: e.tensor_tensor(out=out, in0=in0, in1=in1, op=op), r, w)

    def stt(self, out, in0, scalar, in1, op0, op1, r, w, accum_out=None):
        if accum_out is None:
            self.pr.add("dve", lambda e: e.scalar_tensor_tensor(out=out, in0=in0, scalar=scalar, in1=in1, op0=op0, op1=op1), r, w)
        else:
            self.pr.add("dve", lambda e: e.scalar_tensor_tensor(out=out, in0=in0, scalar=scalar, in1=in1, op0=op0, op1=op1,
                                                                accum_out=accum_out), r, w)

    def cp(self, eng, out, in_, r, w):
        if eng == "act":
            self.pr.add("act", lambda e: e.copy(out=out, in_=in_), r, w)
        else:
            self.pr.add(eng, lambda e: e.tensor_copy(out=out, in_=in_), r, w)

    def recip(self, out, in_, r, w):
        self.pr.add("dve", lambda e: e.reciprocal(out=out, in_=in_), r, w)

    def memset(self, eng, ap, val, w):
        self.pr.add(eng, lambda e: e.memset(ap, val), (), w)

    def finish(self):
        self.pr.emit(self.es)
        self.es.close()
        return self.nc


def _fin(kb, own):
    kb.end_phase()
    kb.io = {}
    if own:
        return kb.finish()
    return None


def load_w(kb, q, dst, src_ap, kchunks, r=(), extra_w=()):
    v = src_ap.t.rearrange("(c p) n -> p c n", p=128)
    for c in range(kchunks):
        kb.dma(q, dst[:, c, :], v[:, c, :], [src_ap] + list(r), [(dst, c)] + list(extra_w))


def rmsnorm_fm(kb, xg, hT, gcol, ones, epsT, ps_ss, tmpA, rstdB, G=512):
    kb.tt("pool", hT[:, :, :], xg[:, :, :], xg[:, :, :], ALU.mult, [xg], [hT])
    for c in range(8):
        kb.mm(ps_ss[:, :G], ones[:, :], hT[:, c, :], c == 0, c == 7, [ones, hT], [ps_ss])
    kb.act(tmpA[:, :G], ps_ss[:, :G], AF.Sqrt, [ps_ss, epsT], [tmpA], bias=epsT[:, 0:1], scale=1.0 / D)
    kb.recip(rstdB[:, :G], tmpA[:, :G], [tmpA], [rstdB])
    for c in range(8):
        kb.stt(hT[:, c, :], xg[:, c, :], gcol[:, c:c + 1], rstdB[:, :G], ALU.mult, ALU.mult, [xg, gcol, rstdB], [(hT, c)])


def build_c1(T, kb=None, io=None, sfx=""):
    own = kb is None
    if own:
        kb = KB()
    kb.io = io or {}
    kb.begin_phase(sfx)
    G = 512
    NG = T // G
    xT = kb.din("xT", [D, T])
    osrc_fn = kb.io.get("osrc_fn")
    osrc_tn = kb.io.get("osrc_tn")
    o_gath = kb.io.get("o_gath")
    msel_d = kb.din("msel", [128, 2]) if o_gath is not None else None
    if osrc_fn is None and o_gath is None:
        oaT = kb.din("oaT", [512, T], BF16)
        obT = kb.din("obT", [512, T], BF16)
        oaTv = oaT.t.rearrange("(c p) t -> p c t", p=128)
        obTv = obT.t.rearrange("(c p) t -> p c t", p=128)
    gA_d = kb.din("gA", [128, 8])
    wsgu_d = kb.din("w_sgu", [D, 1024])
    gB_d = kb.din("sgu_gB", [128, 512])
    swT_d = kb.din("sgu_wT", [128, 4, 128])
    sbB_d = kb.din("sgu_bB", [128, 4, 128])
    cm_d = kb.din("cm01", [128, 128])
    wbr_d = kb.din("w_br", [1536, D])
    wm_d = kb.din("w_merge", [D, 3072])
    bm_d = kb.din("bm", [128, 24])
    wo_d = kb.din("w_out", [D, D])
    x1T = kb.dout("x1T", [D, T])

    Wsgu = kb.sb("Wsgu", [128, 8, 1024], BF16)
    Wbr = kb.sb("Wbr", [128, 12, 1024], BF16)
    Wm = kb.sb("Wm", [128, 8, 3072], BF16)
    Wo = kb.sb("Wo", [128, 8, 1024], BF16)
    gA = kb.sb("gA_s", [128, 8], F32)
    gB = kb.sb("gB_s", [128, 512], F32)
    swT = kb.sb("swT_s", [128, 4, 128], F32)
    swTb = kb.sb("swTb", [128, 4, 128], BF16)
    sbB = kb.sb("sbB_s", [128, 4, 128], F32)
    cm = kb.sb("cm_s", [128, 128], F32)
    bm = kb.sb("bm_s", [128, 24], F32)
    ones = kb.sb("ones", [128, 128], BF16)
    epsT = kb.sb("epsT", [128, 1], F32)
    eps512 = kb.sb("eps512", [128, 1], F32)

    xg = kb.sb("xg", [128, 8, G], F32)
    hT = kb.sb("hT", [128, 8, G], BF16)
    tmpA = kb.sb("tmpA", [128, G], F32)
    rstdB = kb.sb("rstdB", [128, G], F32)
    uT = kb.sb("uT", [128, 4, G], BF16)
    vg = kb.sb("vg", [128, 512], F32)
    vjunk = kb.sb("vjunk", [128, 512], F32)
    vn = kb.sb("vn", [128, 512], BF16)
    ssv = kb.sb("ssv", [128, 4], F32)
    ocT = kb.sb("ocT", [128, 4, G], BF16)
    stmp = kb.sb("stmp", [128, G], F32)
    oa = kb.sb("oa", [128, 4, G], BF16)
    ob = kb.sb("ob", [128, 4, G], BF16)
    gate = [kb.sb("gate%d" % i, [128, G], F32) for i in range(2)]
    acc = kb.sb("acc", [128, G], F32)
    mixedT = kb.sb("mixedT", [128, 8, G], BF16)
    ost = [kb.sb("ost%d" % i, [128, 2, G], BF16) for i in range(2)]
    msel = kb.sb("msel_s", [128, 2], F32)
    sti = [0]
    PS = [kb.psum("ps%d" % i) for i in range(7)]
    kb.phase_barrier()
    if msel_d is not None:
        kb.dma("q_sp", msel.t[:], msel_d.t, [msel_d], [msel])

    kb.memset("dve", ones[:, :], 1.0, [ones])
    kb.memset("dve", epsT[:, :], EPS, [epsT])
    kb.memset("dve", eps512[:, :], EPS, [eps512])
    for (dst, src) in ((gA, gA_d), (gB, gB_d), (sbB, sbB_d), (cm, cm_d), (bm, bm_d), (swT, swT_d)):
        kb.dma("q_sp", dst.t[:], src.t, [src], [dst])
    for g4 in range(4):
        kb.tt("dve", swTb[:, g4, :], swT[:, g4, :], cm[:, :], ALU.mult, [swT, cm], [(swTb, g4)])
    load_w(kb, "q_pool", Wsgu, wsgu_d, 8)
    load_w(kb, "q_pool", Wm, wm_d, 8)
    load_w(kb, "q_pool", Wbr, wbr_d, 12)
    load_w(kb, "q_pool", Wo, wo_d, 8)

    xTv = xT.t.rearrange("(c p) t -> p c t", p=128)
    x1Tv = x1T.t.rearrange("(c p) t -> p c t", p=128)
    pi = [0]

    def nps():
        p = PS[pi[0] % 3]
        pi[0] += 1
        return p

    for gi in range(NG):
        t0 = gi * G
        for c in range(8):
            kb.dma("q_sp", xg[:, c, :], xTv[:, c, t0:t0 + G], [xT], [(xg, c)])
        for c in range(4):
            if osrc_fn is None and o_gath is None:
                kb.dma("q_sp", oa[:, c, :], oaTv[:, c, t0:t0 + G], [oaT], [(oa, c)])
                kb.dma("q_sp", ob[:, c, :], obTv[:, c, t0:t0 + G], [obT], [(ob, c)])
            elif o_gath is None:
                kb.dma("q_sp", oa[:, c, :], osrc_fn(0, c, t0, G), [osrc_tn], [(oa, c)])
                kb.dma("q_sp", ob[:, c, :], osrc_fn(1, c, t0, G), [osrc_tn], [(ob, c)])
            else:
                for which, dst in ((0, oa), (1, ob)):
                    c4 = 2 * which + c % 2
                    r = c // 2
                    st = ost[sti[0] % 2]
                    sti[0] += 1
                    for half in range(2):
                        kb.dma("q_sp", st[:, half, :], o_gath.t[c4, half, r * 128:(r + 1) * 128, t0:t0 + G], [o_gath], [(st, half)])
                    kb.ts("dve", dst[:, c, :], st[:, 0, :], msel[:, 0:1], None, ALU.mult, None, [st, msel], [(dst, c)])
                    kb.stt(dst[:, c, :], st[:, 1, :], msel[:, 1:2], dst[:, c, :], ALU.mult, ALU.add, [st, msel, (dst, c)], [(dst, c)])
        rmsnorm_fm(kb, xg, hT, gA, ones, epsT, nps(), tmpA, rstdB, G)
        for uc in range(4):
            p = nps()
            for k in range(8):
                kb.mm(p[:, :G], Wsgu[:, k, uc * 128:(uc + 1) * 128], hT[:, k, :], k == 0, k == 7, [(Wsgu, k), hT], [p])
            kb.act(uT[:, uc, :], p[:, :G], AF.Gelu_apprx_tanh, [p], [(uT, uc)])
        ps_s = PS[3:7]
        for tt in range(4):
            p = nps()
            for k in range(8):
                kb.mm(p[:, :512], hT[:, k, tt * 128:(tt + 1) * 128], Wsgu[:, k, 512:1024], k == 0, k == 7, [hT, (Wsgu, k)], [p])
            kb.act(vg[:, :], p[:, :512], AF.Gelu_apprx_tanh, [p], [vg])
            kb.stt(vjunk[:, :], vg[:, :], 1.0, vg[:, :], ALU.mult, ALU.mult, [vg], [vjunk, (ssv, tt)], accum_out=ssv[:, tt:tt + 1])
            kb.act(ssv[:, tt:tt + 1], ssv[:, tt:tt + 1], AF.Sqrt, [(ssv, tt), eps512], [(ssv, tt)], bias=eps512[:, 0:1], scale=1.0 / 512)
            kb.recip(ssv[:, tt:tt + 1], ssv[:, tt:tt + 1], [(ssv, tt)], [(ssv, tt)])
            kb.stt(vn[:, :], vg[:, :], ssv[:, tt:tt + 1], gB[:, :], ALU.mult, ALU.mult, [vg, (ssv, tt), gB], [vn])
            for g4 in range(4):
                kb.mm(ps_s[g4][:, tt * 128:(tt + 1) * 128], vn[:, g4 * 128:(g4 + 1) * 128], swTb[:, g4, :], True, True,
                      [vn, swTb], [(ps_s[g4], tt)])
        for g4 in range(4):
            for tt in range(4):
                kb.tt("dve", stmp[:, tt * 128:(tt + 1) * 128], ps_s[g4][:, tt * 128:(tt + 1) * 128], sbB[:, g4, :], ALU.add,
                      [ps_s[g4], sbB], [(stmp, tt)])
            kb.tt("dve", ocT[:, g4, :], stmp[:, :], uT[:, g4, :], ALU.mult, [stmp, (uT, g4)], [(ocT, g4)])
        osrc = (oa, ob, ocT)
        for oc in range(8):
            for br in range(3):
                pg = nps()
                for k in range(8):
                    kb.mm(pg[:, :G], Wm[:, k, br * 1024 + oc * 128: br * 1024 + (oc + 1) * 128], hT[:, k, :], k == 0, k == 7,
                          [(Wm, k), hT], [pg])
                gt = gate[br % 2]
                kb.act(gt[:, :], pg[:, :G], AF.Sigmoid, [pg, bm], [gt], bias=bm[:, br * 8 + oc: br * 8 + oc + 1])
                pb = nps()
                for k in range(4):
                    kb.mm(pb[:, :G], Wbr[:, br * 4 + k, oc * 128:(oc + 1) * 128], osrc[br][:, k, :], k == 0, k == 3,
                          [(Wbr, br * 4 + k), (osrc[br], k)], [pb])
                if br == 0:
                    kb.tt("dve", acc[:, :], gt[:, :], pb[:, :G], ALU.mult, [gt, pb], [acc])
                else:
                    kb.tt("dve", gt[:, :], gt[:, :], pb[:, :G], ALU.mult, [gt, pb], [gt])
                    if br == 1:
                        kb.tt("dve", acc[:, :], acc[:, :], gt[:, :], ALU.add, [acc, gt], [acc])
                    else:
                        kb.tt("dve", mixedT[:, oc, :], acc[:, :], gt[:, :], ALU.add, [acc, gt], [(mixedT, oc)])
        for oc in range(8):
            p = nps()
            for k in range(8):
                kb.mm(p[:, :G], Wo[:, k, oc * 128:(oc + 1) * 128], mixedT[:, k, :], k == 0, k == 7, [(Wo, k), (mixedT, k)], [p])
            kb.tt("dve", xg[:, oc, :], xg[:, oc, :], p[:, :G], ALU.add, [(xg, oc), p], [(xg, oc)])
            kb.dma("q_sp", x1Tv[:, oc, t0:t0 + G], xg[:, oc, :], [(xg, oc)], [x1T])
    return _fin(kb, own)


def build_c2(T, kb=None, io=None, sfx=""):
    own = kb is None
    if own:
        kb = KB()
    kb.io = io or {}
    kb.begin_phase(sfx)
    G = 512
    NG = T // G
    NF = DFF // 128
    x1T = kb.din("x1T", [D, T])
    gF_d = kb.din("gF", [128, 8])
    gZ_d = kb.din("gZ", [128, 8])
    w1_d = kb.din("w1", [D, DFF])
    w3_d = kb.din("w3", [D, DFF])
    w2_d = kb.din("w2", [DFF, D])
    x2T = kb.dout("x2T", [D, T]) if kb.io.get("want_x2", True) else None
    x2nT = kb.dout("x2nT", [D, T]) if kb.io.get("want_x2n", True) else None

    W1 = kb.sb("W1", [128, 8, DFF], BF16)
    W3 = kb.sb("W3", [128, 8, DFF], BF16)
    W2 = kb.sb("W2", [128, NF, D], BF16)
    gF = kb.sb("gF_s", [128, 8], F32)
    gZ = kb.sb("gZ_s", [128, 8], F32)
    ones = kb.sb("ones", [128, 128], BF16)
    epsT = kb.sb("epsT", [128, 1], F32)
    xg = kb.sb("xg", [128, 8, G], F32)
    hT = kb.sb("hT", [128, 8, G], BF16)
    tmpA = kb.sb("tmpA", [128, G], F32)
    rstdB = kb.sb("rstdB", [128, G], F32)
    aT = kb.sb("aT", [128, NF, G], BF16)
    sl = [kb.sb("sl%d" % i, [128, G], F32) for i in range(2)]
    gN = kb.sb("gN_s", [128, 8], F32)
    PS = [kb.psum("ps%d" % i) for i in range(7)]
    kb.phase_barrier()
    want_x2 = kb.io.get("want_x2", True)
    want_x2n = kb.io.get("want_x2n", True)
    h_next = kb.io.get("h_next")
    if h_next is not None:
        gN_d = kb.din("gN", [128, 8])
        kb.dma("q_sp", gN.t[:], gN_d.t, [gN_d], [gN])

    kb.memset("dve", ones[:, :], 1.0, [ones])
    kb.memset("dve", epsT[:, :], EPS, [epsT])
    kb.dma("q_sp", gF.t[:], gF_d.t, [gF_d], [gF])
    kb.dma("q_sp", gZ.t[:], gZ_d.t, [gZ_d], [gZ])
    load_w(kb, "q_pool", W1, w1_d, 8)
    load_w(kb, "q_pool", W3, w3_d, 8)
    load_w(kb, "q_pool", W2, w2_d, NF)
    x1Tv = x1T.t.rearrange("(c p) t -> p c t", p=128)
    x2Tv = x2T.t.rearrange("(c p) t -> p c t", p=128) if x2T is not None else None
    x2nTv = x2nT.t.rearrange("(c p) t -> p c t", p=128) if x2nT is not None else None
    pi = [0]

    def nps():
        p = PS[pi[0] % 7]
        pi[0] += 1
        return p

    for gi in range(NG):
        t0 = gi * G
        for c in range(8):
            kb.dma("q_sp", xg[:, c, :], x1Tv[:, c, t0:t0 + G], [x1T], [(xg, c)])
        rmsnorm_fm(kb, xg, hT, gF, ones, epsT, nps(), tmpA, rstdB, G)
        for fc in range(NF):
            p1 = nps()
            for k in range(8):
                kb.mm(p1[:, :G], W1[:, k, fc * 128:(fc + 1) * 128], hT[:, k, :], k == 0, k == 7, [(W1, k), hT], [p1])
            p3 = nps()
            for k in range(8):
                kb.mm(p3[:, :G], W3[:, k, fc * 128:(fc + 1) * 128], hT[:, k, :], k == 0, k == 7, [(W3, k), hT], [p3])
            s = sl[fc % 2]
            kb.act(s[:, :], p1[:, :G], AF.Silu, [p1], [s])
            kb.tt("dve", aT[:, fc, :], s[:, :], p3[:, :G], ALU.mult, [s, p3], [(aT, fc)])
        for oc in range(8):
            p = nps()
            for k in range(NF):
                kb.mm(p[:, :G], W2[:, k, oc * 128:(oc + 1) * 128], aT[:, k, :], k == 0, k == NF - 1, [(W2, k), (aT, k)], [p])
            kb.tt("dve", xg[:, oc, :], xg[:, oc, :], p[:, :G], ALU.add, [(xg, oc), p], [(xg, oc)])
            if want_x2:
                kb.dma("q_sp", x2Tv[:, oc, t0:t0 + G], xg[:, oc, :], [(xg, oc)], [x2T])
        if h_next is not None:
            rmsnorm_fm(kb, xg, hT, gN, ones, epsT, nps(), tmpA, rstdB, G)
            for c in range(8):
                kb.dma("q_sp", h_next.t[c, :, t0:t0 + G], hT[:, c, :], [(hT, c)], [(h_next, c)])
        if want_x2n:
            rmsnorm_fm_f32(kb, xg, hT, gZ, ones, epsT, nps(), tmpA, rstdB, G)
            for oc in range(8):
                kb.dma("q_sp", x2nTv[:, oc, t0:t0 + G], xg[:, oc, :], [(xg, oc)], [x2nT])
    return _fin(kb, own)


def rmsnorm_fm_f32(kb, xg, hT, gcol, ones, epsT, ps_ss, tmpA, rstdB, G=512):
    kb.tt("pool", hT[:, :, :], xg[:, :, :], xg[:, :, :], ALU.mult, [xg], [hT])
    for c in range(8):
        kb.mm(ps_ss[:, :G], ones[:, :], hT[:, c, :], c == 0, c == 7, [ones, hT], [ps_ss])
    kb.act(tmpA[:, :G], ps_ss[:, :G], AF.Sqrt, [ps_ss, epsT], [tmpA], bias=epsT[:, 0:1], scale=1.0 / D)
    kb.recip(rstdB[:, :G], tmpA[:, :G], [tmpA], [rstdB])
    for c in range(8):
        kb.stt(xg[:, c, :], xg[:, c, :], gcol[:, c:c + 1], rstdB[:, :G], ALU.mult, ALU.mult, [(xg, c), gcol, rstdB], [(xg, c)])


def _col8(v):
    return np.ascontiguousarray(v.reshape(8, 128).T)


def prep_c1(l, inp):
    w_in = inp["w_in"][l]
    d = {}
    d["gA"] = _col8(inp["attn_norm"][l])
    d["w_sgu"] = np.ascontiguousarray(w_in[:, 2840:3864])
    d["sgu_gB"] = np.ascontiguousarray(np.broadcast_to(inp["sgu_norm"][l][None, :], (128, 512)))
    d["sgu_wT"] = np.ascontiguousarray(inp["sgu_w"][l].transpose(2, 0, 1))
    d["sgu_bB"] = np.ascontiguousarray(np.broadcast_to(inp["sgu_b"][l][None], (128, 4, 128)))
    d["cm01"] = np.triu(np.ones((128, 128), np.float32))
    d["w_br"] = np.ascontiguousarray(np.concatenate([inp["w_branch_a"][l], inp["w_branch_b"][l], inp["w_branch_c"][l]], 0))
    d["w_merge"] = np.ascontiguousarray(inp["w_merge"][l])
    d["bm"] = np.ascontiguousarray(inp["b_merge"][l].reshape(24, 128).T)
    d["w_out"] = np.ascontiguousarray(inp["w_out"][l])
    return d


def prep_c2(l, inp):
    d = {}
    d["gF"] = _col8(inp["ffn_norm"][l])
    d["gZ"] = _col8(inp["final_norm"])
    d["w1"] = np.ascontiguousarray(inp["w_ffn1"][l])
    d["w3"] = np.ascontiguousarray(inp["w_ffn3"][l])
    d["w2"] = np.ascontiguousarray(inp["w_ffn2"][l])
    return d


NFM = 17
FM_ROPE = {0: 9, 1: 10, 3: 11, 4: 12, 5: 13, 6: 14, 7: 15, 8: 16}
TWO_PI = 2.0 * math.pi
C1 = 6.28125
C2 = TWO_PI - C1
BIGS = 1.0e9


def build_ab(S, stop=None, kb=None, io=None, sfx=""):
    own = kb is None
    if own:
        kb = KB()
    kb.io = io or {}
    kb.begin_phase(sfx)
    PG = 256
    NPG = S // PG
    QG = 512
    NQ = S // QG
    NT = S // 128
    NCP = S // 16
    ncmp = NCP - 1
    NCT = (NCP + 127) // 128
    NCW = NCT * 128

    h_src = kb.io.get("h_src")
    o_piece = kb.io.get("o_piece")
    xT = kb.din("xT", [D, S]) if h_src is None else None
    posB_d = kb.din("posB", [128, S], I32)
    gA_d = kb.din("gA", [128, 8])
    wfm_d = kb.din("w_fm", [D, NFM * 128])
    wtm_d = kb.din("w_tm", [D, 396])
    identb_d = kb.din("identb", [128, 128], BF16)
    identN_d = kb.din("identN", [128, 128], BF16)
    colc_d = kb.din("colc", [128, 4])
    cmpbias_d = kb.din("cmpbias", [128, 5, 512], BF16)
    cdiag_d = kb.din("cdiag", [128, 2, 128], BF16)
    EE_d = kb.din("EE", [128, NT, 128], BF16)
    eaW_d = kb.din("eaW", [128, 2, 254])
    ovl_d = kb.din("ovl", [128, NCT, 128], BF16)
    w1kv_d = kb.din("w1kv", [128, 32, 256])
    posT_d = kb.din("posT", [128, 32])
    w2k_d = kb.din("w2k", [128, 2, 128])
    w2v_d = kb.din("w2v", [128, 2, 64])
    lqk_d = kb.din("lqk", [128, 4, 64])
    sublnB_d = kb.din("sublnB", [128, 128])
    oT = kb.dout("oT", [512, S], BF16) if o_piece is None else None
    qscr = kb.dscr("qscr", [7, 128, S], BF16)

    ident = kb.sb("ident", [128, 128], BF16)
    identN = kb.sb("identN", [128, 128], BF16)
    ones = kb.sb("ones", [128, 128], BF16)
    gA = kb.sb("gA_s", [128, 8], F32)
    epsT = kb.sb("epsT", [128, 1], F32)
    eps128 = kb.sb("eps128", [128, 1], F32)
    colc = kb.sb("colc_s", [128, 4], F32)
    cmpbias = kb.sb("cmpbias_s", [128, 5, 512], BF16)
    cdiag = kb.sb("cdiag_s", [128, 2, 128], BF16)
    eaW = kb.sb("eaW_s", [128, 2, 254], F32)
    sublnB = kb.sb("sublnB_s", [128, 128], F32)
    lqk = kb.sb("lqk_s", [128, 4, 64], F32)
    lsc = kb.sb("lsc", [128, 8], F32)
    kslc = kb.sb("kslc", [128, S], BF16)
    kb0 = kb.sb("kb0", [128, S], BF16)
    kb1 = kb.sb("kb1", [128, S], BF16)
    Vnsa = kb.sb("Vnsa", [128, NT, 130], BF16)
    Vd = kb.sb("Vd", [128, NT, 258], BF16)
    gates = kb.sb("gates", [128, NT, 12], F32)
    kcT = kb.sb("kcT", [128, NCW], BF16)
    Vc = kb.sb("Vc", [128, NCT, 193], BF16)
    kvT1 = kb.sb("kvT1", [128, S], BF16)
    EE = Tn(kb.nc.alloc_sbuf_tensor_at("EE_s" + kb.sfx, [128, NT, 128], BF16, offset=kb.off - 2 * S), "EE_s")
    PS = [kb.psum("ps%d" % i) for i in range(7)]
    PSB = kb.psum("psb", [128, 1024], BF16)
    mark = kb.off

    Wfm = kb.sb("Wfm", [128, 8, NFM * 128], BF16)
    Wtm = kb.sb("Wtm", [128, 8, 396], BF16)
    xg = kb.sb("xg", [128, 8, PG], F32)
    hT = kb.sb("hT", [128, 8, PG], BF16)
    tmpA = kb.sb("tmpA", [128, PG], F32)
    rstdB = kb.sb("rstdB", [128, PG], F32)
    posi = kb.sb("posi", [128, PG], I32)
    ang = kb.sb("ang", [128, PG], F32)
    ra = kb.sb("ra", [128, PG], F32)
    rk = kb.sb("rk", [128, PG], F32)
    rki = kb.sb("rki", [128, PG], I32)
    rfix = kb.sb("rfix", [128, PG], F32)
    cosT = kb.sb("cosT", [128, PG], F32)
    sinT = kb.sb("sinT", [128, PG], F32)
    t1 = kb.sb("t1", [128, PG], F32)
    t2 = kb.sb("t2", [128, PG], F32)
    qst = [kb.sb("qstP%d" % i, [128, 7, PG], BF16) for i in range(2)]
    P_list = [Wfm, Wtm, xg, hT, tmpA, rstdB, posi, ang, ra, rk, rki, rfix, cosT, sinT, t1, t2] + qst
    kb.alloc_log.append(EE)
    kb.phase_barrier()

    kb.memset("dve", ones[:, :], 1.0, [ones])
    kb.memset("dve", epsT[:, :], EPS, [epsT])
    kb.memset("dve", eps128[:, :], EPS, [eps128])
    kb.memset("pool", Vnsa[:, :, :], 1.0, [Vnsa])
    kb.memset("pool", Vd[:, :, :], 1.0, [Vd])
    for (dst, src) in ((ident, identb_d), (identN, identN_d), (gA, gA_d), (colc, colc_d), (cmpbias, cmpbias_d),
                       (cdiag, cdiag_d), (eaW, eaW_d), (sublnB, sublnB_d), (lqk, lqk_d)):
        kb.dma("q_sp", dst.t[:], src.t, [src], [dst])
    load_w(kb, "q_pool", Wfm, wfm_d, 8)
    load_w(kb, "q_pool", Wtm, wtm_d, 8)
    invf = colc[:, 0:1]
    sgn = colc[:, 1:2]
    kb.tt("dve", lqk[:, 0, :], lqk[:, 0, :], lqk[:, 1, :], ALU.mult, [lqk], [lqk])
    kb.tt("dve", lqk[:, 2, :], lqk[:, 2, :], lqk[:, 3, :], ALU.mult, [lqk], [lqk])
    kb.pr.add("dve", lambda e: e.reduce_sum(out=lsc[:, 0:1], in_=lqk[:, 0, :], axis=AX.X), [lqk], [lsc])
    kb.pr.add("dve", lambda e: e.reduce_sum(out=lsc[:, 1:2], in_=lqk[:, 2, :], axis=AX.X), [lqk], [lsc])
    kb.act(lsc[:, 2:4], lsc[:, 0:2], AF.Exp, [lsc], [lsc])
    kb.tt("dve", lsc[:, 4:5], lsc[:, 3:4], lsc[:, 2:3], ALU.subtract, [lsc], [lsc])
    kb.tt("dve", lsc[:, 4:5], lsc[:, 4:5], colc[:, 2:3], ALU.subtract, [lsc, colc], [lsc])
    neglam = lsc[:, 4:5]
    kb.ts("dve", sublnB[:, :], sublnB[:, :], colc[:, 3:4], None, ALU.mult, None, [sublnB, colc], [sublnB])

    if stop == 'C':
        return _fin(kb, own)
    xTv = xT.t.rearrange("(c p) t -> p c t", p=128) if xT is not None else None
    pi = [0]

    def nps():
        p = PS[pi[0] % 7]
        pi[0] += 1
        return p

    def p_load(gi):
        t0 = gi * PG
        if h_src is None:
            for c in range(8):
                kb.dma("q_sp", xg[:, c, :], xTv[:, c, t0:t0 + PG], [xT], [(xg, c)])
        kb.dma("q_sp", posi[:, :], posB_d[:, t0:t0 + PG], [posB_d], [posi])

    p_load(0)
    for gi in range(NPG):
        t0 = gi * PG
        kb.cp("dve", ang[:, :], posi[:, :], [posi], [ang])
        kb.ts("dve", ang[:, :], ang[:, :], invf, None, ALU.mult, None, [ang, colc], [ang])
        for which in range(2):
            dst = sinT if which == 0 else cosT
            if which == 0:
                src = ang
            else:
                kb.ts("dve", ra[:, :], ang[:, :], math.pi / 2, None, ALU.add, None, [ang], [ra])
                src = ra
            kb.ts("dve", rk[:, :], src[:, :], 1.0 / TWO_PI, None, ALU.mult, None, [src], [rk])
            kb.cp("dve", rki[:, :], rk[:, :], [rk], [rki])
            kb.cp("dve", rk[:, :], rki[:, :], [rki], [rk])
            kb.pr.add("dve", lambda e, src=src: e.scalar_tensor_tensor(out=rfix[:, :], in0=rk[:, :], scalar=-C1, in1=src[:, :],
                                                                       op0=ALU.mult, op1=ALU.add), [rk, src], [rfix])
            kb.pr.add("dve", lambda e: e.scalar_tensor_tensor(out=rfix[:, :], in0=rk[:, :], scalar=-C2, in1=rfix[:, :],
                                                              op0=ALU.mult, op1=ALU.add), [rk, rfix], [rfix])
            kb.ts("dve", rk[:, :], rfix[:, :], math.pi, -TWO_PI, ALU.is_gt, ALU.mult, [rfix], [rk])
            kb.tt("dve", rfix[:, :], rfix[:, :], rk[:, :], ALU.add, [rfix, rk], [rfix])
            kb.ts("dve", rk[:, :], rfix[:, :], -math.pi, TWO_PI, ALU.is_lt, ALU.mult, [rfix], [rk])
            kb.tt("dve", rfix[:, :], rfix[:, :], rk[:, :], ALU.add, [rfix, rk], [rfix])
            kb.ts("dve", rfix[:, :], rfix[:, :], math.pi, -math.pi, ALU.min, ALU.max, [rfix], [rfix])
            if which == 0:
                kb.act(dst[:, :], rfix[:, :], AF.Sin, [rfix, colc], [dst], scale=sgn)
            else:
                kb.act(dst[:, :], rfix[:, :], AF.Sin, [rfix], [dst])
        if h_src is None:
            rmsnorm_fm(kb, xg, hT, gA, ones, epsT, nps(), tmpA, rstdB, PG)
        else:
            Th = S // 2
            rr, col = t0 // Th, t0 % Th
            for c in range(8):
                kb.dma("q_sp", hT[:, c, :], h_src.t[c, rr * 128:(rr + 1) * 128, col:col + PG], [h_src], [(hT, c)])
        if gi + 1 < NPG:
            p_load(gi + 1)
        if stop == 'P2':
            return _fin(kb, own)
        qs = qst[gi % 2]
        for ch in range(9):
            pp = nps()
            for k in range(8):
                kb.mm(pp[:, :PG], Wfm[:, k, ch * 128:(ch + 1) * 128], hT[:, k, :], k == 0, k == 7, [(Wfm, k), hT], [pp])
            if ch in (0, 1):
                kb.cp("act", qs[:, ch, :], pp[:, :PG], [pp], [(qs, ch)])
            if ch == 2:
                kb.cp("act", kvT1[:, t0:t0 + PG], pp[:, :PG], [pp], [(kvT1, gi)])
                continue
            sw = FM_ROPE[ch]
            psw = nps()
            for k in range(8):
                kb.mm(psw[:, :PG], Wfm[:, k, sw * 128:(sw + 1) * 128], hT[:, k, :], k == 0, k == 7, [(Wfm, k), hT], [psw])
            kb.tt("dve", t1[:, :], pp[:, :PG], cosT[:, :], ALU.mult, [pp, cosT], [t1])
            kb.tt("dve", t2[:, :], psw[:, :PG], sinT[:, :], ALU.mult, [psw, sinT], [t2])
            if ch in (0, 1):
                dst, dk, dt_ = qs[:, 2 + ch, :], (qs, 2 + ch), qs
            elif ch == 3:
                dst, dk, dt_ = kslc[:, t0:t0 + PG], (kslc, gi), kslc
            elif ch == 4:
                dst, dk, dt_ = qs[:, 6, :], (qs, 6), qs
            elif ch in (5, 6):
                dst, dk, dt_ = qs[:, ch - 1, :], (qs, ch - 1), qs
            elif ch == 7:
                dst, dk, dt_ = kb0[:, t0:t0 + PG], (kb0, gi), kb0
            else:
                dst, dk, dt_ = kb1[:, t0:t0 + PG], (kb1, gi), kb1
            kb.tt("pool", dst, t1[:, :], t2[:, :], ALU.add, [t1, t2], [dk])
        if stop == 'P3':
            return _fin(kb, own)
        for j in range(7):
            kb.dma("q_sp", qscr[j, :, t0:t0 + PG], qs[:, j, :], [(qs, j)], [qscr])
        if stop == 'P4':
            return _fin(kb, own)
        for tt in range(PG // 128):
            T_ = gi * (PG // 128) + tt
            p = nps()
            for k in range(8):
                kb.mm(p[:, :396], hT[:, k, tt * 128:(tt + 1) * 128], Wtm[:, k, :], k == 0, k == 7, [hT, (Wtm, k)], [p])
            kb.cp("act", Vnsa[:, T_, 0:64], p[:, 0:64], [p], [(Vnsa, T_)])
            kb.cp("act", Vnsa[:, T_, 65:129], p[:, 64:128], [p], [(Vnsa, T_)])
            kb.act(gates[:, T_, :], p[:, 128:140], AF.Sigmoid, [p], [(gates, T_)])
            kb.cp("dve", Vd[:, T_, 0:128], p[:, 140:268], [p], [(Vd, T_)])
            kb.cp("dve", Vd[:, T_, 129:257], p[:, 268:396], [p], [(Vd, T_)])
        if stop == 'P5' or (stop == 'P6' and gi == 1):
            return _fin(kb, own)

    if stop == 'P':
        return _fin(kb, own)
    kb.off = mark
    w1kv = kb.sb("w1kv", [128, 32, 256], BF16)
    posT = kb.sb("posT", [128, 32], BF16)
    w2k = kb.sb("w2k", [128, 2, 128], BF16)
    w2v = kb.sb("w2v", [128, 2, 64], BF16)
    hidT = kb.sb("hidT", [128, 2, 2, NCW], BF16)
    posb = kb.sb("posb", [128, 4], F32)
    X_list = [w1kv, posT, w2k, w2v, hidT, posb]
    kb.barrier(P_list, X_list)
    for l4 in range(4):
        kb.dma("q_pool", w1kv[:, l4 * 8:(l4 + 1) * 8, :], w1kv_d[:, l4 * 8:(l4 + 1) * 8, :], [w1kv_d], [(w1kv, l4)])
    kb.dma("q_pool", posT[:, :], posT_d.t, [posT_d], [posT])
    kb.dma("q_pool", w2k[:, :, :], w2k_d.t, [w2k_d], [w2k])
    kb.dma("q_pool", w2v[:, :, :], w2v_d.t, [w2v_d], [w2v])
    kb.memset("pool", hidT[:, :, :, :], 0.0, [hidT])
    kvv = kvT1.t.reshape([128, NCP, 16])
    for which in range(2):
        r0 = 64 * which
        for half in range(2):
            ph = nps()
            for l in range(32):
                kb.mm(ph[:, :ncmp], w1kv[r0:r0 + 64, l, half * 128:(half + 1) * 128],
                      kvv[r0:r0 + 64, (l // 16):(l // 16) + ncmp, l % 16], l == 0, l == 31, [w1kv, kvT1], [ph])
            pb = nps()
            for l in range(32):
                kb.mm(pb[:, 0:1], w1kv[r0:r0 + 64, l, half * 128:(half + 1) * 128], posT[r0:r0 + 64, l:l + 1], l == 0, l == 31,
                      [w1kv, posT], [pb])
            idx = which * 2 + half
            kb.cp("dve", posb[:, idx:idx + 1], pb[:, 0:1], [pb], [(posb, idx)])
            kb.act(hidT[:, which, half, :ncmp], ph[:, :ncmp], AF.Gelu_apprx_tanh, [ph, (posb, idx)], [(hidT, idx)],
                   bias=posb[:, idx:idx + 1])
    pk = nps()
    for half in range(2):
        kb.mm(pk[:, :NCW], w2k[:, half, :], hidT[:, 0, half, :], half == 0, half == 1, [w2k, hidT], [pk])
    kb.cp("act", kcT[:, :], pk[:, :NCW], [pk], [kcT])
    kb.memset("pool", Vc[:, :, :], 1.0, [Vc])
    for nt in range(NCT):
        pv = nps()
        for half in range(2):
            kb.mm(pv[:, 0:64], hidT[:, 1, half, nt * 128:(nt + 1) * 128], w2v[:, half, :], half == 0, half == 1, [hidT, w2v], [pv])
        kb.cp("dve", Vc[:, nt, 0:64], pv[:, 0:64], [pv], [Vc])
    kb.dma("q_sp", Vc[:, :, 65:193], ovl_d.t, [ovl_d], [Vc])

    if stop == 'X':
        return _fin(kb, own)
    kb.off = mark
    qstA = [kb.sb("qstA%d" % i, [128, 6, QG], BF16) for i in range(2)]
    qzA = [kb.sb("qzA%d" % i, [128, 12, QG], BF16) for i in range(1)]
    kwst = [kb.sb("kwst%d" % i, [128, 1024], BF16) for i in range(2)]
    PT = [kb.sb("PT%d" % i, [128, 512], BF16) for i in range(4)]
    negT = kb.sb("negT", [128, 512], BF16)
    Oev = [kb.sb("Oev%d" % i, [128, 4, 193], F32) for i in range(2)]
    rc = [kb.sb("rc%d" % i, [128, 4], F32) for i in range(2)]
    coef = [kb.sb("coef%d" % i, [128, 4], F32) for i in range(2)]
    ocmp = kb.sb("ocmp", [128, 4, 4, 64], F32)
    oacc = kb.sb("oacc", [128, 4, 256], F32)
    obt = kb.sb("obt", [128, 4, 256], F32)
    imp = kb.sb("imp", [128, 4, 128], F32)
    score = kb.sb("score", [128, 4, 128], F32)
    sc2 = kb.sb("sc2", [128, 4, 128], F32)
    m8 = kb.sb("m8", [128, 4, 8], F32)
    neg01 = kb.sb("neg01", [128, 4, 128], BF16)
    od0 = kb.sb("od0", [128, 4, 128], F32)
    od1 = kb.sb("od1", [128, 4, 128], F32)
    djunk = kb.sb("djunk", [128, 128], F32)
    dss = kb.sb("dss", [128, 4], F32)
    o16 = kb.sb("o16", [128, 4, 512], BF16)
    oTs = kb.sb("oTs", [128, 4, 512], BF16)
    A_list = qzA + qstA + kwst + PT + [negT, ocmp, oacc, obt, imp, score, sc2, m8, neg01, od0, od1, djunk, dss, o16, oTs] + Oev + rc + coef
    kb.barrier(X_list + P_list + [kvT1], A_list + [EE])
    kb.dma("q_sp", EE.t[:], EE_d.t, [EE_d], [EE])
    for qzb in qzA:
        kb.memset("pool", qzb[:, :, :], 0.0, [qzb])
    LB = [PS[0], PS[1], PS[6]]
    DEPTH = 2
    ACC = [(PS[2], PS[3]), (PS[4], PS[5])]
    PSM = PS[6]
    li = [0]
    ai = [0]
    pti = [0]
    ei = [0]
    oTv = oT.t.rearrange("(c p) t -> p c t", p=128) if oT is not None else None

    def nL():
        li[0] += 1
        return LB[li[0] % 3]

    def nA():
        ai[0] += 1
        return ACC[ai[0] % 2]

    def nPT():
        pti[0] += 1
        return PT[pti[0] % 4]

    def nE():
        ei[0] += 1
        return Oev[ei[0] % 2], rc[ei[0] % 2], coef[ei[0] % 2]

    def evac(acc, w, nbank_q):
        ev, r_, cf = nE()
        nb = 4 // nbank_q
        for bnk in range(nb):
            kb.cp("dve", ev[:, bnk * nbank_q:(bnk + 1) * nbank_q, 0:w],
                  acc[bnk][:, 0:nbank_q * w].rearrange("p (q w) -> p q w", w=w), [acc[bnk]], [(ev, bnk)])
        sumcol = 64 if w in (65, 193) else 128
        kb.ts("dve", r_[:, :], ev[:, :, sumcol], 1e-30, None, ALU.max, None, [ev], [r_])
        kb.recip(r_[:, :], r_[:, :], [r_], [r_])
        return ev, r_, cf

    for Q in range(NQ):
        q0 = Q * QG
        qs = qstA[Q % 2]
        kw = kwst[Q % 2]
        for j in range(6):
            kb.dma("q_sp", qs[:, j, :], qscr[j, :, q0:q0 + QG], [qscr], [(qs, j)])
        qz = qzA[0]
        for j in range(6):
            for hf in range(2):
                kb.cp("pool", qz[64 * hf:64 * hf + 64, 2 * j + hf, :], qs[64 * hf:64 * hf + 64, j, :], [(qs, j)], [(qz, 2 * j + hf)])
        klo = max(0, q0 - 512)
        kb.dma("q_sp", kw[:, (klo - (q0 - 512)):1024], qscr[6, :, klo:q0 + 512], [qscr], [kw])
        for h in range(4):
            r0 = 64 * (h % 2)
            qa = qz[:, 2 * (h // 2) + h % 2, :]
            nts = [nt for nt in range(NCT) if Q - 4 * nt >= 0]
            acc = nA()
            def c_qk(ix, nt):
                Dd = Q - 4 * nt
                L = nL()
                kb.mm(L[:, :512], kcT[:, nt * 128:(nt + 1) * 128], qa, True, Dd > 4, [kcT, qz], [L])
                if Dd <= 4:
                    kb.mm(L[:, :512], identN[:, :], cmpbias[:, Dd, :], False, True, [identN, cmpbias], [L])
                pt = nPT()
                kb.act(pt[:, :], L[:, :512], AF.Exp, [L], [pt], scale=0.125)
                return pt

            def c_pv(ix, nt, pt):
                for qt in range(4):
                    kb.mm(acc[qt // 2][:, (qt % 2) * 193:(qt % 2) * 193 + 193], pt[:, qt * 128:(qt + 1) * 128], Vc[:, nt, :],
                          ix == 0 and qt % 2 == 0, ix == len(nts) - 1, [pt, Vc], [acc[qt // 2]])

            pend = []
            for ix, nt in enumerate(nts):
                pt = c_qk(ix, nt)
                pend.append((ix, nt, pt))
                if len(pend) > DEPTH:
                    c_pv(*pend.pop(0))
            while pend:
                c_pv(*pend.pop(0))
            ev, r_, cf = evac(acc, 193, 2)
            for qt in range(4):
                kb.ts("dve", ocmp[:, h, qt, :], ev[:, qt, 0:64], r_[:, qt:qt + 1], None, ALU.mult, None, [ev, r_], [(ocmp, h)])
                if h == 0:
                    kb.ts("dve", imp[:, qt, :], ev[:, qt, 65:193], r_[:, qt:qt + 1], None, ALU.mult, None, [ev, r_], [imp])
                else:
                    kb.stt(imp[:, qt, :], ev[:, qt, 65:193], r_[:, qt:qt + 1], imp[:, qt, :], ALU.mult, ALU.add, [ev, r_, imp], [imp])
        for qt in range(4):
            qta = 4 * Q + qt
            off = 126 - 2 * qta
            kb.tt("dve", score[:, qt, :], imp[:, qt, :], eaW[:, 0, off:off + 128], ALU.mult, [imp, eaW], [score])
            kb.tt("dve", score[:, qt, :], score[:, qt, :], eaW[:, 1, off:off + 128], ALU.add, [score, eaW], [score])
            kb.memset("dve", score[:, qt, 0:1], BIGS, [score])
            kb.pr.add("dve", lambda e, qt=qt: e.max(out=m8[:, qt, :], in_=score[:, qt, :]), [score], [m8])
            kb.pr.add("dve", lambda e, qt=qt: e.match_replace(out=sc2[:, qt, :], in_to_replace=m8[:, qt, :],
                                                              in_values=score[:, qt, :], imm_value=-3.0e38), [score, m8], [sc2])
            kb.pr.add("dve", lambda e, qt=qt: e.max(out=m8[:, qt, :], in_=sc2[:, qt, :]), [sc2], [m8])
            kb.ts("dve", neg01[:, qt, :], score[:, qt, :], m8[:, qt, 7:8], 1.0, ALU.is_ge, ALU.subtract, [score, m8], [neg01])
            kb.tr(PSB[:, qt * 128:(qt + 1) * 128], neg01[:, qt, :], ident[:, :], [neg01, ident], [PSB])
        kb.cp("dve", negT[:, :], PSB[:, 0:512], [PSB], [negT])
        for h in range(4):
            r0 = 64 * (h % 2)
            qr = qz[:, 2 * (2 + h // 2) + h % 2, :]
            acc = nA()
            firstb = True
            ilist = [i for i in range(8) if 4 * Q - 4 + i >= 0]
            def w_qk(i):
                qts = [qt for qt in range(4) if 0 <= 4 - i + qt <= 4]
                c0, c1 = qts[0] * 128, (qts[-1] + 1) * 128
                L = nL()
                kb.mm(L[:, c0:c1], kw[:, i * 128:(i + 1) * 128], qr[:, c0:c1], True, False, [kw, qz], [L])
                for qt in qts:
                    dd = 4 - i + qt
                    if dd == 0:
                        kb.mm(L[:, qt * 128:(qt + 1) * 128], identN[:, :], cdiag[:, 0, :], False, True, [identN, cdiag], [L])
                    elif dd == 4:
                        kb.mm(L[:, qt * 128:(qt + 1) * 128], identN[:, :], cdiag[:, 1, :], False, True, [identN, cdiag], [L])
                pt = nPT()
                kb.act(pt[:, c0:c1], L[:, c0:c1], AF.Exp, [L], [pt], scale=0.125)
                return qts, pt

            fb = [True]

            def w_pv(i, qts, pt):
                kt = 4 * Q - 4 + i
                for qt in qts:
                    kb.mm(acc[0][:, qt * 65:qt * 65 + 65], pt[:, qt * 128:(qt + 1) * 128], Vnsa[:, kt, 65:130],
                          fb[0], i == qt + 4, [pt, Vnsa], [acc[0]])
                    fb[0] = False

            pend = []
            for i in ilist:
                qts, pt = w_qk(i)
                pend.append((i, qts, pt))
                if len(pend) > DEPTH:
                    w_pv(*pend.pop(0))
            while pend:
                w_pv(*pend.pop(0))
            ev, r_, cf = evac(acc, 65, 4)
            for qt in range(4):
                qta = 4 * Q + qt
                kb.tt("dve", cf[:, qt:qt + 1], r_[:, qt:qt + 1], gates[:, qta, h * 3 + 2:h * 3 + 3], ALU.mult, [r_, gates], [cf])
                kb.ts("dve", oacc[:, qt, h * 64:(h + 1) * 64], ocmp[:, h, qt, :], gates[:, qta, h * 3:h * 3 + 1], None, ALU.mult, None,
                      [(ocmp, h), gates], [(oacc, h)])
                kb.stt(oacc[:, qt, h * 64:(h + 1) * 64], ev[:, qt, 0:64], cf[:, qt:qt + 1], oacc[:, qt, h * 64:(h + 1) * 64],
                       ALU.mult, ALU.add, [ev, cf, (oacc, h)], [(oacc, h)])

        def causal_unit(qap, kT, vfn, w, nbq, use_sel, qbuf):
            acc = nA()
            nkt = 4 * Q + 4
            ktn, vtn = kT_tn[0], vT_tn[0]

            def u_qk(kt):
                i = kt - 4 * Q
                c0 = max(i, 0) * 128
                L = nL()
                kb.mm(L[:, c0:512], kT[:, kt * 128:(kt + 1) * 128], qap[:, c0:512], True, False, [ktn, qbuf], [L])
                if use_sel:
                    kb.mm(L[:, c0:512], EE[:, kt, :], negT[:, c0:512], False, False, [EE, negT], [L])
                if i >= 0:
                    kb.mm(L[:, c0:c0 + 128], identN[:, :], cdiag[:, 0, :], False, True, [identN, cdiag], [L])
                pt = nPT()
                kb.act(pt[:, c0:512], L[:, c0:512], AF.Exp, [L], [pt], scale=0.125)
                return pt

            def u_pv(kt, pt):
                i = kt - 4 * Q
                for qt in range(max(i, 0), 4):
                    bnk, sl = qt // nbq, (qt % nbq) * w
                    kb.mm(acc[bnk][:, sl:sl + w], pt[:, qt * 128:(qt + 1) * 128], vfn(kt),
                          kt == 0 and qt % nbq == 0, kt == 4 * Q + qt, [pt, vtn], [acc[bnk]])

            pend = []
            for kt in range(nkt):
                pt = u_qk(kt)
                pend.append((kt, pt))
                if len(pend) > DEPTH:
                    u_pv(*pend.pop(0))
            while pend:
                u_pv(*pend.pop(0))
            return evac(acc, w, nbq)

        kT_tn = [None]
        vT_tn = [None]
        for hh in range(2):
            for m in range(2):
                mi = hh * 2 + m
                kT_tn[0] = kb0 if mi < 2 else kb1
                vT_tn[0] = Vd
                r0 = 64 * (mi % 2)
                qap = qz[:, 2 * (4 + mi // 2) + mi % 2, :]
                ev, r_, cf = causal_unit(qap, kT_tn[0][:, :], lambda kt, hh=hh: Vd[:, kt, hh * 129:hh * 129 + 129], 129, 2, False, qz)
                if m == 0:
                    for qt in range(4):
                        kb.ts("dve", od0[:, qt, :], ev[:, qt, 0:128], r_[:, qt:qt + 1], None, ALU.mult, None, [ev, r_], [od0])
                else:
                    for qt in range(4):
                        kb.ts("dve", od1[:, qt, :], ev[:, qt, 0:128], r_[:, qt:qt + 1], neglam, ALU.mult, ALU.mult, [ev, r_, lsc], [od1])
                        kb.tt("dve", od0[:, qt, :], od0[:, qt, :], od1[:, qt, :], ALU.add, [od0, od1], [od0])
                        kb.stt(djunk[:, :], od0[:, qt, :], 1.0, od0[:, qt, :], ALU.mult, ALU.mult, [od0], [djunk, dss],
                               accum_out=dss[:, qt:qt + 1])
                    kb.act(dss[:, :], dss[:, :], AF.Sqrt, [dss, eps128], [dss], bias=eps128[:, 0:1], scale=1.0 / 128)
                    kb.recip(dss[:, :], dss[:, :], [dss], [dss])
                    for qt in range(4):
                        kb.stt(obt[:, qt, hh * 128:(hh + 1) * 128], od0[:, qt, :], dss[:, qt:qt + 1], sublnB[:, :], ALU.mult, ALU.mult,
                               [od0, dss, sublnB], [(obt, hh)])
        for h in range(4):
            r0 = 64 * (h % 2)
            kT_tn[0] = kslc
            vT_tn[0] = Vnsa
            qap = qz[:, 2 * (2 + h // 2) + h % 2, :]
            ev, r_, cf = causal_unit(qap, kslc[:, :], lambda kt: Vnsa[:, kt, 0:65], 65, 4, True, qz)
            for qt in range(4):
                qta = 4 * Q + qt
                kb.tt("dve", cf[:, qt:qt + 1], r_[:, qt:qt + 1], gates[:, qta, h * 3 + 1:h * 3 + 2], ALU.mult, [r_, gates], [cf])
                kb.stt(oacc[:, qt, h * 64:(h + 1) * 64], ev[:, qt, 0:64], cf[:, qt:qt + 1], oacc[:, qt, h * 64:(h + 1) * 64],
                       ALU.mult, ALU.add, [ev, cf, (oacc, h)], [(oacc, h)])
        kb.cp("act", o16[:, :, 0:256], oacc[:, :, :], [oacc], [o16])
        kb.cp("act", o16[:, :, 256:512], obt[:, :, :], [obt], [o16])
        for c4 in range(4):
            for qt in range(4):
                kb.tr(PSB[:, 512 + qt * 128:512 + (qt + 1) * 128], o16[:, qt, c4 * 128:(c4 + 1) * 128], ident[:, :], [o16, ident], [(PSB, "o")])
            kb.cp("dve", oTs[:, c4, :], PSB[:, 512:1024], [(PSB, "o")], [(oTs, c4)])
            if o_piece is None:
                kb.dma("q_sp", oTv[:, c4, q0:q0 + QG], oTs[:, c4, :], [(oTs, c4)], [oT])
            else:
                hq = NQ // 2
                kb.dma("q_sp", o_piece.t[c4, Q // hq, :, (Q % hq) * QG:(Q % hq + 1) * QG], oTs[:, c4, :], [(oTs, c4)],
                       [(o_piece, (c4, Q // hq))])
        if stop is not None and stop.startswith('A') and int(stop[1:]) == Q:
            return _fin(kb, own)
    return _fin(kb, own)


def _swap64(cols):
    cols = np.asarray(cols).reshape(-1, 64)
    return np.concatenate([cols[:, 32:], cols[:, :32]], axis=1).reshape(-1)


def ab_consts(S):
    NT = S // 128
    NCP = S // 16
    ncmp = NCP - 1
    NCT = (NCP + 127) // 128
    bf = ml_dtypes.bfloat16
    c = {}
    c["identb"] = np.eye(128, dtype=np.float32).astype(bf)
    c["identN"] = (np.eye(128, dtype=np.float32) * NEGM).astype(bf)
    n_ = np.arange(128)[:, None, None]
    D_ = np.arange(5)[None, :, None]
    q_ = np.arange(512)[None, None, :]
    c["cmpbias"] = np.where(16 * n_ + 31 - q_ <= 512 * D_, 0.0, -1.0).astype(np.float32).astype(bf)
    k_ = np.arange(128)[:, None]
    qq = np.arange(128)[None, :]
    cd = np.stack([np.where(k_ <= qq, 0.0, -1.0), np.where(k_ > qq, 0.0, -1.0)], axis=1)
    c["cdiag"] = cd.astype(np.float32).astype(bf)
    j_ = np.arange(128)[:, None, None]
    kt_ = np.arange(NT)[None, :, None]
    kk = np.arange(128)[None, None, :]
    c["EE"] = np.where(j_ == 2 * kt_ + kk // 64, NEGM, 0.0).astype(np.float32).astype(bf)
    qp = np.arange(128)[:, None]
    rel = np.arange(254)[None, :] - 126
    cur = qp // 64
    elig = (rel <= cur).astype(np.float32)
    addw = np.where((rel == cur) | (rel == cur - 1), BIGS, np.where(rel > cur, -BIGS, 0.0)).astype(np.float32)
    c["eaW"] = np.ascontiguousarray(np.stack([elig, addw], axis=1))
    n = np.arange(NCT * 128)[:, None]
    j = np.arange(128)[None, :]
    ov = ((16 * n < 64 * j + 64) & (16 * n + 32 > 64 * j) & (n < ncmp)).astype(np.float32)
    c["ovl"] = np.ascontiguousarray(ov.reshape(NCT, 128, 128).transpose(1, 0, 2)).astype(bf)
    return c


def prep_ab(l, inp, b, g, S):
    w_in = inp["w_in"][l]
    d = {}
    d["posB"] = np.ascontiguousarray(np.broadcast_to(inp["positions"][b][None, :S], (128, S))).astype(np.int32)
    d["gA"] = _col8(inp["attn_norm"][l])
    r64 = np.arange(64)
    r128 = np.arange(128)
    plain = [256 * g + r128, 256 * g + 128 + r128,
             np.concatenate([512 + 64 * g + r64, 512 + 128 + 64 * g + r64]),
             np.concatenate([512 + 256 + 64 * g + r64] * 2),
             np.concatenate([512 + 512 + 64 * g + r64] * 2),
             1304 + 256 * g + r128, 1304 + 256 * g + 128 + r128,
             1816 + 256 * g + r128, 1816 + 256 * g + 128 + r128]
    swaps = [_swap64(plain[i]) for i in (0, 1, 3, 4, 5, 6, 7, 8)]
    cols = np.concatenate(plain + swaps)
    d["w_fm"] = np.ascontiguousarray(w_in[:, cols])
    tcols = np.concatenate([512 + 384 + 64 * g + r64, 512 + 640 + 64 * g + r64, 1280 + 12 * g + np.arange(12),
                            2328 + 256 * g + np.arange(256)])
    d["w_tm"] = np.ascontiguousarray(w_in[:, tcols])
    p = np.arange(128)
    lam_init = 0.8 - 0.6 * math.exp(-0.3 * l)
    colc = np.zeros((128, 4), np.float32)
    colc[:, 0] = (10000.0 ** (-(np.arange(32, dtype=np.float32)) / 32.0)).astype(np.float32)[p % 32]
    colc[:, 1] = np.where((p % 64) < 32, -1.0, 1.0)
    colc[:, 2] = lam_init
    colc[:, 3] = 1.0 - lam_init
    d["colc"] = colc
    w1k = inp["cmp_k_w1"][l].reshape(32, 64, 256).transpose(1, 0, 2)
    w1v = inp["cmp_v_w1"][l].reshape(32, 64, 256).transpose(1, 0, 2)
    d["w1kv"] = np.ascontiguousarray(np.concatenate([w1k, w1v], axis=0))
    d["posT"] = np.ascontiguousarray(np.concatenate([inp["cmp_pos_k"][l].T, inp["cmp_pos_v"][l].T], axis=0))
    w2k = inp["cmp_k_w2"][l].reshape(2, 128, 64).transpose(1, 0, 2)
    d["w2k"] = np.ascontiguousarray(np.concatenate([w2k, w2k], axis=2))
    d["w2v"] = np.ascontiguousarray(inp["cmp_v_w2"][l].reshape(2, 128, 64).transpose(1, 0, 2))
    lq = np.stack([inp["diff_lq1"][l], inp["diff_lk1"][l], inp["diff_lq2"][l], inp["diff_lk2"][l]], axis=0)
    d["lqk"] = np.ascontiguousarray(np.broadcast_to(lq[None], (128, 4, 64)))
    d["sublnB"] = np.ascontiguousarray(np.broadcast_to(inp["diff_subln"][l][None, :], (128, 128)))
    return d


_CACHE = {}


def _prog(name, fn, *args):
    key = (name,) + args
    if key not in _CACHE:
        _CACHE[key] = fn(*args)
    return _CACHE[key]


def kernel_unfused(**inputs):
    inp = {k: np.asarray(v) for k, v in inputs.items()}
    x = inp["x"].astype(np.float32, copy=False)
    B, S, _ = x.shape
    T = S // 2
    L = inp["w_in"].shape[0]
    cores = list(range(8))
    cst = ab_consts(S)
    xT = [np.ascontiguousarray(x[b].T) for b in range(B)]
    out = None
    for l in range(L):
        nc_ab = build_ab(S)
        maps = []
        for c in cores:
            b, g = c // 2, c % 2
            d = prep_ab(l, inp, b, g, S)
            d.update(cst)
            d["xT"] = xT[b]
            maps.append(d)
        res = run_bass_kernel_spmd(nc_ab, maps, core_ids=cores).results
        oaT = [np.concatenate([res[2 * b]["oT"][0:256], res[2 * b + 1]["oT"][0:256]], axis=0) for b in range(B)]
        obT = [np.concatenate([res[2 * b]["oT"][256:512], res[2 * b + 1]["oT"][256:512]], axis=0) for b in range(B)]
        del res, maps
        nc_c1 = build_c1(T)
        maps = []
        p1 = prep_c1(l, inp)
        for c in cores:
            b, g = c // 2, c % 2
            d = dict(p1)
            d["xT"] = np.ascontiguousarray(xT[b][:, g * T:(g + 1) * T])
            d["oaT"] = np.ascontiguousarray(oaT[b][:, g * T:(g + 1) * T])
            d["obT"] = np.ascontiguousarray(obT[b][:, g * T:(g + 1) * T])
            maps.append(d)
        res = run_bass_kernel_spmd(nc_c1, maps, core_ids=cores).results
        x1T = [res[c]["x1T"] for c in cores]
        del res, maps
        nc_c2 = build_c2(T)
        p2 = prep_c2(l, inp)
        maps = []
        for c in cores:
            d = dict(p2)
            d["x1T"] = x1T[c]
            maps.append(d)
        res = run_bass_kernel_spmd(nc_c2, maps, core_ids=cores).results
        xT = [np.concatenate([res[2 * b]["x2T"], res[2 * b + 1]["x2T"]], axis=1) for b in range(B)]
        if l == L - 1:
            out = np.stack([np.concatenate([res[2 * b]["x2nT"], res[2 * b + 1]["x2nT"]], axis=1).T for b in range(B)], axis=0)
        del res, maps
    return np.ascontiguousarray(out.astype(np.float32))


def build_fused(S, L=2):
    kb = KB()
    xin = kb.din("xT_in", [D, S])
    outT = kb.dout("outT", [D, S])
    xcur = xin
    for l in range(L):
        kb.sfx = "_l%d" % l
        oscr = kb.dscr("oscr", [2, 512, S], BF16)
        for g in range(2):
            ov = Tn(oscr.t[g], "oscr_g")
            ov.b = oscr.b
            build_ab(S, kb=kb, io={"xT": xcur, "oT": ov}, sfx="_l%dg%d" % (l, g))
        kb.sfx = "_l%d" % l
        x1 = kb.dscr("x1scr", [D, S], F32)

        def osrc_fn(which, c, t0, G, oscr=oscr):
            r0 = 256 * which + (c % 2) * 128
            return oscr.t[c // 2, r0:r0 + 128, t0:t0 + G]

        build_c1(S, kb=kb, io={"xT": xcur, "x1T": x1, "osrc_fn": osrc_fn, "osrc_tn": oscr}, sfx="_l%d" % l)
        if l < L - 1:
            kb.sfx = "_l%d" % l
            x2 = kb.dscr("x2scr", [D, S], F32)
            build_c2(S, kb=kb, io={"x1T": x1, "x2T": x2, "want_x2": True, "want_x2n": False}, sfx="_l%d" % l)
            xcur = x2
        else:
            build_c2(S, kb=kb, io={"x1T": x1, "x2nT": outT, "want_x2": False, "want_x2n": True}, sfx="_l%d" % l)
    return kb.finish()


def fused_inputs(inp, b, S, cst):
    L = inp["w_in"].shape[0]
    d = dict(cst)
    d["cm01"] = np.triu(np.ones((128, 128), np.float32))
    d["xT_in"] = np.ascontiguousarray(inp["x"][b].T)
    for l in range(L):
        for g in range(2):
            for k, v in prep_ab(l, inp, b, g, S).items():
                if k == "posB":
                    d["posB"] = v
                else:
                    d["%s_l%dg%d" % (k, l, g)] = v
        for k, v in prep_c1(l, inp).items():
            if k != "cm01":
                d["%s_l%d" % (k, l)] = v
        for k, v in prep_c2(l, inp).items():
            d["%s_l%d" % (k, l)] = v
    return d


def kernel_fused4(**inputs):
    inp = {k: np.asarray(v) for k, v in inputs.items()}
    B, S, _ = inp["x"].shape
    nc = build_fused(S)
    cst = ab_consts(S)
    per_b = [fused_inputs(inp, b, S, cst) for b in range(B)]
    maps = [per_b[c % B] for c in range(8)]
    res = run_bass_kernel_spmd(nc, maps, core_ids=list(range(8))).results
    out = np.stack([res[b]["outT"].T for b in range(B)], axis=0)
    return np.ascontiguousarray(out.astype(np.float32))


RG_PAIRS = [[0, 1], [2, 3], [4, 5], [6, 7]]


def _allgather(kb, src, dst, key):
    kb.pr.add("q_cc", lambda e: e.collective_compute("AllGather", ALU.bypass, replica_groups=RG_PAIRS,
                                                     ins=[src.opt()], outs=[dst.opt()]), [key[0]], [key[1]])


def build_fused8(S, L=2):
    kb = KB()
    T = S // 2
    xfull = kb.din("xT_in", [D, S])
    xhalf = kb.din("xT_half", [D, T])
    outT = kb.dout("outT", [D, T])
    hgath = None
    xown = xhalf
    for l in range(L):
        kb.sfx = "_l%d" % l
        opc = kb.dscr("opc", [4, 2, 128, T], BF16)
        ogath = kb.dscr("ogath", [4, 2, 256, T], BF16)
        io = {"o_piece": opc}
        if hgath is None:
            io["xT"] = xfull
        else:
            io["h_src"] = hgath
        build_ab(S, kb=kb, io=io, sfx="_l%d" % l)
        for c4 in range(4):
            for half in range(2):
                _allgather(kb, opc.t[c4, half], ogath.t[c4, half], ((opc, (c4, half)), (ogath, (c4, half))))
        kb.sfx = "_l%d" % l
        x1 = kb.dscr("x1scr", [D, T], F32)
        build_c1(T, kb=kb, io={"xT": xown, "x1T": x1, "o_gath": ogath}, sfx="_l%d" % l)
        if l < L - 1:
            kb.sfx = "_l%d" % l
            x2 = kb.dscr("x2scr", [D, T], F32)
            hpc = kb.dscr("hpc", [8, 128, T], BF16)
            hg = kb.dscr("hgath", [8, 256, T], BF16)
            build_c2(T, kb=kb, io={"x1T": x1, "x2T": x2, "want_x2": True, "want_x2n": False, "h_next": hpc}, sfx="_l%d" % l)
            for c in range(8):
                _allgather(kb, hpc.t[c], hg.t[c], ((hpc, c), (hg, c)))
            hgath = hg
            xown = x2
        else:
            build_c2(T, kb=kb, io={"x1T": x1, "x2nT": outT, "want_x2": False, "want_x2n": True}, sfx="_l%d" % l)
    return kb.finish()


def fused8_inputs(inp, b, g, S, cst):
    L = inp["w_in"].shape[0]
    T = S // 2
    d = dict(cst)
    d["cm01"] = np.triu(np.ones((128, 128), np.float32))
    xT = np.ascontiguousarray(inp["x"][b].T)
    d["xT_in"] = xT
    d["xT_half"] = np.ascontiguousarray(xT[:, g * T:(g + 1) * T])
    m = np.zeros((128, 2), np.float32)
    m[:, g] = 1.0
    for l in range(L):
        for k, v in prep_ab(l, inp, b, g, S).items():
            if k == "posB":
                d["posB"] = v
            elif not (l > 0 and k == "gA"):
                d["%s_l%d" % (k, l)] = v
        for k, v in prep_c1(l, inp).items():
            if k != "cm01":
                d["%s_l%d" % (k, l)] = v
        d["msel_l%d" % l] = m
        for k, v in prep_c2(l, inp).items():
            d["%s_l%d" % (k, l)] = v
        if l < L - 1:
            d["gN_l%d" % l] = _col8(inp["attn_norm"][l + 1])
    return d


def kernel(**inputs):
    inp = {k: np.asarray(v) for k, v in inputs.items()}
    B, S, _ = inp["x"].shape
    T = S // 2
    nc = build_fused8(S)
    cst = ab_consts(S)
    maps = [fused8_inputs(inp, c // 2, c % 2, S, cst) for c in range(8)]
    res = run_bass_kernel_spmd(nc, maps, core_ids=list(range(8))).results
    out = np.stack([np.concatenate([res[2 * b]["outT"].T, res[2 * b + 1]["outT"].T], axis=0) for b in range(B)], axis=0)
    return np.ascontiguousarray(out.astype(np.float32))
```
